# Optimizing a Trainium2 kernel written in Bass

```python
import math
import jax, jax.numpy as jnp
from jax import lax
import numpy as np

D_MODEL = 1024
BATCH = 32
SEQ = 2048
DEPTH = 1
DEC_BATCH = 16
DEC_SEQ = 2048
PAST_LEN = 128

H_GDN = 8
DK_GDN = 128
DV_GDN = 128
W_GDN = H_GDN * DV_GDN
CONV_K = 5
CHUNK = 64
H_MLA = 8
Q_LORA = 384
KV_LORA = 256
D_NOPE = 128
D_ROPE = 64
D_V_MLA = 128
W_MLA = H_MLA * D_V_MLA
ROPE_THETA = 10000.0
Q_BLOCK = 128
H_MEM = 4
D_MEM = 128
W_MEM = H_MEM * D_MEM
N_MEM = 256
N_BRANCH = 3
EPS = 1e-6
W_QKV = H_GDN * (2 * DK_GDN + DV_GDN)
SPLIT_SIZES = (W_QKV, 4 * H_GDN, W_GDN, Q_LORA, KV_LORA + D_ROPE, W_MLA, W_MEM, W_MEM, N_BRANCH * D_MODEL)
D_IN = sum(SPLIT_SIZES)

kernel_name = "hybrid_gdn_mla_memory_encoder"


def _rmsnorm(x, gain):
    xf = x.astype(jnp.float32)
    y = xf * lax.rsqrt(jnp.mean(xf * xf, axis=-1, keepdims=True) + EPS)
    return (y * gain.astype(jnp.float32)).astype(x.dtype)


def _l2norm(x):
    xf = x.astype(jnp.float32)
    return xf * lax.rsqrt(jnp.sum(xf * xf, axis=-1, keepdims=True) + EPS)


def _centred_conv_silu(x, w):
    c = x.shape[-1]
    y = lax.conv_general_dilated(
        x, w[:, None, :].astype(x.dtype), window_strides=(1,),
        padding=[((CONV_K - 1) // 2, CONV_K // 2)],
        dimension_numbers=("NWC", "WIO", "NWC"), feature_group_count=c)
    return jax.nn.silu(y)


def _rope(x, pos):
    half = x.shape[-1] // 2
    inv_freq = ROPE_THETA ** (-jnp.arange(half, dtype=jnp.float32) / half)
    ang = pos.astype(jnp.float32)[:, None] * inv_freq[None, :]
    cos = jnp.cos(ang)[None, :, None, :]
    sin = jnp.sin(ang)[None, :, None, :]
    xf = x.astype(jnp.float32)
    x1, x2 = xf[..., :half], xf[..., half:]
    return jnp.concatenate([x1 * cos - x2 * sin, x2 * cos + x1 * sin], axis=-1).astype(x.dtype)


def _gdn_chunked(q, k, v, g, beta):
    b, s, h, dk = q.shape
    dv = v.shape[-1]
    n = s // CHUNK

    def chunks(t):
        return t.reshape(b, n, CHUNK, h, -1).transpose(1, 0, 3, 2, 4)

    qc = chunks(q * dk ** -0.5)
    kc = chunks(k)
    vc = chunks(v)
    gc = jnp.cumsum(g.reshape(b, n, CHUNK, h).transpose(1, 0, 3, 2), axis=-1)
    bc = beta.reshape(b, n, CHUNK, h).transpose(1, 0, 3, 2)[..., None]
    lower = jnp.tril(jnp.ones((CHUNK, CHUNK), dtype=bool))
    decay = jnp.exp(jnp.where(lower, gc[..., :, None] - gc[..., None, :], -jnp.inf))
    kb = kc * bc
    eye = jnp.eye(CHUNK, dtype=jnp.float32)
    lmat = jnp.einsum("nbhck,nbhek->nbhce", kb, kc) * decay * (1.0 - eye)
    t_inv = lax.linalg.triangular_solve(lmat + eye, jnp.broadcast_to(eye, lmat.shape),
                                        left_side=True, lower=True, unit_diagonal=True)
    u = jnp.einsum("nbhce,nbhev->nbhcv", t_inv, vc * bc)
    w = jnp.einsum("nbhce,nbhek->nbhck", t_inv, kb * jnp.exp(gc)[..., None])

    def step(state, xs):
        qi, ki, ui, wi, gi, di = xs
        v_new = ui - jnp.einsum("bhck,bhkv->bhcv", wi, state)
        intra = jnp.einsum("bhck,bhek->bhce", qi, ki) * di
        o = (jnp.einsum("bhck,bhkv->bhcv", qi * jnp.exp(gi)[..., None], state)
             + jnp.einsum("bhce,bhev->bhcv", intra, v_new))
        g_last = gi[..., -1]
        state = (state * jnp.exp(g_last)[..., None, None]
                 + jnp.einsum("bhck,bhcv->bhkv", ki * jnp.exp(g_last[..., None] - gi)[..., None], v_new))
        return state, o

    state0 = jnp.zeros((b, h, dk, dv), jnp.float32)
    _, o = lax.scan(step, state0, (qc, kc, u, w, gc, decay))
    return o.transpose(1, 0, 3, 2, 4).reshape(b, s, h, dv)


def _blocked_attention(q, k, v, scale):
    b, s, h, dq = q.shape
    nb = s // Q_BLOCK
    qb = q.reshape(b, nb, Q_BLOCK, h, dq).transpose(1, 0, 2, 3, 4)

    def one_block(qi):
        sc = jnp.einsum("bqhd,bkhd->bhqk", qi, k).astype(jnp.float32) * scale
        p = jax.nn.softmax(sc, axis=-1)
        return jnp.einsum("bhqk,bkhd->bqhd", p.astype(v.dtype), v)

    o = lax.map(one_block, qb)
    return o.transpose(1, 0, 2, 3, 4).reshape(b, s, h, v.shape[-1])


def _layer(x, mem, pos, attn_norm_gain, w_in, conv_w, a_log, dt_bias, gdn_norm_gain,
           q_norm_gain, w_q_up, kv_norm_gain, w_kv_up, mem_norm_gain, w_mem_kv,
           w_br_gdn, w_br_mla, w_br_mem, w_out):
    b, s, _ = x.shape
    h = _rmsnorm(x, attn_norm_gain)
    proj = h @ w_in
    idx = [int(i) for i in np.cumsum(SPLIT_SIZES)[:-1]]
    qkv, ab, z_gdn, cq, ckv, z_mla, q_mem, z_mem, gate_logits = jnp.split(proj, idx, axis=-1)

    qkv = _centred_conv_silu(qkv, conv_w)
    q, k, v = jnp.split(qkv, [H_GDN * DK_GDN, 2 * H_GDN * DK_GDN], axis=-1)
    q = _l2norm(q.reshape(b, s, H_GDN, DK_GDN))
    k = _l2norm(k.reshape(b, s, H_GDN, DK_GDN))
    v = v.reshape(b, s, H_GDN, DV_GDN).astype(jnp.float32)
    ab = ab.astype(jnp.float32).reshape(b, s, 4, H_GDN)
    g = -jnp.exp(a_log.astype(jnp.float32)) * jax.nn.softplus(ab[:, :, 0:2] + dt_bias.astype(jnp.float32))
    beta = jax.nn.sigmoid(ab[:, :, 2:4])
    o_fwd = _gdn_chunked(q, k, v, g[:, :, 0], beta[:, :, 0])
    o_bwd = _gdn_chunked(q[:, ::-1], k[:, ::-1], v[:, ::-1], g[:, ::-1, 1], beta[:, ::-1, 1])[:, ::-1]
    o_gdn = _rmsnorm(o_fwd + o_bwd, gdn_norm_gain).reshape(b, s, W_GDN).astype(x.dtype) * jax.nn.silu(z_gdn)

    q_lat = _rmsnorm(cq, q_norm_gain)
    qm = (q_lat @ w_q_up).reshape(b, s, H_MLA, D_NOPE + D_ROPE)
    q_nope, q_pe = jnp.split(qm, [D_NOPE], axis=-1)
    q_pe = _rope(q_pe, pos)
    kv_lat, k_pe = jnp.split(ckv, [KV_LORA], axis=-1)
    kv_lat = _rmsnorm(kv_lat, kv_norm_gain)
    k_pe = _rope(k_pe[:, :, None, :], pos)
    kv = (kv_lat @ w_kv_up).reshape(b, s, H_MLA, D_NOPE + D_V_MLA)
    k_nope, v_mla = jnp.split(kv, [D_NOPE], axis=-1)
    q_full = jnp.concatenate([q_nope, q_pe], axis=-1)
    k_full = jnp.concatenate([k_nope, jnp.broadcast_to(k_pe, (b, s, H_MLA, D_ROPE))], axis=-1)
    o_mla = _blocked_attention(q_full, k_full, v_mla, (D_NOPE + D_ROPE) ** -0.5)
    o_mla = o_mla.reshape(b, s, W_MLA) * jax.nn.silu(z_mla)

    mn = _rmsnorm(mem, mem_norm_gain)
    km, vm = jnp.split(mn @ w_mem_kv, 2, axis=-1)
    km = km.reshape(b, N_MEM, H_MEM, D_MEM)
    vm = vm.reshape(b, N_MEM, H_MEM, D_MEM)
    qx = q_mem.reshape(b, s, H_MEM, D_MEM)
    sc = jnp.einsum("bqhd,bkhd->bhqk", qx, km).astype(jnp.float32) * D_MEM ** -0.5
    p = jax.nn.softmax(sc, axis=-1)
    o_mem = jnp.einsum("bhqk,bkhd->bqhd", p.astype(vm.dtype), vm).reshape(b, s, W_MEM) * jax.nn.silu(z_mem)

    gates = jax.nn.sigmoid(gate_logits.astype(jnp.float32)).astype(x.dtype).reshape(b, s, N_BRANCH, D_MODEL)
    merged = (gates[:, :, 0] * (o_gdn @ w_br_gdn)
              + gates[:, :, 1] * (o_mla @ w_br_mla)
              + gates[:, :, 2] * (o_mem @ w_br_mem))
    return x + merged @ w_out


def _encode(x, mem, layer_weights, final_norm_gain):
    pos = jnp.arange(x.shape[1], dtype=jnp.int32)
    h = x
    for l in range(DEPTH):
        h = _layer(h, mem, pos, *[w[l] for w in layer_weights])
    return _rmsnorm(h, final_norm_gain)


def setup_inputs(seed: int = 0) -> dict:
    key = jax.random.key(seed)
    ks = jax.random.split(key, 24)
    f32 = jnp.float32

    def nrm(k, shape, fan_in):
        return jax.random.normal(k, shape, f32) * fan_in ** -0.5

    def gain(k, shape):
        return 1.0 + 0.02 * jax.random.normal(k, shape, f32)

    dt = jnp.exp(jax.random.uniform(ks[6], (DEPTH, 2, H_GDN), f32, math.log(1e-3), math.log(0.1)))
    return {
        "x_prompt": jax.random.normal(ks[0], (BATCH, SEQ, D_MODEL), f32),
        "x_sample": jax.random.normal(ks[1], (DEC_BATCH, DEC_SEQ, D_MODEL), f32),
        "mem_prompt": jax.random.normal(ks[2], (BATCH, N_MEM, D_MODEL), f32),
        "mem_sample": jax.random.normal(ks[3], (DEC_BATCH, N_MEM, D_MODEL), f32),
        "attn_norm_gain": gain(ks[4], (DEPTH, D_MODEL)),
        "w_in": nrm(ks[5], (DEPTH, D_MODEL, D_IN), D_MODEL),
        "conv_w": nrm(ks[7], (DEPTH, CONV_K, W_QKV), CONV_K),
        "a_log": jnp.log(jax.random.uniform(ks[8], (DEPTH, 2, H_GDN), f32, 1.0, 16.0)),
        "dt_bias": dt + jnp.log(-jnp.expm1(-dt)),
        "gdn_norm_gain": gain(ks[9], (DEPTH, DV_GDN)),
        "q_norm_gain": gain(ks[10], (DEPTH, Q_LORA)),
        "w_q_up": nrm(ks[11], (DEPTH, Q_LORA, H_MLA * (D_NOPE + D_ROPE)), Q_LORA),
        "kv_norm_gain": gain(ks[12], (DEPTH, KV_LORA)),
        "w_kv_up": nrm(ks[13], (DEPTH, KV_LORA, H_MLA * (D_NOPE + D_V_MLA)), KV_LORA),
        "mem_norm_gain": gain(ks[14], (DEPTH, D_MODEL)),
        "w_mem_kv": nrm(ks[15], (DEPTH, D_MODEL, 2 * W_MEM), D_MODEL),
        "w_br_gdn": nrm(ks[16], (DEPTH, W_GDN, D_MODEL), W_GDN),
        "w_br_mla": nrm(ks[17], (DEPTH, W_MLA, D_MODEL), W_MLA),
        "w_br_mem": nrm(ks[18], (DEPTH, W_MEM, D_MODEL), W_MEM),
        "w_out": nrm(ks[19], (DEPTH, D_MODEL, D_MODEL), D_MODEL),
        "final_norm_gain": gain(ks[20], (D_MODEL,)),
    }


def reference(x_prompt, x_sample, mem_prompt, mem_sample, attn_norm_gain, w_in, conv_w, a_log,
              dt_bias, gdn_norm_gain, q_norm_gain, w_q_up, kv_norm_gain, w_kv_up, mem_norm_gain,
              w_mem_kv, w_br_gdn, w_br_mla, w_br_mem, w_out, final_norm_gain):
    layer_weights = (attn_norm_gain, w_in, conv_w, a_log, dt_bias, gdn_norm_gain, q_norm_gain,
                     w_q_up, kv_norm_gain, w_kv_up, mem_norm_gain, w_mem_kv, w_br_gdn,
                     w_br_mla, w_br_mem, w_out)
    y_prompt = _encode(x_prompt, mem_prompt, layer_weights, final_norm_gain)
    y_sample = _encode(x_sample, mem_sample, layer_weights, final_norm_gain)
    return (y_prompt, y_sample)
```

```python
import contextlib
import math
import numpy as np
import concourse.bass as bass
import concourse.mybir as mybir
from concourse.bass_utils import run_bass_kernel_spmd

F32 = mybir.dt.float32
BF16 = mybir.dt.bfloat16
I32 = mybir.dt.int32
ALU = mybir.AluOpType
AF = mybir.ActivationFunctionType
AX = mybir.AxisListType

ENGINES = ("tensor", "vector", "scalar", "gpsimd", "sync")
N_DMA_SEMS = 12


class Buf:
    __slots__ = ("name", "w", "r", "excl")

    def __init__(self, name):
        self.name = name
        self.excl = False
        self.w = {}
        self.r = {}


class Op:
    __slots__ = ("eng", "idx", "fn", "deps", "dma", "signal", "done_sem", "done_val", "tag")

    def __init__(self, eng, idx, fn, dma, tag):
        self.eng = eng
        self.idx = idx
        self.fn = fn
        self.dma = dma
        self.deps = {}
        self.signal = False
        self.done_sem = None
        self.done_val = None
        self.tag = tag


class Sched:
    def __init__(self, nc):
        self.nc = nc
        self.ops = {e: [] for e in ENGINES}
        self.stack = contextlib.ExitStack()
        self.nbuf = 0
        self.final_waits = []

    def sbuf(self, name, shape, dtype):
        return self.stack.enter_context(self.nc.sbuf_tensor(name, list(shape), dtype))

    def psum(self, name, shape, dtype):
        return self.stack.enter_context(self.nc.psum_tensor(name, list(shape), dtype))

    def buf(self, name=None):
        self.nbuf += 1
        return Buf(name or f"b{self.nbuf}")

    def bufs(self, n, name="b"):
        return [self.buf(f"{name}{i}") for i in range(n)]

    def add(self, eng, fn, reads=(), writes=(), dma=False, tag=None, final=False):
        lst = self.ops[eng]
        op = Op(eng, len(lst), fn, dma, tag)
        lst.append(op)
        deps = []
        for b in reads:
            deps.extend(b.w.values())
            if b.excl:
                deps.extend(o for k_, o in b.r.items() if o.eng != eng)
        for b in writes:
            deps.extend(b.w.values())
            deps.extend(b.r.values())
        for d in deps:
            if d is op:
                continue
            if d.eng == eng and eng == "tensor" and not d.dma and not dma:
                continue
            key = (d.eng, d.idx) if d.dma else d.eng
            cur = op.deps.get(key)
            if cur is None or cur.idx < d.idx:
                op.deps[key] = d
        mykey = (eng, op.idx) if dma else eng
        for b in reads:
            b.r[mykey] = op
        for b in writes:
            b.w = {mykey: op}
            b.r = {}
        if final:
            self.final_waits.append(op)
        return op

    def pe(self, fn, reads=(), writes=(), **k):
        return self.add("tensor", fn, reads, writes, **k)

    def dve(self, fn, reads=(), writes=(), **k):
        return self.add("vector", fn, reads, writes, **k)

    def act(self, fn, reads=(), writes=(), **k):
        return self.add("scalar", fn, reads, writes, **k)

    def pool(self, fn, reads=(), writes=(), **k):
        return self.add("gpsimd", fn, reads, writes, **k)

    def dma(self, fn, reads=(), writes=(), eng="sync", **k):
        return self.add(eng, fn, reads, writes, dma=True, **k)

    def emit(self):
        nc = self.nc
        for e in ENGINES:
            for op in self.ops[e]:
                for d in op.deps.values():
                    d.signal = True
        for op in self.final_waits:
            op.signal = True
        sems = {e: self.stack.enter_context(nc.semaphore(f"s_{e}")) for e in ENGINES}
        dma_sems = {e: [self.stack.enter_context(nc.semaphore(f"d_{e}_{j}")) for j in range(N_DMA_SEMS)]
                    for e in ("sync", "gpsimd", "scalar")}
        dma_prev = {}
        for e in ENGINES:
            cnt = 0
            nd = 0
            dcount = [0] * N_DMA_SEMS
            for op in self.ops[e]:
                if op.dma:
                    j = nd % N_DMA_SEMS
                    nd += 1
                    dcount[j] += 1
                    op.done_sem = dma_sems[e][j]
                    op.done_val = 16 * dcount[j]
                    op.signal = True
                elif op.signal:
                    cnt += 1
                    op.done_sem = sems[e]
                    op.done_val = cnt
        block = self.stack.enter_context(nc.Block())
        sched = self

        def run(e, eng):
            waited = {}
            for op in sched.ops[e]:
                ws = []
                for d in op.deps.values():
                    ws.append((d.done_sem, d.done_val))
                if op.dma and op.done_val > 16:
                    ws.append((op.done_sem, op.done_val - 16))
                for sem, val in ws:
                    k = id(sem)
                    if waited.get(k, 0) >= val:
                        continue
                    waited[k] = val
                    eng.wait_ge(sem, val)
                inst = op.fn(eng)
                if op.signal:
                    inst.then_inc(op.done_sem, 16 if op.dma else 1)
            if e == "sync":
                for op in sched.final_waits:
                    eng.wait_ge(op.done_sem, op.done_val)

        @block.tensor
        def _(eng):
            run("tensor", eng)

        @block.vector
        def _(eng):
            run("vector", eng)

        @block.scalar
        def _(eng):
            run("scalar", eng)

        @block.gpsimd
        def _(eng):
            run("gpsimd", eng)

        @block.sync
        def _(eng):
            run("sync", eng)

        self.stack.close()

    def stats(self):
        return {e: len(self.ops[e]) for e in ENGINES}

def _sched_barrier(self):
    deps = []
    for e in ENGINES:
        last = None
        for op in reversed(self.ops[e]):
            if not op.dma:
                last = op
                break
        if last is not None:
            deps.append(last)
    deps.extend(self.dma_open)
    self.dma_open = []
    lst = self.ops["sync"]
    op = Op("sync", len(lst), lambda eng: eng.nop(), False, "barrier")
    lst.append(op)
    for d in deps:
        key = (d.eng, d.idx) if d.dma else d.eng
        op.deps[key] = d
    self.bar = op


Sched.barrier = _sched_barrier


class Tile:
    def __init__(self, t, nb, name, S):
        self.t = t
        self.b = [S.buf(f"{name}.{i}") for i in range(nb)]

    def __getitem__(self, k):
        return self.t[k]


T = 2048
D = 1024
NT = 16
NB = 4
EPS = 1e-6
O_QKV, O_AB, O_ZG, O_CQ, O_CKV, O_ZM, O_QM, O_ZMEM, O_GATE = 0, 3072, 3104, 4128, 4512, 4832, 5856, 6368, 6880
SC_GDN = 128 ** -0.5
SC_MLA = 192 ** -0.5
SC_MEM = 128 ** -0.5
ARENA_ELEMS = 74 * 1024


def host_consts():
    c = {}
    p = np.arange(128)
    same = (p[:, None] // 64) == (p[None, :] // 64)
    c["ident"] = np.eye(128, dtype=np.float32)
    c["ones"] = np.ones((128, 128), np.float32)
    c["lfwd"] = (same & (p[:, None] <= p[None, :])).astype(np.float32)
    c["lbwd"] = (same & (p[:, None] >= p[None, :])).astype(np.float32)
    c["cblk"] = same.astype(np.float32)
    c["c0"] = np.repeat((p < 64)[:, None], 128, 1).astype(np.float32)
    c["c1"] = np.repeat((p >= 64)[:, None], 128, 1).astype(np.float32)
    c["ms_f"] = (same & (p[:, None] < p[None, :])).astype(np.float32)
    c["mi_f"] = c["lfwd"].copy()
    c["ms_b"] = (same & (p[:, None] > p[None, :])).astype(np.float32)
    c["mi_b"] = c["lbwd"].copy()
    half = 32
    inv = 10000.0 ** (-np.arange(half, dtype=np.float32) / half)
    ang = np.arange(T, dtype=np.float32)[None, :] * inv[:, None]
    cos = np.cos(ang).astype(np.float32)
    sin = np.sin(ang).astype(np.float32)
    c["cos2"] = np.concatenate([cos, cos], 0)
    c["sins"] = np.concatenate([-sin, sin], 0)
    return {k: np.ascontiguousarray(v, dtype=np.float32) for k, v in c.items()}


CONST_SHAPES = {"ident": [128, 128], "ones": [128, 128], "lfwd": [128, 128], "lbwd": [128, 128], "cblk": [128, 128],
                "c0": [128, 128], "c1": [128, 128], "ms_f": [128, 128], "mi_f": [128, 128], "ms_b": [128, 128],
                "mi_b": [128, 128], "cos2": [64, T], "sins": [64, T]}

WEIGHT_SHAPES = {"attn_norm_gain": [1, 1024], "w_in": [1, 1024, 9952], "conv_w": [1, 5, 3072], "a_log": [1, 2, 8],
                 "dt_bias": [1, 2, 8], "gdn_norm_gain": [1, 128], "q_norm_gain": [1, 384], "w_q_up": [1, 384, 1536],
                 "kv_norm_gain": [1, 256], "w_kv_up": [1, 256, 2048], "mem_norm_gain": [1, 1024],
                 "w_mem_kv": [1, 1024, 1024], "w_br_gdn": [1, 1024, 1024], "w_br_mla": [1, 1024, 1024],
                 "w_br_mem": [1, 512, 1024], "w_out": [1, 1024, 1024], "final_norm_gain": [1024]}


def build_nc(NSEQ, phases="GMCX", dbg=False):
    nc = bass.Bass("TRN2", target_bir_lowering=False)
    dr = {}
    x = nc.dram_tensor("x", [NSEQ, T, D], F32, kind="ExternalInput").ap()
    mem = nc.dram_tensor("mem", [NSEQ, 256, D], F32, kind="ExternalInput").ap()
    for k, shp in WEIGHT_SHAPES.items():
        dr[k] = nc.dram_tensor(k, shp, F32, kind="ExternalInput").ap()
    for k, shp in CONST_SHAPES.items():
        dr[k] = nc.dram_tensor("c_" + k, shp, F32, kind="ExternalInput").ap()
    y = nc.dram_tensor("y", [NSEQ, T, D], F32, kind="ExternalOutput").ap()
    scr = nc.dram_tensor("scr", [20, 128, T], BF16, kind=("ExternalOutput" if dbg else "Internal")).ap()
    w_in = dr["w_in"][0]

    S = Sched(nc)
    S.dma_open = []
    S.bar = None
    _orig_add = S.add

    def add(eng, fn, reads=(), writes=(), dma=False, tag=None, final=False):
        op = _orig_add(eng, fn, reads, writes, dma=dma, tag=tag, final=final)
        if S.bar is not None:
            op.deps["sync"] = S.bar if ("sync" not in op.deps or op.deps["sync"].idx < S.bar.idx) else op.deps["sync"]
        if dma:
            S.dma_open.append(op)
        return op
    S.add = add

    def tile(name, shape, dtype, nb=1):
        return Tile(S.sbuf(name, shape, dtype), nb, name, S)

    cst = {}
    for k in ("ident", "ones", "lfwd", "lbwd", "cblk", "c0", "c1"):
        cst[k] = tile("k_" + k, [128, 128], F32)
    ident_bf = tile("ident_bf", [128, 128], BF16)
    ones_bf = tile("ones_bf", [128, 128], BF16)
    masks = {k: tile("m_" + k, [128, 128], BF16) for k in ("ms_f", "mi_f", "ms_b", "mi_b")}
    hT = tile("hT", [128, 8, T], BF16, nb=NB)
    gcol_attn = tile("gcol_attn", [128, 8], F32)
    gcol_mem = tile("gcol_mem", [128, 8], F32)
    gcol_q = tile("gcol_q", [128, 3], F32)
    gcol_kv = tile("gcol_kv", [128, 2], F32)
    gcol_gdn = tile("gcol_gdn", [128, 1], F32)
    fgain = tile("fgain", [128, D], F32)
    cw = tile("cw", [128, 5, 24], F32)
    alog_bc = tile("alog_bc", [128, 16], F32)
    dtb_bc = tile("dtb_bc", [128, 16], F32)
    negA = tile("negA", [128, 16], F32)
    stage = tile("stage", [128, 128], F32)
    arena = S.sbuf("arena", [128, ARENA_ELEMS], BF16)
    PS = S.psum("PS", [128, 8, 512], F32)
    pb = [S.buf(f"bank{i}") for i in range(8)]
    for _b in pb:
        _b.excl = True
    wpool = [tile(f"wp{i}", [128, 8, 128], BF16) for i in range(5)]
    wctr = [0]

    def bank(i):
        return PS[:, i, :]

    def bank_bf(i):
        return PS[:, i, :].bitcast(BF16)

    def bank2(i):
        return PS[:, i:i + 2, :]

    astate = {"off": 0}

    def areset():
        astate["off"] = 0

    def alloc(name, shape, dtype, nb=1):
        n = int(np.prod(shape[1:]))
        n2 = n * 2 if dtype == F32 else n
        n2 = (n2 + 1) // 2 * 2
        off = astate["off"]
        assert off + n2 <= ARENA_ELEMS, (name, off, n2)
        astate["off"] = off + n2
        v = arena[:shape[0], off:off + n2]
        if dtype == F32:
            v = v.bitcast(F32)
        if len(shape) == 3:
            v = v.rearrange("p (a b) -> p a b", a=shape[1])
        elif len(shape) == 4:
            v = v.rearrange("p (a b c) -> p a b c", a=shape[1], b=shape[2])
        return Tile(v, nb, name, S)

    def load_w(src2d, c0, ncols, nk=8, dst=None):
        if dst is None:
            dst = wpool[wctr[0] % len(wpool)]
            wctr[0] += 1
        src = src2d[:, c0:c0 + ncols].rearrange("(k p) n -> p k n", p=128)
        S.dma(lambda e: e.dma_start(out=dst[:, 0:nk, 0:ncols], in_=src), writes=dst.b, eng="gpsimd")
        return dst

    def mm(out, lhsT, rhs, start, stop, reads, writes):
        S.pe(lambda e: e.matmul(out, lhsT=lhsT, rhs=rhs, start=start, stop=stop), reads=reads, writes=writes)

    def tr(out, in_, reads, writes, f32=False):
        idt = cst["ident"] if f32 else ident_bf
        S.pe(lambda e: e.transpose(out=out, in_=in_, identity=idt[:]), reads=list(reads) + idt.b, writes=writes)

    def proj_fm(wt, tb, bk, nk=8, m=128):
        for k in range(nk):
            mm(bank(bk)[0:m, :], wt[:, k, 0:m], hT[:, k, tb * 512:(tb + 1) * 512], k == 0, k == nk - 1,
               wt.b + [hT.b[tb]], [pb[bk]])

    def rsqrt_inplace(t_ap, bufs, scale, eps):
        S.act(lambda e: e.activation(out=t_ap, in_=t_ap, func=AF.Sqrt, scale=scale, bias=eps_t[:t_ap.shape[0], 0:1]), reads=bufs + eps_t.b, writes=bufs)
        S.dve(lambda e: e.reciprocal(out=t_ap, in_=t_ap), reads=bufs, writes=bufs)

    eps_t = tile("eps_t", [128, 1], F32)
    S.pool(lambda e: e.memset(eps_t[:], EPS), writes=eps_t.b)
    lnsc_t = tile("lnsc_t", [128, 1], F32)
    S.pool(lambda e: e.memset(lnsc_t[:], math.log(SC_GDN)), writes=lnsc_t.b)
    zero_t = tile("zero_t", [128, 1], F32)
    S.pool(lambda e: e.memset(zero_t[:], 0.0), writes=zero_t.b)

    def ld(dst, src, eng="sync"):
        S.dma(lambda e: e.dma_start(out=dst[:], in_=src), writes=dst.b, eng=eng)

    for k in ("ident", "ones", "lfwd", "lbwd", "cblk", "c0", "c1"):
        ld(cst[k], dr[k])
    S.dve(lambda e: e.tensor_copy(out=ident_bf[:], in_=cst["ident"][:]), reads=cst["ident"].b, writes=ident_bf.b)
    S.dve(lambda e: e.tensor_copy(out=ones_bf[:], in_=cst["ones"][:]), reads=cst["ones"].b, writes=ones_bf.b)
    for k in masks:
        S.dma(lambda e, k=k: e.dma_start(out=stage[:], in_=dr[k]), writes=stage.b)
        S.dve(lambda e, k=k: e.tensor_copy(out=masks[k][:], in_=stage[:]), reads=stage.b, writes=masks[k].b)

    def rep8(tl):
        return tl[:].unsqueeze(1).to_broadcast([128, 8, 128])
    S.dma(lambda e: e.dma_start(out=gcol_attn[:], in_=dr["attn_norm_gain"][0].rearrange("(k p) -> p k", p=128), allow_slow_non_contiguous=True), writes=gcol_attn.b)
    S.dma(lambda e: e.dma_start(out=gcol_mem[:], in_=dr["mem_norm_gain"][0].rearrange("(k p) -> p k", p=128), allow_slow_non_contiguous=True), writes=gcol_mem.b)
    S.dma(lambda e: e.dma_start(out=gcol_q[:], in_=dr["q_norm_gain"][0].rearrange("(k p) -> p k", p=128), allow_slow_non_contiguous=True), writes=gcol_q.b)
    S.dma(lambda e: e.dma_start(out=gcol_kv[:], in_=dr["kv_norm_gain"][0].rearrange("(k p) -> p k", p=128), allow_slow_non_contiguous=True), writes=gcol_kv.b)
    S.dma(lambda e: e.dma_start(out=gcol_gdn[:], in_=dr["gdn_norm_gain"][0].rearrange("(k p) -> p k", p=128), allow_slow_non_contiguous=True), writes=gcol_gdn.b)
    S.dma(lambda e: e.dma_start(out=fgain[:], in_=dr["final_norm_gain"].partition_broadcast(128)), writes=fgain.b, eng="gpsimd")
    for j in range(5):
        S.dma(lambda e, j=j: e.dma_start(out=cw[:, j, :], in_=dr["conv_w"][0, j].rearrange("(c p) -> p c", p=128), allow_slow_non_contiguous=True), writes=cw.b)
    S.dma(lambda e: e.dma_start(out=alog_bc[:], in_=dr["a_log"][0].rearrange("a b -> (a b)").partition_broadcast(128)), writes=alog_bc.b, eng="gpsimd")
    S.dma(lambda e: e.dma_start(out=dtb_bc[:], in_=dr["dt_bias"][0].rearrange("a b -> (a b)").partition_broadcast(128)), writes=dtb_bc.b, eng="gpsimd")
    S.act(lambda e: e.activation(out=negA[:], in_=alog_bc[:], func=AF.Exp), reads=alog_bc.b, writes=negA.b)
    S.dve(lambda e: e.tensor_scalar(out=negA[:], in0=negA[:], scalar1=-1.0, scalar2=None, op0=ALU.mult), reads=negA.b, writes=negA.b)

    def phase_h(s):
        areset()
        xt = [alloc(f"xt{i}", [128, D], F32) for i in range(3)]
        junk = alloc("junk", [128, D], BF16)
        hb = [alloc(f"hb{i}", [128, D], BF16) for i in range(2)]
        st = [alloc(f"st{i}", [128, 2], F32) for i in range(2)]
        def stage1(t):
            X, H, ST = xt[t % 3], hb[t % 2], st[t % 2]
            S.dma(lambda e, X=X, t=t: e.dma_start(out=X[:], in_=x[s, t * 128:(t + 1) * 128, :]), writes=X.b)
            S.act(lambda e, X=X, ST=ST: e.activation(out=junk[:], in_=X[:], func=AF.Square, accum_out=ST[:, 0:1]),
                  reads=X.b, writes=junk.b + ST.b)
            rsqrt_inplace(ST[:, 0:1], ST.b, 1.0 / D, EPS)
            S.dve(lambda e, X=X, H=H, ST=ST: e.tensor_scalar(out=H[:], in0=X[:], scalar1=ST[:, 0:1], scalar2=None, op0=ALU.mult),
                  reads=X.b + ST.b, writes=H.b)

        def stage2(t):
            H = hb[t % 2]
            bk = 6 + (t % 2)
            for k in range(8):
                tr(bank_bf(bk)[:, k * 128:(k + 1) * 128], H[:, k * 128:(k + 1) * 128], H.b, [pb[bk]])
            S.dve(lambda e, bk=bk, t=t: e.tensor_tensor(out=hT[:, :, t * 128:(t + 1) * 128],
                                                       in0=bank_bf(bk).rearrange("p (k n) -> p k n", k=8),
                                                       in1=gcol_attn[:].unsqueeze(2).to_broadcast([128, 8, 128]), op=ALU.mult),
                  reads=[pb[bk]] + gcol_attn.b, writes=[hT.b[t // 4]])

        stage1(0)
        for t in range(NT):
            if t + 1 < NT:
                stage1(t + 1)
            stage2(t)

    def finalize_branch(oacc, zs, dst_chunk, gain_col=None, norm=False):
        ob = alloc("ob", [128, T], BF16)
        if norm:
            sq = alloc("fsq", [128, 512], BF16)
            rr = alloc("frr", [128, 512], F32)
            for tb in range(NB):
                sl = slice(tb * 512, (tb + 1) * 512)
                S.act(lambda e, sl=sl: e.activation(out=sq[:], in_=oacc[:, sl], func=AF.Square), reads=oacc.b, writes=sq.b)
                mm(bank(6), ones_bf[:], sq[:], True, True, ones_bf.b + sq.b, [pb[6]])
                S.act(lambda e: e.activation(out=rr[:], in_=bank(6), func=AF.Sqrt, scale=1.0 / 128, bias=eps_t[:, 0:1]), reads=[pb[6]] + eps_t.b, writes=rr.b)
                S.dve(lambda e: e.reciprocal(out=rr[:], in_=rr[:]), reads=rr.b, writes=rr.b)
                S.dve(lambda e, sl=sl: e.tensor_tensor(out=rr[:], in0=oacc[:, sl], in1=rr[:], op=ALU.mult), reads=oacc.b + rr.b, writes=rr.b)
                S.dve(lambda e, sl=sl: e.scalar_tensor_tensor(out=ob[:, sl], in0=rr[:], scalar=gain_col[:, 0:1], in1=zs[:, sl], op0=ALU.mult, op1=ALU.mult),
                      reads=rr.b + gain_col.b + zs.b, writes=ob.b)
        else:
            S.dve(lambda e: e.tensor_tensor(out=ob[:], in0=oacc[:], in1=zs[:], op=ALU.mult), reads=oacc.b + zs.b, writes=ob.b)
        S.dma(lambda e: e.dma_start(out=scr[dst_chunk], in_=ob[:]), reads=ob.b, writes=[scrb[dst_chunk]])

    scrb = [S.buf(f"scr{i}") for i in range(20)]

    def run(gen):
        for _ in gen:
            pass

    def zipper(main, side, every=1, nside=1):
        i = 0
        for _ in main:
            i += 1
            if side is not None and i % every == 0:
                for _r in range(nside):
                    try:
                        next(side)
                    except StopIteration:
                        side = None
                        break
        if side is not None:
            run(side)

    def silu_proj_gen(col0, zs, banks=(6, 7)):
        wt = load_w(w_in, col0, 128)
        for tb in range(NB):
            bk = banks[tb % 2]
            proj_fm(wt, tb, bk)
            S.act(lambda e, bk=bk, tb=tb: e.activation(out=zs[:, tb * 512:(tb + 1) * 512], in_=bank(bk), func=AF.Silu),
                  reads=[pb[bk]], writes=zs.b)
            yield

    def silu_proj(col0, name):
        zs = alloc(name, [128, T], BF16)
        run(silu_proj_gen(col0, zs))
        return zs

    def phase_gdn(s):
        import os as _os
        GZ = int(_os.environ.get('GZ', '1'))
        GE = int(_os.environ.get('GE', '1'))
        SCAN_DVE_SB = int(_os.environ.get('SCAN_DVE_SB', '1'))
        G1B = [int(c) for c in _os.environ.get('G1B', '676767')]
        areset()
        wab = load_w(w_in, O_AB, 32)
        ab = alloc("ab", [128, NT, 32], F32)
        for t in range(NT):
            for k in range(8):
                mm(bank(6)[:, t * 32:(t + 1) * 32], hT[:, k, t * 128:(t + 1) * 128], wab[:, k, 0:32], k == 0, k == 7,
                   [hT.b[t // 4]] + wab.b, [pb[6]])
        S.dve(lambda e: e.tensor_copy(out=ab[:], in_=bank(6).rearrange("p (t n) -> p t n", t=NT)), reads=[pb[6]], writes=ab.b)
        tabs = {n: alloc("tb_" + n, [128, NT, 16], F32) for n in ("xg", "t1", "g", "beta", "gc", "gtot", "cbg", "cd", "nbeta")}
        egrep = alloc("egrep", [128, NT, 2, 16], F32)
        xg, t1, g, beta, gc, gtot, cbg, cd, nbeta = (tabs[n] for n in ("xg", "t1", "g", "beta", "gc", "gtot", "cbg", "cd", "nbeta"))
        bc16 = lambda tl: tl[:].unsqueeze(1).to_broadcast([128, NT, 16])
        S.dve(lambda e: e.tensor_tensor(out=xg[:], in0=ab[:, :, 0:16], in1=bc16(dtb_bc), op=ALU.add), reads=ab.b + dtb_bc.b, writes=xg.b)
        S.act(lambda e: e.activation(out=t1[:], in_=xg[:], func=AF.Abs), reads=xg.b, writes=t1.b)
        S.act(lambda e: e.activation(out=t1[:], in_=t1[:], func=AF.Exp, scale=-1.0), reads=t1.b, writes=t1.b)
        S.act(lambda e: e.activation(out=t1[:], in_=t1[:], func=AF.Ln, bias=1.0), reads=t1.b, writes=t1.b)
        S.dve(lambda e: e.scalar_tensor_tensor(out=t1[:], in0=xg[:], scalar=0.0, in1=t1[:], op0=ALU.max, op1=ALU.add), reads=xg.b + t1.b, writes=t1.b)
        S.dve(lambda e: e.tensor_tensor(out=g[:], in0=t1[:], in1=bc16(negA), op=ALU.mult), reads=t1.b + negA.b, writes=g.b)
        S.act(lambda e: e.activation(out=beta[:], in_=ab[:, :, 16:32], func=AF.Sigmoid), reads=ab.b, writes=beta.b)
        S.dve(lambda e: e.tensor_scalar(out=nbeta[:], in0=beta[:], scalar1=-1.0, scalar2=None, op0=ALU.mult), reads=beta.b, writes=nbeta.b)
        for t in range(NT):
            mm(bank(6)[:, t * 16:t * 16 + 8], cst["lfwd"][:], g[:, t, 0:8], True, True, cst["lfwd"].b + g.b, [pb[6]])
            mm(bank(6)[:, t * 16 + 8:t * 16 + 16], cst["lbwd"][:], g[:, t, 8:16], True, True, cst["lbwd"].b + g.b, [pb[6]])
            mm(bank(7)[:, t * 16:(t + 1) * 16], cst["cblk"][:], g[:, t, :], True, True, cst["cblk"].b + g.b, [pb[7]])
            mm(bank(4)[:, t * 32:t * 32 + 16], cst["c0"][:], g[:, t, :], True, True, cst["c0"].b + g.b, [pb[4]])
            mm(bank(4)[:, t * 32 + 16:t * 32 + 32], cst["c1"][:], g[:, t, :], True, True, cst["c1"].b + g.b, [pb[4]])
        S.dve(lambda e: e.tensor_copy(out=gc[:], in_=bank(6)[:, 0:256].rearrange("p (t n) -> p t n", t=NT)), reads=[pb[6]], writes=gc.b)
        S.dve(lambda e: e.tensor_copy(out=gtot[:], in_=bank(7)[:, 0:256].rearrange("p (t n) -> p t n", t=NT)), reads=[pb[7]], writes=gtot.b)
        S.act(lambda e: e.activation(out=egrep[:].rearrange("p t j n -> p (t j n)"), in_=bank(4), func=AF.Exp), reads=[pb[4]], writes=egrep.b)
        S.act(lambda e: e.activation(out=cbg[:], in_=gc[:], func=AF.Exp), reads=gc.b, writes=cbg.b)
        S.dve(lambda e: e.tensor_tensor(out=cbg[:], in0=cbg[:], in1=beta[:], op=ALU.mult), reads=cbg.b + beta.b, writes=cbg.b)
        S.dve(lambda e: e.tensor_tensor(out=cd[:], in0=gtot[:], in1=gc[:], op=ALU.subtract), reads=gtot.b + gc.b, writes=cd.b)
        S.act(lambda e: e.activation(out=cd[:], in_=cd[:], func=AF.Exp), reads=cd.b, writes=cd.b)
        base_off = astate["off"]
        DIRS = []
        for d in range(2):
            DIRS.append(dict(Kd=alloc(f"Kd{d}", [128, NT, 128], BF16), QgT=alloc(f"QgT{d}", [128, T], BF16),
                             intraT=alloc(f"intraT{d}", [128, NT, 128], BF16), U=alloc(f"U{d}", [128, NT, 128], F32),
                             WT=alloc(f"WT{d}", [128, T], BF16), Kbg=alloc(f"Kbg{d}", [128, NT, 128], BF16),
                             Vb=alloc(f"Vb{d}", [128, NT, 128], BF16)))
        qT = alloc("qT", [128, T], BF16)
        kT = alloc("kT", [128, T], BF16)
        vT = alloc("vT", [128, T], BF16)
        off_R = astate["off"]
        G1T = [dict(xpad=alloc(f"xpad{i}", [128, T + 4], BF16), diag=alloc(f"diag{i}", [128, 5, 128], BF16),
                    ysil=alloc(f"ysil{i}", [128, 512], F32), sq=alloc(f"sq{i}", [128, 512], BF16), rr=alloc(f"rr{i}", [128, 512], F32))
               for i in range(2)]
        G1T.append(dict(xpad=alloc("xpad2", [128, T + 4], BF16), diag=alloc("diag2", [128, 5, 128], BF16), ysil=None, sq=None, rr=None))
        Ktok = alloc("Ktok", [128, NT, 128], BF16)
        Vtok = alloc("Vtok", [128, NT, 128], BF16)
        end1 = astate["off"]
        astate["off"] = off_R
        PT = [dict(diagG=alloc(f"diagG{i}", [128, 8, 128], F32), ET=alloc(f"ET{i}", [128, 8, 128], BF16),
                   EMs=alloc(f"EMs{i}", [128, 8, 128], BF16), EMi=alloc(f"EMi{i}", [128, 8, 128], BF16),
                   expR=alloc(f"expR{i}", [128, 8, 128], BF16), X=alloc(f"X{i}", [128, 8, 128], BF16),
                   N=alloc(f"N{i}", [128, 8, 128], BF16), P=alloc(f"P{i}", [128, 8, 128], BF16)) for i in range(2)]
        astate["off"] = max(end1, astate["off"])
        oacc = alloc("oacc", [128, T], F32, nb=NB)
        Sf = [alloc(f"Sf{d}", [128, 128], F32) for d in range(2)]
        Sb = [alloc(f"Sb{d}", [128, 128], BF16) for d in range(2)]
        vn = [[alloc(f"vn{d}{i}", [128, 128], BF16) for i in range(2)] for d in range(2)]
        zs_g = alloc("zs_g", [128, T], BF16)
        ob_g = alloc("ob_g", [128, T], BF16)
        fsq = alloc("fsq", [128, 512], BF16)
        frr = alloc("frr", [128, 512], F32)

        def rsqrt_act(out_ap, in_ap, scale, rd, wr):
            S.act(lambda e: e.activation(out=out_ap, in_=in_ap, func=AF.Ln, scale=scale, bias=eps_t[:, 0:1]), reads=rd + eps_t.b, writes=wr)
            S.act(lambda e: e.activation(out=out_ap, in_=out_ap, func=AF.Exp, scale=-0.5), reads=wr, writes=wr)

        def g1_chunk(which, h, dstT, TT_, bA, bB):
            xpad, diag, ysil, sq, rr = TT_["xpad"], TT_["diag"], TT_["ysil"], TT_["sq"], TT_["rr"]
            chunk = which * 8 + h
            wt = load_w(w_in, O_QKV + chunk * 128, 128)
            S.pool(lambda e: e.memset(xpad[:, 0:2], 0.0), writes=xpad.b)
            S.pool(lambda e: e.memset(xpad[:, T + 2:T + 4], 0.0), writes=xpad.b)
            for j in range(5):
                S.pool(lambda e, j=j: e.tensor_scalar(out=diag[:, j, :], in0=cst["ident"][:], scalar1=cw[:, j, chunk:chunk + 1], scalar2=None, op0=ALU.mult),
                       reads=cst["ident"].b + cw.b, writes=diag.b)
            yield
            for tb in range(NB):
                proj_fm(wt, tb, bA)
                S.act(lambda e, tb=tb: e.activation(out=xpad[:, 2 + tb * 512:2 + (tb + 1) * 512], in_=bank(bA), func=AF.Copy),
                      reads=[pb[bA]], writes=xpad.b)
                yield

            def stage_c(tb):
                sl = slice(tb * 512, (tb + 1) * 512)
                mm(bank(bA), ones_bf[:], sq[:], True, True, ones_bf.b + sq.b, [pb[bA]])
                S.act(lambda e: e.activation(out=rr[:], in_=bank(bA), func=AF.Ln, scale=1.0, bias=eps_t[:, 0:1]), reads=[pb[bA]] + eps_t.b, writes=rr.b)
                lb = lnsc_t if which == 0 else zero_t
                S.act(lambda e, lb=lb: e.activation(out=rr[:], in_=rr[:], func=AF.Exp, scale=-0.5, bias=lb[:, 0:1]), reads=rr.b + lb.b, writes=rr.b)
                S.pool(lambda e, sl=sl: e.tensor_tensor(out=dstT[:, sl], in0=ysil[:], in1=rr[:], op=ALU.mult), reads=ysil.b + rr.b, writes=dstT.b)

            for tb in range(NB):
                sl = slice(tb * 512, (tb + 1) * 512)
                for j in range(5):
                    mm(bank(bB), diag[:, j, :], xpad[:, tb * 512 + j:tb * 512 + j + 512], j == 0, j == 4, diag.b + xpad.b, [pb[bB]])
                if which == 2:
                    S.act(lambda e, sl=sl: e.activation(out=dstT[:, sl], in_=bank(bB), func=AF.Silu), reads=[pb[bB]], writes=dstT.b)
                    yield
                    continue
                S.act(lambda e: e.activation(out=ysil[:], in_=bank(bB), func=AF.Silu), reads=[pb[bB]], writes=ysil.b)
                S.pool(lambda e: e.tensor_tensor(out=sq[:], in0=ysil[:], in1=ysil[:], op=ALU.mult), reads=ysil.b, writes=sq.b)
                yield
                stage_c(tb)
                yield

        def rr_zip(gens):
            gens = list(gens)
            while gens:
                for g_ in list(gens):
                    try:
                        next(g_)
                    except StopIteration:
                        gens.remove(g_)
                    yield

        def g1_gen(h):
            return rr_zip([g1_chunk(0, h, qT, G1T[0], G1B[0], G1B[1]), g1_chunk(1, h, kT, G1T[1], G1B[2], G1B[3]), g1_chunk(2, h, vT, G1T[2], G1B[4], G1B[5])])

        def g23(h):
            for srcT, dst in ((kT, Ktok), (vT, Vtok)):
                for half in range(2):
                    bk = 6 + half
                    for j in range(8):
                        t = half * 8 + j
                        tr(bank_bf(bk)[:, j * 128:(j + 1) * 128], srcT[:, t * 128:(t + 1) * 128], srcT.b, [pb[bk]])
                    S.act(lambda e, bk=bk, dst=dst, half=half: e.activation(out=dst[:, half * 8:(half + 1) * 8, :], in_=bank_bf(bk).rearrange("p (t n) -> p t n", t=8), func=AF.Copy),
                          reads=[pb[bk]], writes=dst.b)
            for d in range(2):
                DD = DIRS[d]
                hd = d * 8 + h
                bcc = lambda tl, hd=hd: tl[:, :, hd:hd + 1].to_broadcast([128, NT, 128])
                S.pool(lambda e, DD=DD, bcc=bcc: e.tensor_tensor(out=DD["Kbg"][:], in0=Ktok[:], in1=bcc(cbg), op=ALU.mult), reads=Ktok.b + cbg.b, writes=DD["Kbg"].b)
                S.dve(lambda e, DD=DD, bcc=bcc: e.tensor_tensor(out=DD["Kd"][:], in0=Ktok[:], in1=bcc(cd), op=ALU.mult), reads=Ktok.b + cd.b, writes=DD["Kd"].b)
                S.dve(lambda e, DD=DD, bcc=bcc: e.tensor_tensor(out=DD["Vb"][:], in0=Vtok[:], in1=bcc(beta), op=ALU.mult), reads=Vtok.b + beta.b, writes=DD["Vb"].b)

        def prep_inst(h, d, half, TP, b0):
            DD = DIRS[d]
            hd = d * 8 + h
            MS = masks["ms_f" if d == 0 else "ms_b"]
            MI = masks["mi_f" if d == 0 else "mi_b"]
            diagG, ET, EMs, EMi, expR, X, N, P = (TP[k_] for k_ in ("diagG", "ET", "EMs", "EMi", "expR", "X", "N", "P"))
            Dm, Y = diagG, ET
            t0 = half * 8
            tsl = slice(t0, t0 + 8)
            csl = slice(t0 * 128, (t0 + 8) * 128)
            B0, B1, B2, B3 = b0, b0 + 1, b0 + 2, b0 + 3
            v3 = lambda bk: bank2(bk).rearrange("p a (b c) -> p (a b) c", c=128)
            blk = lambda bk, j: bank(bk + j // 4)[:, (j % 4) * 128:(j % 4 + 1) * 128]
            gcb = gc[:, tsl, hd:hd + 1].to_broadcast([128, 8, 128])
            S.pool(lambda e: e.tensor_tensor(out=diagG[:], in0=rep8(cst["ident"]), in1=gcb, op=ALU.mult), reads=cst["ident"].b + gc.b, writes=diagG.b)
            for q in range(2):
                mm(bank(B0 + q), cst["ones"][:], diagG[:, q * 4:(q + 1) * 4, :].rearrange("p a b -> p (a b)"), True, True, cst["ones"].b + diagG.b, [pb[B0 + q]])
            yield
            S.act(lambda e: e.activation(out=expR[:], in_=v3(B0), func=AF.Exp), reads=[pb[B0], pb[B1]], writes=expR.b)
            S.dve(lambda e: e.tensor_tensor(out=Dm[:], in0=v3(B0), in1=gcb, op=ALU.subtract), reads=[pb[B0], pb[B1]] + gc.b, writes=Dm.b)
            S.dve(lambda e: e.tensor_scalar(out=Dm[:], in0=Dm[:], scalar1=0.0, scalar2=None, op0=ALU.min), reads=Dm.b, writes=Dm.b)
            yield
            S.act(lambda e: e.activation(out=ET[:], in_=Dm[:], func=AF.Exp), reads=Dm.b, writes=ET.b)
            S.pool(lambda e: e.tensor_tensor(out=EMs[:], in0=ET[:], in1=rep8(MS), op=ALU.mult), reads=ET.b + MS.b, writes=EMs.b)
            S.pool(lambda e: e.tensor_tensor(out=EMi[:], in0=ET[:], in1=rep8(MI), op=ALU.mult), reads=ET.b + MI.b, writes=EMi.b)
            S.pool(lambda e: e.tensor_tensor(out=DD["QgT"][:, csl], in0=qT[:, csl], in1=expR[:].rearrange("p a b -> p (a b)"), op=ALU.mult),
                   reads=qT.b + expR.b, writes=DD["QgT"].b)
            for j in range(8):
                ksl = slice((t0 + j) * 128, (t0 + j + 1) * 128)
                mm(blk(B2, j), kT[:, ksl], kT[:, ksl], True, True, kT.b, [pb[B2 + j // 4]])
            for j in range(8):
                ksl = slice((t0 + j) * 128, (t0 + j + 1) * 128)
                mm(blk(B0, j), kT[:, ksl], qT[:, ksl], True, True, kT.b + qT.b, [pb[B0 + j // 4]])
            yield
            S.dve(lambda e: e.tensor_tensor(out=Y[:], in0=v3(B2), in1=EMs[:], op=ALU.mult), reads=[pb[B2], pb[B3]] + EMs.b, writes=Y.b)
            S.dve(lambda e: e.tensor_tensor(out=DD["intraT"][:, tsl, :], in0=v3(B0), in1=EMi[:], op=ALU.mult),
                  reads=[pb[B0], pb[B1]] + EMi.b, writes=DD["intraT"].b)
            for j in range(8):
                tr(bank_bf(B2)[:, j * 128:(j + 1) * 128], Y[:, j, :], Y.b, [pb[B2]])
            yield
            nbb = nbeta[:, tsl, hd:hd + 1].to_broadcast([128, 8, 128])
            S.dve(lambda e: e.tensor_tensor(out=N[:], in0=bank_bf(B2).rearrange("p (t n) -> p t n", t=8), in1=nbb, op=ALU.mult),
                  reads=[pb[B2]] + nbeta.b, writes=N.b)
            for j in range(8):
                tr(bank_bf(B3)[:, j * 128:(j + 1) * 128], N[:, j, :], N.b, [pb[B3]])
            yield
            S.act(lambda e: e.activation(out=X[:], in_=bank_bf(B3).rearrange("p (t n) -> p t n", t=8), func=AF.Copy), reads=[pb[B3]], writes=X.b)
            S.dve(lambda e: e.tensor_tensor(out=P[:], in0=bank_bf(B3).rearrange("p (t n) -> p t n", t=8), in1=rep8(ident_bf), op=ALU.add),
                  reads=[pb[B3]] + ident_bf.b, writes=P.b)
            yield
            for n in range(1, 6):
                for j in range(8):
                    mm(blk(B0, j), X[:, j, :], N[:, j, :], True, True, X.b + N.b, [pb[B0 + j // 4]])
                if n <= 4:
                    for j in range(8):
                        mm(blk(B2, j), N[:, j, :], X[:, j, :], True, True, X.b + N.b, [pb[B2 + j // 4]])
                yield
                S.act(lambda e: e.activation(out=N[:], in_=v3(B0), func=AF.Copy), reads=[pb[B0], pb[B1]], writes=N.b)
                if n <= 4:
                    S.dve(lambda e: e.tensor_copy(out=X[:], in_=v3(B2)), reads=[pb[B2], pb[B3]], writes=X.b)
                for j in range(8):
                    mm(blk(B0, j), N[:, j, :], P[:, j, :], True, True, N.b + P.b, [pb[B0 + j // 4]])
                yield
                S.dve(lambda e: e.tensor_tensor(out=P[:], in0=v3(B0), in1=P[:], op=ALU.add), reads=[pb[B0], pb[B1]] + P.b, writes=P.b)
            for j in range(8):
                t = t0 + j
                mm(blk(B2, j), P[:, j, :], DD["Vb"][:, t, :], True, True, P.b + DD["Vb"].b, [pb[B2 + j // 4]])
            for j in range(8):
                t = t0 + j
                mm(blk(B0, j), DD["Kbg"][:, t, :], P[:, j, :], True, True, P.b + DD["Kbg"].b, [pb[B0 + j // 4]])
            yield
            S.act(lambda e: e.activation(out=DD["U"][:, tsl, :], in_=v3(B2), func=AF.Copy), reads=[pb[B2], pb[B3]], writes=DD["U"].b)
            S.dve(lambda e: e.tensor_copy(out=DD["WT"][:, csl], in_=bank2(B0).rearrange("p a b -> p (a b)")), reads=[pb[B0], pb[B1]], writes=DD["WT"].b)
            yield

        def scan_gen(h):
            for d in range(2):
                S.pool(lambda e, d=d: e.memset(Sf[d][:], 0.0), writes=Sf[d].b)
                S.pool(lambda e, d=d: e.memset(Sb[d][:], 0.0), writes=Sb[d].b)
            seen = [0] * NB
            for step in range(32):
                info = []
                for d in range(2):
                    ci = step if d == 0 else 31 - step
                    t, jc = ci // 2, ci % 2
                    info.append(dict(d=d, DD=DIRS[d], hd=d * 8 + h, ci=ci, t=t, jc=jc, p0=jc * 64, blk=ci // 8, obk=d, wsb=2 + d, sub=4 + d,
                                     V=vn[d][step % 2], csl=slice(ci * 64, (ci + 1) * 64), oc=slice((ci % 8) * 64, (ci % 8 + 1) * 64)))
                for I in info:
                    d, DD, t = I["d"], I["DD"], I["t"]
                    mm(bank(I["wsb"])[:, 0:128], DD["WT"][:, t * 128:(t + 1) * 128], Sb[d][:], True, True, DD["WT"].b + Sb[d].b, [pb[I["wsb"]]])
                for I in info:
                    d, DD = I["d"], I["DD"]
                    mm(bank(I["obk"])[:, I["oc"]], Sb[d][:], DD["QgT"][:, I["csl"]], True, False, Sb[d].b + DD["QgT"].b, [pb[I["obk"]]])
                for I in info:
                    DD, V, t, p0, wsb = I["DD"], I["V"], I["t"], I["p0"], I["wsb"]
                    S.dve(lambda e, DD=DD, V=V, t=t, p0=p0, wsb=wsb: e.tensor_tensor(out=V[p0:p0 + 64, :], in0=DD["U"][p0:p0 + 64, t, :], in1=bank(wsb)[p0:p0 + 64, 0:128], op=ALU.subtract),
                          reads=DD["U"].b + [pb[wsb]], writes=V.b)
                for I in info:
                    DD, V, t, p0, sub = I["DD"], I["V"], I["t"], I["p0"], I["sub"]
                    mm(bank(sub)[:, 0:128], DD["Kd"][p0:p0 + 64, t, :], V[p0:p0 + 64, :], True, True, DD["Kd"].b + V.b, [pb[sub]])
                for I in info:
                    d, DD, V, t, p0 = I["d"], I["DD"], I["V"], I["t"], I["p0"]
                    mm(bank(I["obk"])[:, I["oc"]], V[p0:p0 + 64, :], DD["intraT"][p0:p0 + 64, t, p0:p0 + 64], False, True, V.b + DD["intraT"].b, [pb[I["obk"]]])
                if SCAN_DVE_SB:
                    for I in info:
                        d, sub = I["d"], I["sub"]
                        eg = egrep[:, I["t"], I["jc"], I["hd"]:I["hd"] + 1]
                        S.dve(lambda e, d=d, eg=eg, sub=sub: e.scalar_tensor_tensor(out=Sb[d][:], in0=Sf[d][:], scalar=eg, in1=bank(sub)[:, 0:128], op0=ALU.mult, op1=ALU.add),
                              reads=Sf[d].b + egrep.b + [pb[sub]], writes=Sb[d].b)
                for I in info:
                    d, sub = I["d"], I["sub"]
                    eg = egrep[:, I["t"], I["jc"], I["hd"]:I["hd"] + 1]
                    S.dve(lambda e, d=d, eg=eg, sub=sub: e.scalar_tensor_tensor(out=Sf[d][:], in0=Sf[d][:], scalar=eg, in1=bank(sub)[:, 0:128], op0=ALU.mult, op1=ALU.add),
                          reads=Sf[d].b + egrep.b + [pb[sub]], writes=Sf[d].b)
                if not SCAN_DVE_SB:
                    for I in info:
                        d = I["d"]
                        S.act(lambda e, d=d: e.activation(out=Sb[d][:], in_=Sf[d][:], func=AF.Copy), reads=Sf[d].b, writes=Sb[d].b)
                for I in info:
                    d, ci, blk_, obk = I["d"], I["ci"], I["blk"], I["obk"]
                    last = (ci % 8 == 7) if d == 0 else (ci % 8 == 0)
                    if last:
                        bsl = slice(blk_ * 512, (blk_ + 1) * 512)
                        if seen[blk_] == 0:
                            S.act(lambda e, bsl=bsl, obk=obk: e.activation(out=oacc[:, bsl], in_=bank(obk), func=AF.Copy), reads=[pb[obk]], writes=[oacc.b[blk_]])
                        else:
                            S.dve(lambda e, bsl=bsl, obk=obk: e.tensor_tensor(out=oacc[:, bsl], in0=bank(obk), in1=oacc[:, bsl], op=ALU.add), reads=[pb[obk], oacc.b[blk_]], writes=[oacc.b[blk_]])
                        seen[blk_] += 1
                yield
            yield from silu_proj_gen(O_ZG + h * 128, zs_g, banks=(6, 6))
            for tb in range(NB):
                sl = slice(tb * 512, (tb + 1) * 512)
                S.act(lambda e, sl=sl: e.activation(out=fsq[:], in_=oacc[:, sl], func=AF.Square), reads=[oacc.b[tb]], writes=fsq.b)
                mm(bank(7), ones_bf[:], fsq[:], True, True, ones_bf.b + fsq.b, [pb[7]])
                rsqrt_act(frr[:], bank(7), 1.0 / 128, [pb[7]], frr.b)
                S.dve(lambda e, sl=sl: e.tensor_tensor(out=frr[:], in0=oacc[:, sl], in1=frr[:], op=ALU.mult), reads=[oacc.b[tb]] + frr.b, writes=frr.b)
                S.dve(lambda e, sl=sl: e.scalar_tensor_tensor(out=ob_g[:, sl], in0=frr[:], scalar=gcol_gdn[:, 0:1], in1=zs_g[:, sl], op0=ALU.mult, op1=ALU.mult),
                      reads=frr.b + gcol_gdn.b + zs_g.b, writes=ob_g.b)
                yield
            S.dma(lambda e: e.dma_start(out=scr[h], in_=ob_g[:]), reads=ob_g.b, writes=[scrb[h]])

        run(g1_gen(0))
        for h in range(8):
            g23(h)
            S.barrier()
            for half in range(2):
                run(rr_zip([prep_inst(h, 0, half, PT[0], 0), prep_inst(h, 1, half, PT[1], 4)]))
            S.barrier()
            zipper(scan_gen(h), g1_gen(h + 1) if h < 7 else None, every=GE, nside=GZ)

    def attn_gen(kT_fn, q_fn, v_fn, nkt, scale, zs, dst_chunk, AT, extra=None):
        pTs, rec, tmpo, ob = AT["pTs"], AT["rec"], AT["tmpo"], AT["ob"][AT["ctr"] % 2]
        AT["ctr"] += 1
        its = [(qb, kt) for qb in range(NB) for kt in range(nkt)]
        SBK = (0, 1, 2)

        def emit_s(i):
            qb, kt = its[i]
            sbk = SBK[i % len(SBK)]
            pT = pTs[i % len(pTs)]
            ka, kb_ = kT_fn(kt)
            qa, qb_ = q_fn(qb)
            mm(bank(sbk), ka, qa, True, extra is None, kb_ + qb_, [pb[sbk]])
            if extra is not None:
                k2, k2b = extra[0](kt)
                q2, q2b = extra[1](qb)
                mm(bank(sbk), k2, q2, False, True, k2b + q2b, [pb[sbk]])
            S.act(lambda e, pT=pT, sbk=sbk: e.activation(out=pT[:], in_=bank(sbk), func=AF.Exp, scale=scale), reads=[pb[sbk]], writes=pT.b)

        def emit_pv(i):
            qb, kt = its[i]
            ob_, sb_ = 3 + (qb % 2), 5
            pT = pTs[i % len(pTs)]
            va, vb_ = v_fn(kt)
            mm(bank(ob_), va, pT[:], kt == 0, kt == nkt - 1, vb_ + pT.b, [pb[ob_]])
            if PAIRSUM and nkt % 2 == 0:
                if kt % 2 == 1:
                    pP = pTs[(i - 1) % len(pTs)]
                    p2 = AT["p2"][(i // 2) % 2]
                    S.pool(lambda e, p2=p2, pP=pP, pT=pT: e.tensor_tensor(out=p2[:], in0=pP[:], in1=pT[:], op=ALU.add), reads=pP.b + pT.b, writes=p2.b)
                    mm(bank(sb_), ones_bf[:], p2[:], kt == 1, kt == nkt - 1, ones_bf.b + p2.b, [pb[sb_]])
            else:
                mm(bank(sb_), ones_bf[:], pT[:], kt == 0, kt == nkt - 1, ones_bf.b + pT.b, [pb[sb_]])
            if kt == nkt - 1:
                sl = slice(qb * 512, (qb + 1) * 512)
                S.act(lambda e, sb_=sb_: e.activation(out=rec[:], in_=bank(sb_), func=AF.Copy), reads=[pb[sb_]], writes=rec.b)
                S.dve(lambda e: e.reciprocal(out=rec[:], in_=rec[:]), reads=rec.b, writes=rec.b)
                S.dve(lambda e, ob_=ob_: e.tensor_tensor(out=tmpo[:], in0=bank(ob_), in1=rec[:], op=ALU.mult), reads=[pb[ob_]] + rec.b, writes=tmpo.b)
                S.pool(lambda e, sl=sl: e.tensor_tensor(out=ob[:, sl], in0=tmpo[:], in1=zs[:, sl], op=ALU.mult), reads=tmpo.b + zs.b, writes=ob.b)

        import os as _os
        SK = int(_os.environ.get("SKEW", "3"))
        PAIRSUM = int(_os.environ.get("PAIRSUM", "0"))
        for i0 in range(min(SK, len(its))):
            emit_s(i0)
        for i in range(len(its)):
            if i + SK < len(its):
                emit_s(i + SK)
            emit_pv(i)
            yield
        S.dma(lambda e: e.dma_start(out=scr[dst_chunk], in_=ob[:]), reads=ob.b, writes=[scrb[dst_chunk]])

    def attn_temps():
        return dict(pTs=[alloc(f"pT{i}", [128, 512], BF16) for i in range(5)], rec=alloc("a_rec", [128, 512], F32),
                    tmpo=alloc("a_tmpo", [128, 512], F32), p2=[alloc(f"a_p2{i}", [128, 512], BF16) for i in range(2)], sacc=[alloc(f"a_sacc{i}", [128, 512], F32) for i in range(2)], ob=[alloc(f"a_ob{i}", [128, T], BF16) for i in range(2)], ctr=0)

    def phase_mla(s):
        areset()
        cqg = alloc("cqg", [128, 3, T], BF16)
        kvg = alloc("kvg", [128, 2, T], BF16)
        rq = alloc("rq", [128, T], F32)
        rkv = alloc("rkv", [128, T], F32)
        rkvc = alloc("rkvc", [128, NT], F32)
        kpeT = alloc("kpeT", [64, T], BF16)
        sqs = [alloc(f"msq{i}", [128, 512], BF16) for i in range(3)]
        r1 = alloc("r1", [64, 512], F32)
        r2 = alloc("r2", [64, 512], F32)
        cos2 = alloc("cos2", [64, T], F32)
        sins = alloc("sins", [64, T], F32)
        S.dma(lambda e: e.dma_start(out=cos2[:], in_=dr["cos2"]), writes=cos2.b)
        S.dma(lambda e: e.dma_start(out=sins[:], in_=dr["sins"]), writes=sins.b)
        for (col0, nchunk, dstg, gcolt, rdst, nfeat) in ((O_CQ, 3, cqg, gcol_q, rq, 384), (O_CKV, 2, kvg, gcol_kv, rkv, 256)):
            wts = [load_w(w_in, col0 + c * 128, 128) for c in range(nchunk)]
            for tb in range(NB):
                sl = slice(tb * 512, (tb + 1) * 512)
                for c in range(nchunk):
                    bk = 6 + (c % 2)
                    proj_fm(wts[c], tb, bk)
                    S.act(lambda e, bk=bk, c=c: e.activation(out=sqs[c][:], in_=bank(bk), func=AF.Square), reads=[pb[bk]], writes=sqs[c].b)
                    S.dve(lambda e, bk=bk, c=c, sl=sl, dstg=dstg, gcolt=gcolt: e.tensor_scalar(out=dstg[:, c, sl], in0=bank(bk), scalar1=gcolt[:, c:c + 1], scalar2=None, op0=ALU.mult),
                          reads=[pb[bk]] + gcolt.b + sqs[c].b, writes=dstg.b)
                for c in range(nchunk):
                    mm(bank(5), ones_bf[:], sqs[c][:], c == 0, c == nchunk - 1, ones_bf.b + sqs[c].b, [pb[5]])
                S.act(lambda e, sl=sl, rdst=rdst, nfeat=nfeat: e.activation(out=rdst[:, sl], in_=bank(5), func=AF.Sqrt, scale=1.0 / nfeat, bias=eps_t[:, 0:1]),
                      reads=[pb[5]] + eps_t.b, writes=rdst.b)
                S.dve(lambda e, sl=sl, rdst=rdst: e.reciprocal(out=rdst[:, sl], in_=rdst[:, sl]), reads=rdst.b, writes=rdst.b)
                import os as _os
                if nchunk == 2 and not _os.environ.get("NO_N1"):
                    for j in range(4):
                        t = tb * 4 + j
                        for c in range(2):
                            mm(bank(4)[:, t:t + 1], sqs[c][:, j * 128:(j + 1) * 128], ones_bf[:, 0:1], c == 0, c == 1, sqs[c].b + ones_bf.b, [pb[4]])
        S.act(lambda e: e.activation(out=rkvc[:], in_=bank(4)[:, 0:NT], func=AF.Sqrt, scale=1.0 / 256, bias=eps_t[:, 0:1]), reads=[pb[4]] + eps_t.b, writes=rkvc.b)
        S.dve(lambda e: e.reciprocal(out=rkvc[:], in_=rkvc[:]), reads=rkvc.b, writes=rkvc.b)

        def rope_pair(wsrc, colA, nk, rhs_fn, dst, rmul):
            wa = load_w(wsrc, colA, 64, nk=nk)
            wsw = wpool[wctr[0] % len(wpool)]
            wctr[0] += 1
            srcA = wsrc[:, colA + 32:colA + 64].rearrange("(k p) n -> p k n", p=128)
            srcB = wsrc[:, colA:colA + 32].rearrange("(k p) n -> p k n", p=128)
            S.dma(lambda e: e.dma_start(out=wsw[:, 0:nk, 0:32], in_=srcA), writes=wsw.b, eng="gpsimd")
            S.dma(lambda e: e.dma_start(out=wsw[:, 0:nk, 32:64], in_=srcB), writes=wsw.b, eng="gpsimd")
            for tb in range(NB):
                sl = slice(tb * 512, (tb + 1) * 512)
                for k in range(nk):
                    ra, rb = rhs_fn(k, sl)
                    mm(bank(6)[0:64, :], wa[:, k, 0:64], ra, k == 0, k == nk - 1, wa.b + rb, [pb[6]])
                for k in range(nk):
                    ra, rb = rhs_fn(k, sl)
                    mm(bank(7)[0:64, :], wsw[:, k, 0:64], ra, k == 0, k == nk - 1, wsw.b + rb, [pb[7]])
                S.dve(lambda e, sl=sl: e.tensor_tensor(out=r1[:], in0=bank(6)[0:64, :], in1=cos2[:, sl], op=ALU.mult), reads=[pb[6]] + cos2.b, writes=r1.b)
                S.dve(lambda e, sl=sl: e.tensor_tensor(out=r2[:], in0=bank(7)[0:64, :], in1=sins[:, sl], op=ALU.mult), reads=[pb[7]] + sins.b, writes=r2.b)
                if rmul is None:
                    S.dve(lambda e, sl=sl: e.tensor_tensor(out=dst[0:64, sl], in0=r1[:], in1=r2[:], op=ALU.add), reads=r1.b + r2.b, writes=dst.b)
                else:
                    S.dve(lambda e: e.tensor_tensor(out=r1[:], in0=r1[:], in1=r2[:], op=ALU.add), reads=r1.b + r2.b, writes=r1.b)
                    S.dve(lambda e, sl=sl: e.tensor_tensor(out=dst[0:64, sl], in0=r1[:], in1=rmul[0:64, sl], op=ALU.mult), reads=r1.b + rmul.b, writes=dst.b)
                yield

        import os as _os
        _stop = int(_os.environ.get("MLA_STOP", "9"))
        if _stop < 1:
            return
        run(rope_pair(w_in, O_CKV + 256, 8, lambda k, sl: (hT[:, k, sl], [hT.b[sl.start // 512]]), kpeT, None))
        wq = dr["w_q_up"][0]
        wkv = dr["w_kv_up"][0]
        HB = [dict(qnT=alloc(f"qnT{i}", [128, T], BF16), qpT=alloc(f"qpT{i}", [64, T], BF16), knT=alloc(f"knT{i}", [128, T], BF16),
                   vtok=alloc(f"vtok{i}", [128, NT, 128], BF16), zs=alloc(f"zs_m{i}", [128, T], BF16)) for i in range(2)]
        AT = attn_temps()

        def prep_gen(h, B):
            qnT, qpT, knT, vtok, zs = B["qnT"], B["qpT"], B["knT"], B["vtok"], B["zs"]
            wqn = load_w(wq, h * 192, 128, nk=3)
            wkn = load_w(wkv, h * 256, 128, nk=2)
            wv = load_w(wkv, h * 256 + 128, 128, nk=2)
            for tb in range(NB):
                sl = slice(tb * 512, (tb + 1) * 512)
                for k in range(3):
                    mm(bank(6), wqn[:, k, :], cqg[:, k, sl], k == 0, k == 2, wqn.b + cqg.b, [pb[6]])
                S.dve(lambda e, sl=sl: e.tensor_tensor(out=qnT[:, sl], in0=bank(6), in1=rq[:, sl], op=ALU.mult), reads=[pb[6]] + rq.b, writes=qnT.b)
                for k in range(2):
                    mm(bank(7), wkn[:, k, :], kvg[:, k, sl], k == 0, k == 1, wkn.b + kvg.b, [pb[7]])
                S.dve(lambda e, sl=sl: e.tensor_tensor(out=knT[:, sl], in0=bank(7), in1=rkv[:, sl], op=ALU.mult), reads=[pb[7]] + rkv.b, writes=knT.b)
                yield
            for q4 in range(4):
                bk = 6 + (q4 % 2)
                for j in range(4):
                    t = q4 * 4 + j
                    for k in range(2):
                        mm(bank(bk)[:, j * 128:(j + 1) * 128], kvg[:, k, t * 128:(t + 1) * 128], wv[:, k, :], k == 0, k == 1, kvg.b + wv.b, [pb[bk]])
                S.dve(lambda e, q4=q4, bk=bk: e.tensor_tensor(out=vtok[:, q4 * 4:(q4 + 1) * 4, :], in0=bank(bk).rearrange("p (t n) -> p t n", t=4),
                                                             in1=rkvc[:, q4 * 4:(q4 + 1) * 4].unsqueeze(2).to_broadcast([128, 4, 128]), op=ALU.mult),
                      reads=[pb[bk]] + rkvc.b, writes=vtok.b)
                yield
            yield from rope_pair(wq, h * 192 + 128, 3, lambda k, sl: (cqg[:, k, sl], cqg.b), qpT, rq)
            yield from silu_proj_gen(O_ZM + h * 128, zs)

        def head_attn(h, B):
            qnT, qpT, knT, vtok, zs = B["qnT"], B["qpT"], B["knT"], B["vtok"], B["zs"]
            return attn_gen(lambda kt: (knT[:, kt * 128:(kt + 1) * 128], knT.b),
                            lambda qb: (qnT[:, qb * 512:(qb + 1) * 512], qnT.b),
                            lambda kt: (vtok[:, kt, :], vtok.b), NT, SC_MLA, zs, 8 + h, AT,
                            extra=(lambda kt: (kpeT[0:64, kt * 128:(kt + 1) * 128], kpeT.b),
                                   lambda qb: (qpT[0:64, qb * 512:(qb + 1) * 512], qpT.b)))

        run(prep_gen(0, HB[0]))
        for h in range(8):
            side = prep_gen(h + 1, HB[(h + 1) % 2]) if h + 1 < 8 else None
            zipper(head_attn(h, HB[h % 2]), side, every=3)

    def phase_mem(s):
        areset()
        mt_ = [alloc(f"memt{i}", [128, D], F32) for i in range(2)]
        junk = alloc("mjunk", [128, D], BF16)
        mb = alloc("mb", [128, D], BF16)
        mst = alloc("mst", [128, 2], F32)
        mnT = alloc("mnT", [128, 8, 256], BF16)
        kmT = alloc("kmT", [128, 4, 256], BF16)
        vm = alloc("vm", [128, 2, 512], BF16)
        wmv = alloc("wmv", [128, 8, 512], BF16)
        wm = dr["w_mem_kv"][0]
        for i in range(2):
            X = mt_[i]
            S.dma(lambda e, X=X, i=i: e.dma_start(out=X[:], in_=mem[s, i * 128:(i + 1) * 128, :]), writes=X.b)
            S.act(lambda e, X=X: e.activation(out=junk[:], in_=X[:], func=AF.Square, accum_out=mst[:, 0:1]), reads=X.b, writes=junk.b + mst.b)
            rsqrt_inplace(mst[:, 0:1], mst.b, 1.0 / D, EPS)
            S.dve(lambda e, X=X: e.tensor_scalar(out=mb[:], in0=X[:], scalar1=mst[:, 0:1], scalar2=None, op0=ALU.mult), reads=X.b + mst.b, writes=mb.b)
            for k in range(8):
                tr(bank_bf(6)[:, k * 128:(k + 1) * 128], mb[:, k * 128:(k + 1) * 128], mb.b, [pb[6]])
            S.dve(lambda e, i=i: e.tensor_tensor(out=mnT[:, :, i * 128:(i + 1) * 128], in0=bank_bf(6).rearrange("p (k n) -> p k n", k=8),
                                                 in1=gcol_mem[:].unsqueeze(2).to_broadcast([128, 8, 128]), op=ALU.mult),
                  reads=[pb[6]] + gcol_mem.b, writes=mnT.b)
        for h in range(4):
            wt = load_w(wm, h * 128, 128)
            for k in range(8):
                mm(bank(7)[:, 0:256], wt[:, k, :], mnT[:, k, :], k == 0, k == 7, wt.b + mnT.b, [pb[7]])
            S.act(lambda e, h=h: e.activation(out=kmT[:, h, :], in_=bank(7)[:, 0:256], func=AF.Copy), reads=[pb[7]], writes=kmT.b)
        load_w(wm, 512, 512, dst=wmv)
        for i in range(2):
            for k in range(8):
                mm(bank(6), mnT[:, k, i * 128:(i + 1) * 128], wmv[:, k, :], k == 0, k == 7, mnT.b + wmv.b, [pb[6]])
            S.act(lambda e, i=i: e.activation(out=vm[:, i, :], in_=bank(6), func=AF.Copy), reads=[pb[6]], writes=vm.b)
        AT = attn_temps()
        MB = [dict(qmT=alloc(f"qmT{i}", [128, T], BF16), zs=alloc(f"zs_mem{i}", [128, T], BF16)) for i in range(2)]

        def mprep_gen(h, B):
            qmT = B["qmT"]
            wt = load_w(w_in, O_QM + h * 128, 128)
            for tb in range(NB):
                bk = 6 + (tb % 2)
                proj_fm(wt, tb, bk)
                S.act(lambda e, bk=bk, tb=tb: e.activation(out=qmT[:, tb * 512:(tb + 1) * 512], in_=bank(bk), func=AF.Copy), reads=[pb[bk]], writes=qmT.b)
                yield
            yield from silu_proj_gen(O_ZMEM + h * 128, B["zs"])

        def mattn(h, B):
            qmT = B["qmT"]
            return attn_gen(lambda kt: (kmT[:, h, kt * 128:(kt + 1) * 128], kmT.b),
                            lambda qb: (qmT[:, qb * 512:(qb + 1) * 512], qmT.b),
                            lambda kt: (vm[:, kt, h * 128:(h + 1) * 128], vm.b), 2, SC_MEM, B["zs"], 16 + h, AT)

        run(mprep_gen(0, MB[0]))
        for h in range(4):
            side = mprep_gen(h + 1, MB[(h + 1) % 2]) if h + 1 < 4 else None
            zipper(mattn(h, MB[h % 2]), side, every=1)

    TOP_OFF = ARENA_ELEMS - 28 * 1024

    def merge_weight_tiles():
        save = astate["off"]
        astate["off"] = TOP_OFF
        wbr = alloc("wbr", [128, 20, D], BF16)
        wout = alloc("wout", [128, 8, D], BF16)
        astate["off"] = save
        return wbr, wout

    def prefetch_merge_weights():
        wbr, wout = merge_weight_tiles()
        for br, (nm, nk, c0) in enumerate((("w_br_gdn", 8, 0), ("w_br_mla", 8, 8), ("w_br_mem", 4, 16))):
            src = dr[nm][0].rearrange("(k p) n -> p k n", p=128)
            for k in range(0, nk, 4):
                S.dma(lambda e, src=src, k=k, c0=c0: e.dma_start(out=wbr[:, c0 + k:c0 + k + 4, :], in_=src[:, k:k + 4, :]), writes=wbr.b, eng="gpsimd")
        src = dr["w_out"][0].rearrange("(k p) n -> p k n", p=128)
        for k in range(0, 8, 4):
            S.dma(lambda e, src=src, k=k: e.dma_start(out=wout[:, k:k + 4, :], in_=src[:, k:k + 4, :]), writes=wout.b, eng="gpsimd")

    def phase_merge(s, prefetched):
        areset()
        if not prefetched:
            prefetch_merge_weights()
        wbr, wout = merge_weight_tiles()
        oT = [alloc(f"oTb{i}", [128, 20, 512], BF16) for i in range(2)]
        mgs = [alloc(f"mg{i}", [128, 8, 512], BF16) for i in range(2)]
        acc = alloc("macc", [128, 512], F32)
        sig = [alloc(f"sig{i}", [128, 512], F32) for i in range(2)]
        tmp = alloc("mtmp", [128, 512], F32)
        xr = [alloc(f"xr{i}", [128, D], F32) for i in range(2)]
        ysb = [alloc(f"ysb{i}", [128, D], F32) for i in range(2)]
        junk = alloc("fjunk", [128, D], BF16)
        st = [alloc(f"fst{i}", [128, 2], F32) for i in range(2)]
        assert astate["off"] <= TOP_OFF
        branches = ((0, 8, 0), (1, 8, 8), (2, 4, 16))

        def gates_gen(tb):
            sl = slice(tb * 512, (tb + 1) * 512)
            O = oT[tb % 2]
            mg = mgs[tb % 2]
            S.dma(lambda e, O=O, sl=sl: e.dma_start(out=O[:], in_=scr[:, :, sl].rearrange("c p n -> p c n")), reads=scrb, writes=O.b)
            for fo in range(8):
                fsl = slice(fo * 128, (fo + 1) * 128)
                for br, nk, c0 in branches:
                    wg = load_w(w_in, O_GATE + br * 1024 + fo * 128, 128)
                    gb = 6 + (br % 2)
                    proj_fm(wg, tb, gb)
                    SG = sig[br % 2]
                    S.act(lambda e, gb=gb, SG=SG: e.activation(out=SG[:], in_=bank(gb), func=AF.Sigmoid), reads=[pb[gb]], writes=SG.b)
                    bb = 4 + (br % 2)
                    for k in range(nk):
                        mm(bank(bb), wbr[:, c0 + k, fsl], O[:, c0 + k, :], k == 0, k == nk - 1, wbr.b + O.b, [pb[bb]])
                    if br == 0:
                        S.dve(lambda e, bb=bb, SG=SG: e.tensor_tensor(out=acc[:], in0=bank(bb), in1=SG[:], op=ALU.mult), reads=[pb[bb]] + SG.b, writes=acc.b)
                    else:
                        S.dve(lambda e, bb=bb, SG=SG: e.tensor_tensor(out=tmp[:], in0=bank(bb), in1=SG[:], op=ALU.mult), reads=[pb[bb]] + SG.b, writes=tmp.b)
                        if br == 1:
                            S.dve(lambda e: e.tensor_tensor(out=acc[:], in0=acc[:], in1=tmp[:], op=ALU.add), reads=acc.b + tmp.b, writes=acc.b)
                        else:
                            S.dve(lambda e, fo=fo, mg=mg: e.tensor_tensor(out=mg[:, fo, :], in0=acc[:], in1=tmp[:], op=ALU.add), reads=acc.b + tmp.b, writes=mg.b)
                    yield

        def epi_gen(tb):
            mg = mgs[tb % 2]
            for j in range(4):
                t = tb * 4 + j
                X, Y, ST = xr[j % 2], ysb[j % 2], st[j % 2]
                S.dma(lambda e, X=X, t=t: e.dma_start(out=X[:], in_=x[s, t * 128:(t + 1) * 128, :]), writes=X.b)
                for hf in range(2):
                    for k in range(8):
                        mm(bank(hf), mg[:, k, j * 128:(j + 1) * 128], wout[:, k, hf * 512:(hf + 1) * 512], k == 0, k == 7, mg.b + wout.b, [pb[hf]])
                    yield
                S.dve(lambda e, X=X, Y=Y: e.tensor_tensor(out=Y[:], in0=bank2(0).rearrange("p a b -> p (a b)"), in1=X[:], op=ALU.add), reads=[pb[0], pb[1]] + X.b, writes=Y.b)
                S.act(lambda e, Y=Y, ST=ST: e.activation(out=junk[:], in_=Y[:], func=AF.Square, accum_out=ST[:, 0:1]), reads=Y.b, writes=junk.b + ST.b)
                yield
                rsqrt_inplace(ST[:, 0:1], ST.b, 1.0 / D, EPS)
                yield
                S.dve(lambda e, Y=Y, ST=ST: e.scalar_tensor_tensor(out=Y[:], in0=Y[:], scalar=ST[:, 0:1], in1=fgain[:], op0=ALU.mult, op1=ALU.mult),
                      reads=Y.b + ST.b + fgain.b, writes=Y.b)
                S.dma(lambda e, Y=Y, t=t: e.dma_start(out=y[s, t * 128:(t + 1) * 128, :], in_=Y[:]), reads=Y.b, final=True)
                yield

        run(gates_gen(0))
        for tb in range(NB):
            if tb + 1 < NB:
                zipper(gates_gen(tb + 1), epi_gen(tb), every=1)
            else:
                run(epi_gen(tb))

    for s in range(NSEQ):
        S.barrier()
        phase_h(s)
        if "G" in phases:
            S.barrier()
            phase_gdn(s)
        if "M" in phases:
            S.barrier()
            phase_mla(s)
        if "C" in phases:
            S.barrier()
            if "X" in phases:
                prefetch_merge_weights()
            phase_mem(s)
        if "X" in phases:
            S.barrier()
            phase_merge(s, "C" in phases)
    S.barrier()
    print("ops", S.stats(), flush=True)
    S.emit()
    return nc


_NC_CACHE = {}


def _run(xs, ms, weights, ncores, nseq, phases="GMCX", dbg=False):
    key = (nseq, phases, dbg)
    if key not in _NC_CACHE:
        _NC_CACHE[key] = build_nc(nseq, phases, dbg)
    nc = _NC_CACHE[key]
    consts = host_consts()
    in_maps = []
    for c in range(ncores):
        m = {"x": np.ascontiguousarray(xs[c * nseq:(c + 1) * nseq]), "mem": np.ascontiguousarray(ms[c * nseq:(c + 1) * nseq])}
        for k in WEIGHT_SHAPES:
            m[k] = np.ascontiguousarray(weights[k], dtype=np.float32)
        for k, v in consts.items():
            m["c_" + k] = v
        in_maps.append(m)
    res = run_bass_kernel_spmd(nc, in_maps, core_ids=list(range(ncores)))
    return res


def kernel(**inputs):
    xs = np.concatenate([inputs["x_prompt"], inputs["x_sample"]], axis=0)
    ms = np.concatenate([inputs["mem_prompt"], inputs["mem_sample"]], axis=0)
    weights = {k: np.asarray(inputs[k]) for k in WEIGHT_SHAPES}
    nseq = xs.shape[0] // 8
    res = _run(xs, ms, weights, 8, nseq)
    yall = np.concatenate([r["y"] for r in res.results], axis=0)
    nb = inputs["x_prompt"].shape[0]
    return (np.ascontiguousarray(yall[:nb]), np.ascontiguousarray(yall[nb:]))
```

```python
import contextlib
import math
import numpy as np
import concourse.bass as bass
import concourse.mybir as mybir
from concourse.bass_utils import run_bass_kernel_spmd

F32 = mybir.dt.float32
BF16 = mybir.dt.bfloat16
I32 = mybir.dt.int32
ALU = mybir.AluOpType
AF = mybir.ActivationFunctionType
AX = mybir.AxisListType

ENGINES = ("tensor", "vector", "scalar", "gpsimd", "sync")
N_DMA_SEMS = 12
import os as _os_fw
ATTACH_ENGINES = tuple(x for x in _os_fw.environ.get('ATTACH', 'vector,scalar,gpsimd').split(',') if x)


class Buf:
    __slots__ = ("name", "w", "r", "excl")

    def __init__(self, name):
        self.name = name
        self.excl = False
        self.w = {}
        self.r = {}


class Op:
    __slots__ = ("eng", "idx", "fn", "deps", "dma", "signal", "done_sem", "done_val", "tag")

    def __init__(self, eng, idx, fn, dma, tag):
        self.eng = eng
        self.idx = idx
        self.fn = fn
        self.dma = dma
        self.deps = {}
        self.signal = False
        self.done_sem = None
        self.done_val = None
        self.tag = tag


class Sched:
    def __init__(self, nc):
        self.nc = nc
        self.ops = {e: [] for e in ENGINES}
        self.stack = contextlib.ExitStack()
        self.nbuf = 0
        self.final_waits = []

    def sbuf(self, name, shape, dtype):
        return self.stack.enter_context(self.nc.sbuf_tensor(name, list(shape), dtype))

    def psum(self, name, shape, dtype):
        return self.stack.enter_context(self.nc.psum_tensor(name, list(shape), dtype))

    def buf(self, name=None):
        self.nbuf += 1
        return Buf(name or f"b{self.nbuf}")

    def bufs(self, n, name="b"):
        return [self.buf(f"{name}{i}") for i in range(n)]

    def add(self, eng, fn, reads=(), writes=(), dma=False, tag=None, final=False):
        lst = self.ops[eng]
        op = Op(eng, len(lst), fn, dma, tag)
        lst.append(op)
        deps = []
        for b in reads:
            deps.extend(b.w.values())
            if b.excl:
                deps.extend(o for k_, o in b.r.items() if o.eng != eng)
        for b in writes:
            deps.extend(b.w.values())
            deps.extend(b.r.values())
        for d in deps:
            if d is op:
                continue
            if d.eng == eng and eng == "tensor" and not d.dma and not dma:
                continue
            key = (d.eng, d.idx) if d.dma else d.eng
            cur = op.deps.get(key)
            if cur is None or cur.idx < d.idx:
                op.deps[key] = d
        mykey = (eng, op.idx) if dma else eng
        for b in reads:
            b.r[mykey] = op
        for b in writes:
            b.w = {mykey: op}
            b.r = {}
        if final:
            self.final_waits.append(op)
        return op

    def pe(self, fn, reads=(), writes=(), **k):
        return self.add("tensor", fn, reads, writes, **k)

    def dve(self, fn, reads=(), writes=(), **k):
        return self.add("vector", fn, reads, writes, **k)

    def act(self, fn, reads=(), writes=(), **k):
        return self.add("scalar", fn, reads, writes, **k)

    def pool(self, fn, reads=(), writes=(), **k):
        return self.add("gpsimd", fn, reads, writes, **k)

    def dma(self, fn, reads=(), writes=(), eng="sync", **k):
        return self.add(eng, fn, reads, writes, dma=True, **k)

    def emit(self):
        nc = self.nc
        for e in ENGINES:
            for op in self.ops[e]:
                for d in op.deps.values():
                    d.signal = True
        for op in self.final_waits:
            op.signal = True
        sems = {e: self.stack.enter_context(nc.semaphore(f"s_{e}")) for e in ENGINES}
        dma_sems = {e: [self.stack.enter_context(nc.semaphore(f"d_{e}_{j}")) for j in range(N_DMA_SEMS)]
                    for e in ("sync", "gpsimd", "scalar")}
        dma_prev = {}
        for e in ENGINES:
            cnt = 0
            nd = 0
            dcount = [0] * N_DMA_SEMS
            for op in self.ops[e]:
                if op.dma:
                    j = nd % N_DMA_SEMS
                    nd += 1
                    dcount[j] += 1
                    op.done_sem = dma_sems[e][j]
                    op.done_val = 16 * dcount[j]
                    op.signal = True
                elif op.signal:
                    cnt += 1
                    op.done_sem = sems[e]
                    op.done_val = cnt
        block = self.stack.enter_context(nc.Block())
        sched = self

        def run(e, eng):
            waited = {}
            for op in sched.ops[e]:
                ws = []
                for d in op.deps.values():
                    ws.append((d.done_sem, d.done_val))
                if op.dma and op.done_val > 16:
                    ws.append((op.done_sem, op.done_val - 16))
                need = []
                for sem, val in ws:
                    k = id(sem)
                    if waited.get(k, 0) >= val:
                        continue
                    waited[k] = val
                    need.append((sem, val))
                attach = None
                if need and e in ATTACH_ENGINES and not op.dma and op.tag != "multi":
                    attach = need.pop()
                for sem, val in need:
                    eng.wait_ge(sem, val)
                inst = op.fn(eng)
                if attach is not None:
                    inst._wait_ge(attach[0], attach[1])
                if op.signal:
                    inst.then_inc(op.done_sem, 16 if op.dma else 1)
            if e == "sync":
                for op in sched.final_waits:
                    eng.wait_ge(op.done_sem, op.done_val)

        @block.tensor
        def _(eng):
            run("tensor", eng)

        @block.vector
        def _(eng):
            run("vector", eng)

        @block.scalar
        def _(eng):
            run("scalar", eng)

        @block.gpsimd
        def _(eng):
            run("gpsimd", eng)

        @block.sync
        def _(eng):
            run("sync", eng)

        self.stack.close()

    def stats(self):
        return {e: len(self.ops[e]) for e in ENGINES}

def _sched_barrier(self):
    deps = []
    for e in ENGINES:
        last = None
        for op in reversed(self.ops[e]):
            if not op.dma:
                last = op
                break
        if last is not None:
            deps.append(last)
    deps.extend(self.dma_open)
    self.dma_open = []
    lst = self.ops["sync"]
    op = Op("sync", len(lst), lambda eng: eng.nop(), False, "barrier")
    lst.append(op)
    for d in deps:
        key = (d.eng, d.idx) if d.dma else d.eng
        op.deps[key] = d
    self.bar = op


Sched.barrier = _sched_barrier


class Tile:
    def __init__(self, t, nb, name, S):
        self.t = t
        self.b = [S.buf(f"{name}.{i}") for i in range(nb)]

    def __getitem__(self, k):
        return self.t[k]


T = 2048
D = 1024
NT = 16
NB = 4
EPS = 1e-6
O_QKV, O_AB, O_ZG, O_CQ, O_CKV, O_ZM, O_QM, O_ZMEM, O_GATE = 0, 3072, 3104, 4128, 4512, 4832, 5856, 6368, 6880
SC_GDN = 128 ** -0.5
SC_MLA = 192 ** -0.5
SC_MEM = 128 ** -0.5
ARENA_ELEMS = 74 * 1024


def host_consts():
    c = {}
    p = np.arange(128)
    same = (p[:, None] // 64) == (p[None, :] // 64)
    c["ident"] = np.eye(128, dtype=np.float32)
    c["ones"] = np.ones((128, 128), np.float32)
    c["lfwd"] = (same & (p[:, None] <= p[None, :])).astype(np.float32)
    c["lbwd"] = (same & (p[:, None] >= p[None, :])).astype(np.float32)
    c["cblk"] = same.astype(np.float32)
    c["c0"] = np.repeat((p < 64)[:, None], 128, 1).astype(np.float32)
    c["c1"] = np.repeat((p >= 64)[:, None], 128, 1).astype(np.float32)
    c["ms_f"] = (same & (p[:, None] < p[None, :])).astype(np.float32)
    c["mi_f"] = c["lfwd"].copy()
    c["ms_b"] = (same & (p[:, None] > p[None, :])).astype(np.float32)
    c["mi_b"] = c["lbwd"].copy()
    half = 32
    inv = 10000.0 ** (-np.arange(half, dtype=np.float32) / half)
    ang = np.arange(T, dtype=np.float32)[None, :] * inv[:, None]
    cos = np.cos(ang).astype(np.float32)
    sin = np.sin(ang).astype(np.float32)
    c["cos2"] = np.concatenate([cos, cos], 0)
    c["sins"] = np.concatenate([-sin, sin], 0)
    return {k: np.ascontiguousarray(v, dtype=np.float32) for k, v in c.items()}


CONST_SHAPES = {"ident": [128, 128], "ones": [128, 128], "lfwd": [128, 128], "lbwd": [128, 128], "cblk": [128, 128],
                "c0": [128, 128], "c1": [128, 128], "ms_f": [128, 128], "mi_f": [128, 128], "ms_b": [128, 128],
                "mi_b": [128, 128], "cos2": [64, T], "sins": [64, T]}

WEIGHT_SHAPES = {"attn_norm_gain": [1, 1024], "w_in": [1, 1024, 9952], "conv_w": [1, 5, 3072], "a_log": [1, 2, 8],
                 "dt_bias": [1, 2, 8], "gdn_norm_gain": [1, 128], "q_norm_gain": [1, 384], "w_q_up": [1, 384, 1536],
                 "kv_norm_gain": [1, 256], "w_kv_up": [1, 256, 2048], "mem_norm_gain": [1, 1024],
                 "w_mem_kv": [1, 1024, 1024], "w_br_gdn": [1, 1024, 1024], "w_br_mla": [1, 1024, 1024],
                 "w_br_mem": [1, 512, 1024], "w_out": [1, 1024, 1024], "final_norm_gain": [1024]}


def build_nc(NSEQ, phases="GMCX", dbg=False):
    nc = bass.Bass("TRN2", target_bir_lowering=False)
    dr = {}
    x = nc.dram_tensor("x", [NSEQ, T, D], F32, kind="ExternalInput").ap()
    mem = nc.dram_tensor("mem", [NSEQ, 256, D], F32, kind="ExternalInput").ap()
    for k, shp in WEIGHT_SHAPES.items():
        dr[k] = nc.dram_tensor(k, shp, F32, kind="ExternalInput").ap()
    for k, shp in CONST_SHAPES.items():
        dr[k] = nc.dram_tensor("c_" + k, shp, F32, kind="ExternalInput").ap()
    y = nc.dram_tensor("y", [NSEQ, T, D], F32, kind="ExternalOutput").ap()
    scr = nc.dram_tensor("scr", [20, 128, T], BF16, kind=("ExternalOutput" if dbg else "Internal")).ap()
    w_in = dr["w_in"][0]

    S = Sched(nc)
    S.dma_open = []
    S.bar = None
    _orig_add = S.add

    def add(eng, fn, reads=(), writes=(), dma=False, tag=None, final=False):
        op = _orig_add(eng, fn, reads, writes, dma=dma, tag=tag, final=final)
        if S.bar is not None:
            op.deps["sync"] = S.bar if ("sync" not in op.deps or op.deps["sync"].idx < S.bar.idx) else op.deps["sync"]
        if dma:
            S.dma_open.append(op)
        return op
    S.add = add

    def tile(name, shape, dtype, nb=1):
        return Tile(S.sbuf(name, shape, dtype), nb, name, S)

    cst = {}
    for k in ("ident", "ones", "lfwd", "lbwd", "cblk", "c0", "c1"):
        cst[k] = tile("k_" + k, [128, 128], F32)
    ident_bf = tile("ident_bf", [128, 128], BF16)
    ones_bf = tile("ones_bf", [128, 128], BF16)
    masks = {k: tile("m_" + k, [128, 128], BF16) for k in ("ms_f", "mi_f", "ms_b", "mi_b")}
    hT = tile("hT", [128, 8, T], BF16, nb=NB)
    gcol_attn = tile("gcol_attn", [128, 8], F32)
    gcol_mem = tile("gcol_mem", [128, 8], F32)
    gcol_q = tile("gcol_q", [128, 3], F32)
    gcol_kv = tile("gcol_kv", [128, 2], F32)
    gcol_gdn = tile("gcol_gdn", [128, 1], F32)
    fgain = tile("fgain", [128, D], F32)
    cw = tile("cw", [128, 5, 24], F32)
    alog_bc = tile("alog_bc", [128, 16], F32)
    dtb_bc = tile("dtb_bc", [128, 16], F32)
    negA = tile("negA", [128, 16], F32)
    stage = tile("stage", [128, 128], F32)
    arena = S.sbuf("arena", [128, ARENA_ELEMS], BF16)
    PS = S.psum("PS", [128, 8, 512], F32)
    pb = [S.buf(f"bank{i}") for i in range(8)]
    for _b in pb:
        _b.excl = True
    wpool = [tile(f"wp{i}", [128, 8, 128], BF16) for i in range(5)]
    wctr = [0]

    def bank(i):
        return PS[:, i, :]

    def bank_bf(i):
        return PS[:, i, :].bitcast(BF16)

    def bank2(i):
        return PS[:, i:i + 2, :]

    astate = {"off": 0}

    def areset():
        astate["off"] = 0

    def alloc(name, shape, dtype, nb=1):
        n = int(np.prod(shape[1:]))
        n2 = n * 2 if dtype == F32 else n
        n2 = (n2 + 1) // 2 * 2
        off = astate["off"]
        assert off + n2 <= ARENA_ELEMS, (name, off, n2)
        astate["off"] = off + n2
        v = arena[:shape[0], off:off + n2]
        if dtype == F32:
            v = v.bitcast(F32)
        if len(shape) == 3:
            v = v.rearrange("p (a b) -> p a b", a=shape[1])
        elif len(shape) == 4:
            v = v.rearrange("p (a b c) -> p a b c", a=shape[1], b=shape[2])
        return Tile(v, nb, name, S)

    def load_w(src2d, c0, ncols, nk=8, dst=None):
        if dst is None:
            dst = wpool[wctr[0] % len(wpool)]
            wctr[0] += 1
        src = src2d[:, c0:c0 + ncols].rearrange("(k p) n -> p k n", p=128)
        S.dma(lambda e: e.dma_start(out=dst[:, 0:nk, 0:ncols], in_=src), writes=dst.b, eng="gpsimd")
        return dst

    def mm(out, lhsT, rhs, start, stop, reads, writes):
        S.pe(lambda e: e.matmul(out, lhsT=lhsT, rhs=rhs, start=start, stop=stop), reads=reads, writes=writes)

    def tr(out, in_, reads, writes, f32=False):
        idt = cst["ident"] if f32 else ident_bf
        S.pe(lambda e: e.transpose(out=out, in_=in_, identity=idt[:]), reads=list(reads) + idt.b, writes=writes)

    def proj_fm(wt, tb, bk, nk=8, m=128):
        for k in range(nk):
            mm(bank(bk)[0:m, :], wt[:, k, 0:m], hT[:, k, tb * 512:(tb + 1) * 512], k == 0, k == nk - 1,
               wt.b + [hT.b[tb]], [pb[bk]])

    def rsqrt_inplace(t_ap, bufs, scale, eps):
        S.act(lambda e: e.activation(out=t_ap, in_=t_ap, func=AF.Sqrt, scale=scale, bias=eps_t[:t_ap.shape[0], 0:1]), reads=bufs + eps_t.b, writes=bufs)
        S.dve(lambda e: e.reciprocal(out=t_ap, in_=t_ap), reads=bufs, writes=bufs)

    eps_t = tile("eps_t", [128, 1], F32)
    S.pool(lambda e: e.memset(eps_t[:], EPS), writes=eps_t.b)
    lnsc_t = tile("lnsc_t", [128, 1], F32)
    S.pool(lambda e: e.memset(lnsc_t[:], math.log(SC_GDN)), writes=lnsc_t.b)
    zero_t = tile("zero_t", [128, 1], F32)
    S.pool(lambda e: e.memset(zero_t[:], 0.0), writes=zero_t.b)

    def ld(dst, src, eng="sync"):
        S.dma(lambda e: e.dma_start(out=dst[:], in_=src), writes=dst.b, eng=eng)

    for k in ("ident", "ones", "lfwd", "lbwd", "cblk", "c0", "c1"):
        ld(cst[k], dr[k])
    S.dve(lambda e: e.tensor_copy(out=ident_bf[:], in_=cst["ident"][:]), reads=cst["ident"].b, writes=ident_bf.b)
    S.dve(lambda e: e.tensor_copy(out=ones_bf[:], in_=cst["ones"][:]), reads=cst["ones"].b, writes=ones_bf.b)
    for k in masks:
        S.dma(lambda e, k=k: e.dma_start(out=stage[:], in_=dr[k]), writes=stage.b)
        S.dve(lambda e, k=k: e.tensor_copy(out=masks[k][:], in_=stage[:]), reads=stage.b, writes=masks[k].b)

    def rep8(tl):
        return tl[:].unsqueeze(1).to_broadcast([128, 8, 128])
    S.dma(lambda e: e.dma_start(out=gcol_attn[:], in_=dr["attn_norm_gain"][0].rearrange("(k p) -> p k", p=128), allow_slow_non_contiguous=True), writes=gcol_attn.b)
    S.dma(lambda e: e.dma_start(out=gcol_mem[:], in_=dr["mem_norm_gain"][0].rearrange("(k p) -> p k", p=128), allow_slow_non_contiguous=True), writes=gcol_mem.b)
    S.dma(lambda e: e.dma_start(out=gcol_q[:], in_=dr["q_norm_gain"][0].rearrange("(k p) -> p k", p=128), allow_slow_non_contiguous=True), writes=gcol_q.b)
    S.dma(lambda e: e.dma_start(out=gcol_kv[:], in_=dr["kv_norm_gain"][0].rearrange("(k p) -> p k", p=128), allow_slow_non_contiguous=True), writes=gcol_kv.b)
    S.dma(lambda e: e.dma_start(out=gcol_gdn[:], in_=dr["gdn_norm_gain"][0].rearrange("(k p) -> p k", p=128), allow_slow_non_contiguous=True), writes=gcol_gdn.b)
    S.dma(lambda e: e.dma_start(out=fgain[:], in_=dr["final_norm_gain"].partition_broadcast(128)), writes=fgain.b, eng="gpsimd")
    for j in range(5):
        S.dma(lambda e, j=j: e.dma_start(out=cw[:, j, :], in_=dr["conv_w"][0, j].rearrange("(c p) -> p c", p=128), allow_slow_non_contiguous=True), writes=cw.b)
    S.dma(lambda e: e.dma_start(out=alog_bc[:], in_=dr["a_log"][0].rearrange("a b -> (a b)").partition_broadcast(128)), writes=alog_bc.b, eng="gpsimd")
    S.dma(lambda e: e.dma_start(out=dtb_bc[:], in_=dr["dt_bias"][0].rearrange("a b -> (a b)").partition_broadcast(128)), writes=dtb_bc.b, eng="gpsimd")
    S.act(lambda e: e.activation(out=negA[:], in_=alog_bc[:], func=AF.Exp), reads=alog_bc.b, writes=negA.b)
    S.dve(lambda e: e.tensor_scalar(out=negA[:], in0=negA[:], scalar1=-1.0, scalar2=None, op0=ALU.mult), reads=negA.b, writes=negA.b)

    def phase_h(s):
        areset()
        xt = [alloc(f"xt{i}", [128, D], F32) for i in range(3)]
        junk = alloc("junk", [128, D], BF16)
        hb = [alloc(f"hb{i}", [128, D], BF16) for i in range(2)]
        st = [alloc(f"st{i}", [128, 2], F32) for i in range(2)]
        def stage1(t):
            X, H, ST = xt[t % 3], hb[t % 2], st[t % 2]
            S.dma(lambda e, X=X, t=t: e.dma_start(out=X[:], in_=x[s, t * 128:(t + 1) * 128, :]), writes=X.b)
            S.act(lambda e, X=X, ST=ST: e.activation(out=junk[:], in_=X[:], func=AF.Square, accum_out=ST[:, 0:1]),
                  reads=X.b, writes=junk.b + ST.b, tag="multi")
            rsqrt_inplace(ST[:, 0:1], ST.b, 1.0 / D, EPS)
            S.dve(lambda e, X=X, H=H, ST=ST: e.tensor_scalar(out=H[:], in0=X[:], scalar1=ST[:, 0:1], scalar2=None, op0=ALU.mult),
                  reads=X.b + ST.b, writes=H.b)

        def stage2(t):
            H = hb[t % 2]
            bk = 6 + (t % 2)
            for k in range(8):
                tr(bank_bf(bk)[:, k * 128:(k + 1) * 128], H[:, k * 128:(k + 1) * 128], H.b, [pb[bk]])
            S.dve(lambda e, bk=bk, t=t: e.tensor_tensor(out=hT[:, :, t * 128:(t + 1) * 128],
                                                       in0=bank_bf(bk).rearrange("p (k n) -> p k n", k=8),
                                                       in1=gcol_attn[:].unsqueeze(2).to_broadcast([128, 8, 128]), op=ALU.mult),
                  reads=[pb[bk]] + gcol_attn.b, writes=[hT.b[t // 4]])

        stage1(0)
        for t in range(NT):
            if t + 1 < NT:
                stage1(t + 1)
            stage2(t)

    def finalize_branch(oacc, zs, dst_chunk, gain_col=None, norm=False):
        ob = alloc("ob", [128, T], BF16)
        if norm:
            sq = alloc("fsq", [128, 512], BF16)
            rr = alloc("frr", [128, 512], F32)
            for tb in range(NB):
                sl = slice(tb * 512, (tb + 1) * 512)
                S.act(lambda e, sl=sl: e.activation(out=sq[:], in_=oacc[:, sl], func=AF.Square), reads=oacc.b, writes=sq.b)
                mm(bank(6), ones_bf[:], sq[:], True, True, ones_bf.b + sq.b, [pb[6]])
                S.act(lambda e: e.activation(out=rr[:], in_=bank(6), func=AF.Sqrt, scale=1.0 / 128, bias=eps_t[:, 0:1]), reads=[pb[6]] + eps_t.b, writes=rr.b)
                S.dve(lambda e: e.reciprocal(out=rr[:], in_=rr[:]), reads=rr.b, writes=rr.b)
                S.dve(lambda e, sl=sl: e.tensor_tensor(out=rr[:], in0=oacc[:, sl], in1=rr[:], op=ALU.mult), reads=oacc.b + rr.b, writes=rr.b)
                S.dve(lambda e, sl=sl: e.scalar_tensor_tensor(out=ob[:, sl], in0=rr[:], scalar=gain_col[:, 0:1], in1=zs[:, sl], op0=ALU.mult, op1=ALU.mult),
                      reads=rr.b + gain_col.b + zs.b, writes=ob.b)
        else:
            S.dve(lambda e: e.tensor_tensor(out=ob[:], in0=oacc[:], in1=zs[:], op=ALU.mult), reads=oacc.b + zs.b, writes=ob.b)
        S.dma(lambda e: e.dma_start(out=scr[dst_chunk], in_=ob[:]), reads=ob.b, writes=[scrb[dst_chunk]])

    scrb = [S.buf(f"scr{i}") for i in range(20)]

    def run(gen):
        for _ in gen:
            pass

    def zipper(main, side, every=1, nside=1):
        i = 0
        for _ in main:
            i += 1
            if side is not None and i % every == 0:
                for _r in range(nside):
                    try:
                        next(side)
                    except StopIteration:
                        side = None
                        break
        if side is not None:
            run(side)

    def silu_proj_gen(col0, zs, banks=(6, 7)):
        wt = load_w(w_in, col0, 128)
        for tb in range(NB):
            bk = banks[tb % 2]
            proj_fm(wt, tb, bk)
            S.act(lambda e, bk=bk, tb=tb: e.activation(out=zs[:, tb * 512:(tb + 1) * 512], in_=bank(bk), func=AF.Silu),
                  reads=[pb[bk]], writes=zs.b)
            yield

    def silu_proj(col0, name):
        zs = alloc(name, [128, T], BF16)
        run(silu_proj_gen(col0, zs))
        return zs

    def phase_gdn(s):
        import os as _os
        GZ = int(_os.environ.get('GZ', '1'))
        GE = int(_os.environ.get('GE', '1'))
        SCAN_DVE_SB = int(_os.environ.get('SCAN_DVE_SB', '1'))
        G1B = [int(c) for c in _os.environ.get('G1B', '676767')]
        areset()
        wab = load_w(w_in, O_AB, 32)
        ab = alloc("ab", [128, NT, 32], F32)
        for t in range(NT):
            for k in range(8):
                mm(bank(6)[:, t * 32:(t + 1) * 32], hT[:, k, t * 128:(t + 1) * 128], wab[:, k, 0:32], k == 0, k == 7,
                   [hT.b[t // 4]] + wab.b, [pb[6]])
        S.dve(lambda e: e.tensor_copy(out=ab[:], in_=bank(6).rearrange("p (t n) -> p t n", t=NT)), reads=[pb[6]], writes=ab.b)
        tabs = {n: alloc("tb_" + n, [128, NT, 16], F32) for n in ("xg", "t1", "g", "beta", "gc", "gtot", "cbg", "cd", "nbeta")}
        egrep = alloc("egrep", [128, NT, 2, 16], F32)
        xg, t1, g, beta, gc, gtot, cbg, cd, nbeta = (tabs[n] for n in ("xg", "t1", "g", "beta", "gc", "gtot", "cbg", "cd", "nbeta"))
        bc16 = lambda tl: tl[:].unsqueeze(1).to_broadcast([128, NT, 16])
        S.dve(lambda e: e.tensor_tensor(out=xg[:], in0=ab[:, :, 0:16], in1=bc16(dtb_bc), op=ALU.add), reads=ab.b + dtb_bc.b, writes=xg.b)
        S.act(lambda e: e.activation(out=t1[:], in_=xg[:], func=AF.Abs), reads=xg.b, writes=t1.b)
        S.act(lambda e: e.activation(out=t1[:], in_=t1[:], func=AF.Exp, scale=-1.0), reads=t1.b, writes=t1.b)
        S.act(lambda e: e.activation(out=t1[:], in_=t1[:], func=AF.Ln, bias=1.0), reads=t1.b, writes=t1.b)
        S.dve(lambda e: e.scalar_tensor_tensor(out=t1[:], in0=xg[:], scalar=0.0, in1=t1[:], op0=ALU.max, op1=ALU.add), reads=xg.b + t1.b, writes=t1.b)
        S.dve(lambda e: e.tensor_tensor(out=g[:], in0=t1[:], in1=bc16(negA), op=ALU.mult), reads=t1.b + negA.b, writes=g.b)
        S.act(lambda e: e.activation(out=beta[:], in_=ab[:, :, 16:32], func=AF.Sigmoid), reads=ab.b, writes=beta.b)
        S.dve(lambda e: e.tensor_scalar(out=nbeta[:], in0=beta[:], scalar1=-1.0, scalar2=None, op0=ALU.mult), reads=beta.b, writes=nbeta.b)
        for t in range(NT):
            mm(bank(6)[:, t * 16:t * 16 + 8], cst["lfwd"][:], g[:, t, 0:8], True, True, cst["lfwd"].b + g.b, [pb[6]])
            mm(bank(6)[:, t * 16 + 8:t * 16 + 16], cst["lbwd"][:], g[:, t, 8:16], True, True, cst["lbwd"].b + g.b, [pb[6]])
            mm(bank(7)[:, t * 16:(t + 1) * 16], cst["cblk"][:], g[:, t, :], True, True, cst["cblk"].b + g.b, [pb[7]])
            mm(bank(4)[:, t * 32:t * 32 + 16], cst["c0"][:], g[:, t, :], True, True, cst["c0"].b + g.b, [pb[4]])
            mm(bank(4)[:, t * 32 + 16:t * 32 + 32], cst["c1"][:], g[:, t, :], True, True, cst["c1"].b + g.b, [pb[4]])
        S.dve(lambda e: e.tensor_copy(out=gc[:], in_=bank(6)[:, 0:256].rearrange("p (t n) -> p t n", t=NT)), reads=[pb[6]], writes=gc.b)
        S.dve(lambda e: e.tensor_copy(out=gtot[:], in_=bank(7)[:, 0:256].rearrange("p (t n) -> p t n", t=NT)), reads=[pb[7]], writes=gtot.b)
        S.act(lambda e: e.activation(out=egrep[:].rearrange("p t j n -> p (t j n)"), in_=bank(4), func=AF.Exp), reads=[pb[4]], writes=egrep.b)
        S.act(lambda e: e.activation(out=cbg[:], in_=gc[:], func=AF.Exp), reads=gc.b, writes=cbg.b)
        S.dve(lambda e: e.tensor_tensor(out=cbg[:], in0=cbg[:], in1=beta[:], op=ALU.mult), reads=cbg.b + beta.b, writes=cbg.b)
        S.dve(lambda e: e.tensor_tensor(out=cd[:], in0=gtot[:], in1=gc[:], op=ALU.subtract), reads=gtot.b + gc.b, writes=cd.b)
        S.act(lambda e: e.activation(out=cd[:], in_=cd[:], func=AF.Exp), reads=cd.b, writes=cd.b)
        base_off = astate["off"]
        DIRS = []
        for d in range(2):
            DIRS.append(dict(Kd=alloc(f"Kd{d}", [128, NT, 128], BF16), QgT=alloc(f"QgT{d}", [128, T], BF16),
                             intraT=alloc(f"intraT{d}", [128, NT, 128], BF16), U=alloc(f"U{d}", [128, NT, 128], F32),
                             WT=alloc(f"WT{d}", [128, T], BF16), Kbg=alloc(f"Kbg{d}", [128, NT, 128], BF16),
                             Vb=alloc(f"Vb{d}", [128, NT, 128], BF16)))
        qT = alloc("qT", [128, T], BF16)
        kT = alloc("kT", [128, T], BF16)
        vT = alloc("vT", [128, T], BF16)
        off_R = astate["off"]
        G1T = [dict(xpad=alloc(f"xpad{i}", [128, T + 4], BF16), diag=alloc(f"diag{i}", [128, 5, 128], BF16),
                    ysil=alloc(f"ysil{i}", [128, 512], F32), sq=alloc(f"sq{i}", [128, 512], BF16), rr=alloc(f"rr{i}", [128, 512], F32))
               for i in range(2)]
        G1T.append(dict(xpad=alloc("xpad2", [128, T + 4], BF16), diag=alloc("diag2", [128, 5, 128], BF16), ysil=None, sq=None, rr=None))
        Ktok = alloc("Ktok", [128, NT, 128], BF16)
        Vtok = alloc("Vtok", [128, NT, 128], BF16)
        end1 = astate["off"]
        astate["off"] = off_R
        PT = [dict(diagG=alloc(f"diagG{i}", [128, 8, 128], F32), ET=alloc(f"ET{i}", [128, 8, 128], BF16),
                   EMs=alloc(f"EMs{i}", [128, 8, 128], BF16), EMi=alloc(f"EMi{i}", [128, 8, 128], BF16),
                   expR=alloc(f"expR{i}", [128, 8, 128], BF16), X=alloc(f"X{i}", [128, 8, 128], BF16),
                   N=alloc(f"N{i}", [128, 8, 128], BF16), P=alloc(f"P{i}", [128, 8, 128], BF16)) for i in range(2)]
        astate["off"] = max(end1, astate["off"])
        oacc = alloc("oacc", [128, T], F32, nb=NB)
        Sf = [alloc(f"Sf{d}", [128, 128], F32) for d in range(2)]
        Sb = [alloc(f"Sb{d}", [128, 128], BF16) for d in range(2)]
        vn = [[alloc(f"vn{d}{i}", [128, 128], BF16) for i in range(2)] for d in range(2)]
        zs_g = alloc("zs_g", [128, T], BF16)
        ob_g = alloc("ob_g", [128, T], BF16)
        fsq = alloc("fsq", [128, 512], BF16)
        frr = alloc("frr", [128, 512], F32)

        def rsqrt_act(out_ap, in_ap, scale, rd, wr):
            S.act(lambda e: e.activation(out=out_ap, in_=in_ap, func=AF.Ln, scale=scale, bias=eps_t[:, 0:1]), reads=rd + eps_t.b, writes=wr)
            S.act(lambda e: e.activation(out=out_ap, in_=out_ap, func=AF.Exp, scale=-0.5), reads=wr, writes=wr)

        def g1_chunk(which, h, dstT, TT_, bA, bB):
            xpad, diag, ysil, sq, rr = TT_["xpad"], TT_["diag"], TT_["ysil"], TT_["sq"], TT_["rr"]
            chunk = which * 8 + h
            wt = load_w(w_in, O_QKV + chunk * 128, 128)
            S.pool(lambda e: e.memset(xpad[:, 0:2], 0.0), writes=xpad.b)
            S.pool(lambda e: e.memset(xpad[:, T + 2:T + 4], 0.0), writes=xpad.b)
            for j in range(5):
                S.pool(lambda e, j=j: e.tensor_scalar(out=diag[:, j, :], in0=cst["ident"][:], scalar1=cw[:, j, chunk:chunk + 1], scalar2=None, op0=ALU.mult),
                       reads=cst["ident"].b + cw.b, writes=diag.b)
            yield
            for tb in range(NB):
                proj_fm(wt, tb, bA)
                S.act(lambda e, tb=tb: e.activation(out=xpad[:, 2 + tb * 512:2 + (tb + 1) * 512], in_=bank(bA), func=AF.Copy),
                      reads=[pb[bA]], writes=xpad.b)
                yield

            def stage_c(tb):
                sl = slice(tb * 512, (tb + 1) * 512)
                mm(bank(bA), ones_bf[:], sq[:], True, True, ones_bf.b + sq.b, [pb[bA]])
                S.act(lambda e: e.activation(out=rr[:], in_=bank(bA), func=AF.Ln, scale=1.0, bias=eps_t[:, 0:1]), reads=[pb[bA]] + eps_t.b, writes=rr.b)
                lb = lnsc_t if which == 0 else zero_t
                S.act(lambda e, lb=lb: e.activation(out=rr[:], in_=rr[:], func=AF.Exp, scale=-0.5, bias=lb[:, 0:1]), reads=rr.b + lb.b, writes=rr.b)
                S.pool(lambda e, sl=sl: e.tensor_tensor(out=dstT[:, sl], in0=ysil[:], in1=rr[:], op=ALU.mult), reads=ysil.b + rr.b, writes=dstT.b)

            for tb in range(NB):
                sl = slice(tb * 512, (tb + 1) * 512)
                for j in range(5):
                    mm(bank(bB), diag[:, j, :], xpad[:, tb * 512 + j:tb * 512 + j + 512], j == 0, j == 4, diag.b + xpad.b, [pb[bB]])
                if which == 2:
                    S.act(lambda e, sl=sl: e.activation(out=dstT[:, sl], in_=bank(bB), func=AF.Silu), reads=[pb[bB]], writes=dstT.b)
                    yield
                    continue
                S.act(lambda e: e.activation(out=ysil[:], in_=bank(bB), func=AF.Silu), reads=[pb[bB]], writes=ysil.b)
                S.pool(lambda e: e.tensor_tensor(out=sq[:], in0=ysil[:], in1=ysil[:], op=ALU.mult), reads=ysil.b, writes=sq.b)
                yield
                stage_c(tb)
                yield

        def rr_zip(gens):
            gens = list(gens)
            while gens:
                for g_ in list(gens):
                    try:
                        next(g_)
                    except StopIteration:
                        gens.remove(g_)
                    yield

        def g1_gen(h):
            return rr_zip([g1_chunk(0, h, qT, G1T[0], G1B[0], G1B[1]), g1_chunk(1, h, kT, G1T[1], G1B[2], G1B[3]), g1_chunk(2, h, vT, G1T[2], G1B[4], G1B[5])])

        def g23(h):
            for srcT, dst in ((kT, Ktok), (vT, Vtok)):
                for half in range(2):
                    bk = 6 + half
                    for j in range(8):
                        t = half * 8 + j
                        tr(bank_bf(bk)[:, j * 128:(j + 1) * 128], srcT[:, t * 128:(t + 1) * 128], srcT.b, [pb[bk]])
                    S.act(lambda e, bk=bk, dst=dst, half=half: e.activation(out=dst[:, half * 8:(half + 1) * 8, :], in_=bank_bf(bk).rearrange("p (t n) -> p t n", t=8), func=AF.Copy),
                          reads=[pb[bk]], writes=dst.b)
            for d in range(2):
                DD = DIRS[d]
                hd = d * 8 + h
                bcc = lambda tl, hd=hd: tl[:, :, hd:hd + 1].to_broadcast([128, NT, 128])
                S.pool(lambda e, DD=DD, bcc=bcc: e.tensor_tensor(out=DD["Kbg"][:], in0=Ktok[:], in1=bcc(cbg), op=ALU.mult), reads=Ktok.b + cbg.b, writes=DD["Kbg"].b)
                S.dve(lambda e, DD=DD, bcc=bcc: e.tensor_tensor(out=DD["Kd"][:], in0=Ktok[:], in1=bcc(cd), op=ALU.mult), reads=Ktok.b + cd.b, writes=DD["Kd"].b)
                S.dve(lambda e, DD=DD, bcc=bcc: e.tensor_tensor(out=DD["Vb"][:], in0=Vtok[:], in1=bcc(beta), op=ALU.mult), reads=Vtok.b + beta.b, writes=DD["Vb"].b)

        def prep_inst(h, d, half, TP, b0):
            DD = DIRS[d]
            hd = d * 8 + h
            MS = masks["ms_f" if d == 0 else "ms_b"]
            MI = masks["mi_f" if d == 0 else "mi_b"]
            diagG, ET, EMs, EMi, expR, X, N, P = (TP[k_] for k_ in ("diagG", "ET", "EMs", "EMi", "expR", "X", "N", "P"))
            Dm, Y = diagG, ET
            t0 = half * 8
            tsl = slice(t0, t0 + 8)
            csl = slice(t0 * 128, (t0 + 8) * 128)
            B0, B1, B2, B3 = b0, b0 + 1, b0 + 2, b0 + 3
            v3 = lambda bk: bank2(bk).rearrange("p a (b c) -> p (a b) c", c=128)
            blk = lambda bk, j: bank(bk + j // 4)[:, (j % 4) * 128:(j % 4 + 1) * 128]
            gcb = gc[:, tsl, hd:hd + 1].to_broadcast([128, 8, 128])
            S.pool(lambda e: e.tensor_tensor(out=diagG[:], in0=rep8(cst["ident"]), in1=gcb, op=ALU.mult), reads=cst["ident"].b + gc.b, writes=diagG.b)
            for q in range(2):
                mm(bank(B0 + q), cst["ones"][:], diagG[:, q * 4:(q + 1) * 4, :].rearrange("p a b -> p (a b)"), True, True, cst["ones"].b + diagG.b, [pb[B0 + q]])
            yield
            S.act(lambda e: e.activation(out=expR[:], in_=v3(B0), func=AF.Exp), reads=[pb[B0], pb[B1]], writes=expR.b)
            S.dve(lambda e: e.tensor_tensor(out=Dm[:], in0=v3(B0), in1=gcb, op=ALU.subtract), reads=[pb[B0], pb[B1]] + gc.b, writes=Dm.b)
            S.dve(lambda e: e.tensor_scalar(out=Dm[:], in0=Dm[:], scalar1=0.0, scalar2=None, op0=ALU.min), reads=Dm.b, writes=Dm.b)
            yield
            S.act(lambda e: e.activation(out=ET[:], in_=Dm[:], func=AF.Exp), reads=Dm.b, writes=ET.b)
            S.pool(lambda e: e.tensor_tensor(out=EMs[:], in0=ET[:], in1=rep8(MS), op=ALU.mult), reads=ET.b + MS.b, writes=EMs.b)
            S.pool(lambda e: e.tensor_tensor(out=EMi[:], in0=ET[:], in1=rep8(MI), op=ALU.mult), reads=ET.b + MI.b, writes=EMi.b)
            S.pool(lambda e: e.tensor_tensor(out=DD["QgT"][:, csl], in0=qT[:, csl], in1=expR[:].rearrange("p a b -> p (a b)"), op=ALU.mult),
                   reads=qT.b + expR.b, writes=DD["QgT"].b)
            for j in range(8):
                ksl = slice((t0 + j) * 128, (t0 + j + 1) * 128)
                mm(blk(B2, j), kT[:, ksl], kT[:, ksl], True, True, kT.b, [pb[B2 + j // 4]])
            for j in range(8):
                ksl = slice((t0 + j) * 128, (t0 + j + 1) * 128)
                mm(blk(B0, j), kT[:, ksl], qT[:, ksl], True, True, kT.b + qT.b, [pb[B0 + j // 4]])
            yield
            S.dve(lambda e: e.tensor_tensor(out=Y[:], in0=v3(B2), in1=EMs[:], op=ALU.mult), reads=[pb[B2], pb[B3]] + EMs.b, writes=Y.b)
            S.dve(lambda e: e.tensor_tensor(out=DD["intraT"][:, tsl, :], in0=v3(B0), in1=EMi[:], op=ALU.mult),
                  reads=[pb[B0], pb[B1]] + EMi.b, writes=DD["intraT"].b)
            for j in range(8):
                tr(bank_bf(B2)[:, j * 128:(j + 1) * 128], Y[:, j, :], Y.b, [pb[B2]])
            yield
            nbb = nbeta[:, tsl, hd:hd + 1].to_broadcast([128, 8, 128])
            S.dve(lambda e: e.tensor_tensor(out=N[:], in0=bank_bf(B2).rearrange("p (t n) -> p t n", t=8), in1=nbb, op=ALU.mult),
                  reads=[pb[B2]] + nbeta.b, writes=N.b)
            for j in range(8):
                tr(bank_bf(B3)[:, j * 128:(j + 1) * 128], N[:, j, :], N.b, [pb[B3]])
            yield
            S.act(lambda e: e.activation(out=X[:], in_=bank_bf(B3).rearrange("p (t n) -> p t n", t=8), func=AF.Copy), reads=[pb[B3]], writes=X.b)
            S.dve(lambda e: e.tensor_tensor(out=P[:], in0=bank_bf(B3).rearrange("p (t n) -> p t n", t=8), in1=rep8(ident_bf), op=ALU.add),
                  reads=[pb[B3]] + ident_bf.b, writes=P.b)
            yield
            for n in range(1, 6):
                for j in range(8):
                    mm(blk(B0, j), X[:, j, :], N[:, j, :], True, True, X.b + N.b, [pb[B0 + j // 4]])
                if n <= 4:
                    for j in range(8):
                        mm(blk(B2, j), N[:, j, :], X[:, j, :], True, True, X.b + N.b, [pb[B2 + j // 4]])
                yield
                S.act(lambda e: e.activation(out=N[:], in_=v3(B0), func=AF.Copy), reads=[pb[B0], pb[B1]], writes=N.b)
                if n <= 4:
                    S.dve(lambda e: e.tensor_copy(out=X[:], in_=v3(B2)), reads=[pb[B2], pb[B3]], writes=X.b)
                for j in range(8):
                    mm(blk(B0, j), N[:, j, :], P[:, j, :], True, True, N.b + P.b, [pb[B0 + j // 4]])
                yield
                S.dve(lambda e: e.tensor_tensor(out=P[:], in0=v3(B0), in1=P[:], op=ALU.add), reads=[pb[B0], pb[B1]] + P.b, writes=P.b)
            for j in range(8):
                t = t0 + j
                mm(blk(B2, j), P[:, j, :], DD["Vb"][:, t, :], True, True, P.b + DD["Vb"].b, [pb[B2 + j // 4]])
            for j in range(8):
                t = t0 + j
                mm(blk(B0, j), DD["Kbg"][:, t, :], P[:, j, :], True, True, P.b + DD["Kbg"].b, [pb[B0 + j // 4]])
            yield
            S.act(lambda e: e.activation(out=DD["U"][:, tsl, :], in_=v3(B2), func=AF.Copy), reads=[pb[B2], pb[B3]], writes=DD["U"].b)
            S.dve(lambda e: e.tensor_copy(out=DD["WT"][:, csl], in_=bank2(B0).rearrange("p a b -> p (a b)")), reads=[pb[B0], pb[B1]], writes=DD["WT"].b)
            yield

        def scan_gen(h):
            for d in range(2):
                S.pool(lambda e, d=d: e.memset(Sf[d][:], 0.0), writes=Sf[d].b)
                S.pool(lambda e, d=d: e.memset(Sb[d][:], 0.0), writes=Sb[d].b)
            seen = [0] * NB
            for step in range(32):
                info = []
                for d in range(2):
                    ci = step if d == 0 else 31 - step
                    t, jc = ci // 2, ci % 2
                    info.append(dict(d=d, DD=DIRS[d], hd=d * 8 + h, ci=ci, t=t, jc=jc, p0=jc * 64, blk=ci // 8, obk=d, wsb=2 + d, sub=4 + d,
                                     V=vn[d][step % 2], csl=slice(ci * 64, (ci + 1) * 64), oc=slice((ci % 8) * 64, (ci % 8 + 1) * 64)))
                for I in info:
                    d, DD, t = I["d"], I["DD"], I["t"]
                    mm(bank(I["wsb"])[:, 0:128], DD["WT"][:, t * 128:(t + 1) * 128], Sb[d][:], True, True, DD["WT"].b + Sb[d].b, [pb[I["wsb"]]])
                for I in info:
                    d, DD = I["d"], I["DD"]
                    mm(bank(I["obk"])[:, I["oc"]], Sb[d][:], DD["QgT"][:, I["csl"]], True, False, Sb[d].b + DD["QgT"].b, [pb[I["obk"]]])
                for I in info:
                    DD, V, t, p0, wsb = I["DD"], I["V"], I["t"], I["p0"], I["wsb"]
                    S.dve(lambda e, DD=DD, V=V, t=t, p0=p0, wsb=wsb: e.tensor_tensor(out=V[p0:p0 + 64, :], in0=DD["U"][p0:p0 + 64, t, :], in1=bank(wsb)[p0:p0 + 64, 0:128], op=ALU.subtract),
                          reads=DD["U"].b + [pb[wsb]], writes=V.b)
                for I in info:
                    DD, V, t, p0, sub = I["DD"], I["V"], I["t"], I["p0"], I["sub"]
                    mm(bank(sub)[:, 0:128], DD["Kd"][p0:p0 + 64, t, :], V[p0:p0 + 64, :], True, True, DD["Kd"].b + V.b, [pb[sub]])
                for I in info:
                    d, DD, V, t, p0 = I["d"], I["DD"], I["V"], I["t"], I["p0"]
                    mm(bank(I["obk"])[:, I["oc"]], V[p0:p0 + 64, :], DD["intraT"][p0:p0 + 64, t, p0:p0 + 64], False, True, V.b + DD["intraT"].b, [pb[I["obk"]]])
                if SCAN_DVE_SB:
                    for I in info:
                        d, sub = I["d"], I["sub"]
                        eg = egrep[:, I["t"], I["jc"], I["hd"]:I["hd"] + 1]
                        S.dve(lambda e, d=d, eg=eg, sub=sub: e.scalar_tensor_tensor(out=Sb[d][:], in0=Sf[d][:], scalar=eg, in1=bank(sub)[:, 0:128], op0=ALU.mult, op1=ALU.add),
                              reads=Sf[d].b + egrep.b + [pb[sub]], writes=Sb[d].b)
                for I in info:
                    d, sub = I["d"], I["sub"]
                    eg = egrep[:, I["t"], I["jc"], I["hd"]:I["hd"] + 1]
                    S.dve(lambda e, d=d, eg=eg, sub=sub: e.scalar_tensor_tensor(out=Sf[d][:], in0=Sf[d][:], scalar=eg, in1=bank(sub)[:, 0:128], op0=ALU.mult, op1=ALU.add),
                          reads=Sf[d].b + egrep.b + [pb[sub]], writes=Sf[d].b)
                if not SCAN_DVE_SB:
                    for I in info:
                        d = I["d"]
                        S.act(lambda e, d=d: e.activation(out=Sb[d][:], in_=Sf[d][:], func=AF.Copy), reads=Sf[d].b, writes=Sb[d].b)
                for I in info:
                    d, ci, blk_, obk = I["d"], I["ci"], I["blk"], I["obk"]
                    last = (ci % 8 == 7) if d == 0 else (ci % 8 == 0)
                    if last:
                        bsl = slice(blk_ * 512, (blk_ + 1) * 512)
                        if seen[blk_] == 0:
                            S.act(lambda e, bsl=bsl, obk=obk: e.activation(out=oacc[:, bsl], in_=bank(obk), func=AF.Copy), reads=[pb[obk]], writes=[oacc.b[blk_]])
                        else:
                            S.dve(lambda e, bsl=bsl, obk=obk: e.tensor_tensor(out=oacc[:, bsl], in0=bank(obk), in1=oacc[:, bsl], op=ALU.add), reads=[pb[obk], oacc.b[blk_]], writes=[oacc.b[blk_]])
                        seen[blk_] += 1
                yield
            yield from silu_proj_gen(O_ZG + h * 128, zs_g, banks=(6, 6))
            for tb in range(NB):
                sl = slice(tb * 512, (tb + 1) * 512)
                S.act(lambda e, sl=sl: e.activation(out=fsq[:], in_=oacc[:, sl], func=AF.Square), reads=[oacc.b[tb]], writes=fsq.b)
                mm(bank(7), ones_bf[:], fsq[:], True, True, ones_bf.b + fsq.b, [pb[7]])
                rsqrt_act(frr[:], bank(7), 1.0 / 128, [pb[7]], frr.b)
                S.dve(lambda e, sl=sl: e.tensor_tensor(out=frr[:], in0=oacc[:, sl], in1=frr[:], op=ALU.mult), reads=[oacc.b[tb]] + frr.b, writes=frr.b)
                S.dve(lambda e, sl=sl: e.scalar_tensor_tensor(out=ob_g[:, sl], in0=frr[:], scalar=gcol_gdn[:, 0:1], in1=zs_g[:, sl], op0=ALU.mult, op1=ALU.mult),
                      reads=frr.b + gcol_gdn.b + zs_g.b, writes=ob_g.b)
                yield
            S.dma(lambda e: e.dma_start(out=scr[h], in_=ob_g[:]), reads=ob_g.b, writes=[scrb[h]])

        run(g1_gen(0))
        for h in range(8):
            g23(h)
            S.barrier()
            for half in range(2):
                run(rr_zip([prep_inst(h, 0, half, PT[0], 0), prep_inst(h, 1, half, PT[1], 4)]))
            S.barrier()
            zipper(scan_gen(h), g1_gen(h + 1) if h < 7 else None, every=GE, nside=GZ)

    def attn_gen(kT_fn, q_fn, v_fn, nkt, scale, zs, dst_chunk, AT, extra=None):
        pTs, rec, tmpo, ob = AT["pTs"], AT["rec"], AT["tmpo"], AT["ob"][AT["ctr"] % 2]
        AT["ctr"] += 1
        its = [(qb, kt) for qb in range(NB) for kt in range(nkt)]
        SBK = (0, 1, 2)

        def emit_s(i):
            qb, kt = its[i]
            sbk = SBK[i % len(SBK)]
            pT = pTs[i % len(pTs)]
            ka, kb_ = kT_fn(kt)
            qa, qb_ = q_fn(qb)
            mm(bank(sbk), ka, qa, True, extra is None, kb_ + qb_, [pb[sbk]])
            if extra is not None:
                k2, k2b = extra[0](kt)
                q2, q2b = extra[1](qb)
                mm(bank(sbk), k2, q2, False, True, k2b + q2b, [pb[sbk]])
            S.act(lambda e, pT=pT, sbk=sbk: e.activation(out=pT[:], in_=bank(sbk), func=AF.Exp, scale=scale), reads=[pb[sbk]], writes=pT.b)

        def emit_pv(i):
            qb, kt = its[i]
            ob_, sb_ = 3 + (qb % 2), 5
            pT = pTs[i % len(pTs)]
            va, vb_ = v_fn(kt)
            mm(bank(ob_), va, pT[:], kt == 0, kt == nkt - 1, vb_ + pT.b, [pb[ob_]])
            if PAIRSUM and nkt % 2 == 0:
                if kt % 2 == 1:
                    pP = pTs[(i - 1) % len(pTs)]
                    p2 = AT["p2"][(i // 2) % 2]
                    S.pool(lambda e, p2=p2, pP=pP, pT=pT: e.tensor_tensor(out=p2[:], in0=pP[:], in1=pT[:], op=ALU.add), reads=pP.b + pT.b, writes=p2.b)
                    mm(bank(sb_), ones_bf[:], p2[:], kt == 1, kt == nkt - 1, ones_bf.b + p2.b, [pb[sb_]])
            else:
                mm(bank(sb_), ones_bf[:], pT[:], kt == 0, kt == nkt - 1, ones_bf.b + pT.b, [pb[sb_]])
            if kt == nkt - 1:
                sl = slice(qb * 512, (qb + 1) * 512)
                S.act(lambda e, sb_=sb_: e.activation(out=rec[:], in_=bank(sb_), func=AF.Ln), reads=[pb[sb_]], writes=rec.b)
                S.act(lambda e: e.activation(out=rec[:], in_=rec[:], func=AF.Exp, scale=-1.0), reads=rec.b, writes=rec.b)
                S.dve(lambda e, ob_=ob_: e.tensor_tensor(out=tmpo[:], in0=bank(ob_), in1=rec[:], op=ALU.mult), reads=[pb[ob_]] + rec.b, writes=tmpo.b)
                S.pool(lambda e, sl=sl: e.tensor_tensor(out=ob[:, sl], in0=tmpo[:], in1=zs[:, sl], op=ALU.mult), reads=tmpo.b + zs.b, writes=ob.b)

        import os as _os
        SK = int(_os.environ.get("SKEW", "3"))
        PAIRSUM = int(_os.environ.get("PAIRSUM", "0"))
        for i0 in range(min(SK, len(its))):
            emit_s(i0)
        for i in range(len(its)):
            if i + SK < len(its):
                emit_s(i + SK)
            emit_pv(i)
            yield
        S.dma(lambda e: e.dma_start(out=scr[dst_chunk], in_=ob[:]), reads=ob.b, writes=[scrb[dst_chunk]])

    def attn_temps():
        return dict(pTs=[alloc(f"pT{i}", [128, 512], BF16) for i in range(5)], rec=alloc("a_rec", [128, 512], F32),
                    tmpo=alloc("a_tmpo", [128, 512], F32), p2=[alloc(f"a_p2{i}", [128, 512], BF16) for i in range(2)], sacc=[alloc(f"a_sacc{i}", [128, 512], F32) for i in range(2)], ob=[alloc(f"a_ob{i}", [128, T], BF16) for i in range(2)], ctr=0)

    def phase_mla(s):
        areset()
        cqg = alloc("cqg", [128, 3, T], BF16)
        kvg = alloc("kvg", [128, 2, T], BF16)
        rq = alloc("rq", [128, T], F32)
        rkv = alloc("rkv", [128, T], F32)
        rkvc = alloc("rkvc", [128, NT], F32)
        kpeT = alloc("kpeT", [64, T], BF16)
        sqs = [alloc(f"msq{i}", [128, 512], BF16) for i in range(3)]
        r1 = alloc("r1", [64, 512], F32)
        r2 = alloc("r2", [64, 512], F32)
        cos2 = alloc("cos2", [64, T], F32)
        sins = alloc("sins", [64, T], F32)
        S.dma(lambda e: e.dma_start(out=cos2[:], in_=dr["cos2"]), writes=cos2.b)
        S.dma(lambda e: e.dma_start(out=sins[:], in_=dr["sins"]), writes=sins.b)
        for (col0, nchunk, dstg, gcolt, rdst, nfeat) in ((O_CQ, 3, cqg, gcol_q, rq, 384), (O_CKV, 2, kvg, gcol_kv, rkv, 256)):
            wts = [load_w(w_in, col0 + c * 128, 128) for c in range(nchunk)]
            for tb in range(NB):
                sl = slice(tb * 512, (tb + 1) * 512)
                for c in range(nchunk):
                    bk = 6 + (c % 2)
                    proj_fm(wts[c], tb, bk)
                    S.act(lambda e, bk=bk, c=c: e.activation(out=sqs[c][:], in_=bank(bk), func=AF.Square), reads=[pb[bk]], writes=sqs[c].b)
                    S.dve(lambda e, bk=bk, c=c, sl=sl, dstg=dstg, gcolt=gcolt: e.tensor_scalar(out=dstg[:, c, sl], in0=bank(bk), scalar1=gcolt[:, c:c + 1], scalar2=None, op0=ALU.mult),
                          reads=[pb[bk]] + gcolt.b + sqs[c].b, writes=dstg.b)
                for c in range(nchunk):
                    mm(bank(5), ones_bf[:], sqs[c][:], c == 0, c == nchunk - 1, ones_bf.b + sqs[c].b, [pb[5]])
                S.act(lambda e, sl=sl, rdst=rdst, nfeat=nfeat: e.activation(out=rdst[:, sl], in_=bank(5), func=AF.Ln, scale=1.0 / nfeat, bias=eps_t[:, 0:1]),
                      reads=[pb[5]] + eps_t.b, writes=rdst.b)
                S.act(lambda e, sl=sl, rdst=rdst: e.activation(out=rdst[:, sl], in_=rdst[:, sl], func=AF.Exp, scale=-0.5), reads=rdst.b, writes=rdst.b)
                import os as _os
                if nchunk == 2 and not _os.environ.get("NO_N1"):
                    for j in range(4):
                        t = tb * 4 + j
                        for c in range(2):
                            mm(bank(4)[:, t:t + 1], sqs[c][:, j * 128:(j + 1) * 128], ones_bf[:, 0:1], c == 0, c == 1, sqs[c].b + ones_bf.b, [pb[4]])
        S.act(lambda e: e.activation(out=rkvc[:], in_=bank(4)[:, 0:NT], func=AF.Sqrt, scale=1.0 / 256, bias=eps_t[:, 0:1]), reads=[pb[4]] + eps_t.b, writes=rkvc.b)
        S.dve(lambda e: e.reciprocal(out=rkvc[:], in_=rkvc[:]), reads=rkvc.b, writes=rkvc.b)

        def rope_pair(wsrc, colA, nk, rhs_fn, dst, rmul):
            wa = load_w(wsrc, colA, 64, nk=nk)
            wsw = wpool[wctr[0] % len(wpool)]
            wctr[0] += 1
            srcA = wsrc[:, colA + 32:colA + 64].rearrange("(k p) n -> p k n", p=128)
            srcB = wsrc[:, colA:colA + 32].rearrange("(k p) n -> p k n", p=128)
            S.dma(lambda e: e.dma_start(out=wsw[:, 0:nk, 0:32], in_=srcA), writes=wsw.b, eng="gpsimd")
            S.dma(lambda e: e.dma_start(out=wsw[:, 0:nk, 32:64], in_=srcB), writes=wsw.b, eng="gpsimd")
            for tb in range(NB):
                sl = slice(tb * 512, (tb + 1) * 512)
                for k in range(nk):
                    ra, rb = rhs_fn(k, sl)
                    mm(bank(6)[0:64, :], wa[:, k, 0:64], ra, k == 0, k == nk - 1, wa.b + rb, [pb[6]])
                for k in range(nk):
                    ra, rb = rhs_fn(k, sl)
                    mm(bank(7)[0:64, :], wsw[:, k, 0:64], ra, k == 0, k == nk - 1, wsw.b + rb, [pb[7]])
                S.dve(lambda e, sl=sl: e.tensor_tensor(out=r1[:], in0=bank(6)[0:64, :], in1=cos2[:, sl], op=ALU.mult), reads=[pb[6]] + cos2.b, writes=r1.b)
                S.dve(lambda e, sl=sl: e.tensor_tensor(out=r2[:], in0=bank(7)[0:64, :], in1=sins[:, sl], op=ALU.mult), reads=[pb[7]] + sins.b, writes=r2.b)
                if rmul is None:
                    S.dve(lambda e, sl=sl: e.tensor_tensor(out=dst[0:64, sl], in0=r1[:], in1=r2[:], op=ALU.add), reads=r1.b + r2.b, writes=dst.b)
                else:
                    S.dve(lambda e: e.tensor_tensor(out=r1[:], in0=r1[:], in1=r2[:], op=ALU.add), reads=r1.b + r2.b, writes=r1.b)
                    S.dve(lambda e, sl=sl: e.tensor_tensor(out=dst[0:64, sl], in0=r1[:], in1=rmul[0:64, sl], op=ALU.mult), reads=r1.b + rmul.b, writes=dst.b)
                yield

        import os as _os
        _stop = int(_os.environ.get("MLA_STOP", "9"))
        if _stop < 1:
            return
        run(rope_pair(w_in, O_CKV + 256, 8, lambda k, sl: (hT[:, k, sl], [hT.b[sl.start // 512]]), kpeT, None))
        wq = dr["w_q_up"][0]
        wkv = dr["w_kv_up"][0]
        HB = [dict(qnT=alloc(f"qnT{i}", [128, T], BF16), qpT=alloc(f"qpT{i}", [64, T], BF16), knT=alloc(f"knT{i}", [128, T], BF16),
                   vtok=alloc(f"vtok{i}", [128, NT, 128], BF16), zs=alloc(f"zs_m{i}", [128, T], BF16)) for i in range(2)]
        AT = attn_temps()

        def prep_gen(h, B):
            qnT, qpT, knT, vtok, zs = B["qnT"], B["qpT"], B["knT"], B["vtok"], B["zs"]
            wqn = load_w(wq, h * 192, 128, nk=3)
            wkn = load_w(wkv, h * 256, 128, nk=2)
            wv = load_w(wkv, h * 256 + 128, 128, nk=2)
            for tb in range(NB):
                sl = slice(tb * 512, (tb + 1) * 512)
                for k in range(3):
                    mm(bank(6), wqn[:, k, :], cqg[:, k, sl], k == 0, k == 2, wqn.b + cqg.b, [pb[6]])
                S.dve(lambda e, sl=sl: e.tensor_tensor(out=qnT[:, sl], in0=bank(6), in1=rq[:, sl], op=ALU.mult), reads=[pb[6]] + rq.b, writes=qnT.b)
                for k in range(2):
                    mm(bank(7), wkn[:, k, :], kvg[:, k, sl], k == 0, k == 1, wkn.b + kvg.b, [pb[7]])
                S.dve(lambda e, sl=sl: e.tensor_tensor(out=knT[:, sl], in0=bank(7), in1=rkv[:, sl], op=ALU.mult), reads=[pb[7]] + rkv.b, writes=knT.b)
                yield
            for q4 in range(4):
                bk = 6 + (q4 % 2)
                for j in range(4):
                    t = q4 * 4 + j
                    for k in range(2):
                        mm(bank(bk)[:, j * 128:(j + 1) * 128], kvg[:, k, t * 128:(t + 1) * 128], wv[:, k, :], k == 0, k == 1, kvg.b + wv.b, [pb[bk]])
                S.dve(lambda e, q4=q4, bk=bk: e.tensor_tensor(out=vtok[:, q4 * 4:(q4 + 1) * 4, :], in0=bank(bk).rearrange("p (t n) -> p t n", t=4),
                                                             in1=rkvc[:, q4 * 4:(q4 + 1) * 4].unsqueeze(2).to_broadcast([128, 4, 128]), op=ALU.mult),
                      reads=[pb[bk]] + rkvc.b, writes=vtok.b)
                yield
            yield from rope_pair(wq, h * 192 + 128, 3, lambda k, sl: (cqg[:, k, sl], cqg.b), qpT, rq)
            yield from silu_proj_gen(O_ZM + h * 128, zs)

        def head_attn(h, B):
            qnT, qpT, knT, vtok, zs = B["qnT"], B["qpT"], B["knT"], B["vtok"], B["zs"]
            return attn_gen(lambda kt: (knT[:, kt * 128:(kt + 1) * 128], knT.b),
                            lambda qb: (qnT[:, qb * 512:(qb + 1) * 512], qnT.b),
                            lambda kt: (vtok[:, kt, :], vtok.b), NT, SC_MLA, zs, 8 + h, AT,
                            extra=(lambda kt: (kpeT[0:64, kt * 128:(kt + 1) * 128], kpeT.b),
                                   lambda qb: (qpT[0:64, qb * 512:(qb + 1) * 512], qpT.b)))

        run(prep_gen(0, HB[0]))
        for h in range(8):
            side = prep_gen(h + 1, HB[(h + 1) % 2]) if h + 1 < 8 else None
            zipper(head_attn(h, HB[h % 2]), side, every=3)

    def phase_mem(s):
        areset()
        mt_ = [alloc(f"memt{i}", [128, D], F32) for i in range(2)]
        junk = alloc("mjunk", [128, D], BF16)
        mb = alloc("mb", [128, D], BF16)
        mst = alloc("mst", [128, 2], F32)
        mnT = alloc("mnT", [128, 8, 256], BF16)
        kmT = alloc("kmT", [128, 4, 256], BF16)
        vm = alloc("vm", [128, 2, 512], BF16)
        wmv = alloc("wmv", [128, 8, 512], BF16)
        wm = dr["w_mem_kv"][0]
        for i in range(2):
            X = mt_[i]
            S.dma(lambda e, X=X, i=i: e.dma_start(out=X[:], in_=mem[s, i * 128:(i + 1) * 128, :]), writes=X.b)
            S.act(lambda e, X=X: e.activation(out=junk[:], in_=X[:], func=AF.Square, accum_out=mst[:, 0:1]), reads=X.b, writes=junk.b + mst.b, tag="multi")
            rsqrt_inplace(mst[:, 0:1], mst.b, 1.0 / D, EPS)
            S.dve(lambda e, X=X: e.tensor_scalar(out=mb[:], in0=X[:], scalar1=mst[:, 0:1], scalar2=None, op0=ALU.mult), reads=X.b + mst.b, writes=mb.b)
            for k in range(8):
                tr(bank_bf(6)[:, k * 128:(k + 1) * 128], mb[:, k * 128:(k + 1) * 128], mb.b, [pb[6]])
            S.dve(lambda e, i=i: e.tensor_tensor(out=mnT[:, :, i * 128:(i + 1) * 128], in0=bank_bf(6).rearrange("p (k n) -> p k n", k=8),
                                                 in1=gcol_mem[:].unsqueeze(2).to_broadcast([128, 8, 128]), op=ALU.mult),
                  reads=[pb[6]] + gcol_mem.b, writes=mnT.b)
        for h in range(4):
            wt = load_w(wm, h * 128, 128)
            for k in range(8):
                mm(bank(7)[:, 0:256], wt[:, k, :], mnT[:, k, :], k == 0, k == 7, wt.b + mnT.b, [pb[7]])
            S.act(lambda e, h=h: e.activation(out=kmT[:, h, :], in_=bank(7)[:, 0:256], func=AF.Copy), reads=[pb[7]], writes=kmT.b)
        load_w(wm, 512, 512, dst=wmv)
        for i in range(2):
            for k in range(8):
                mm(bank(6), mnT[:, k, i * 128:(i + 1) * 128], wmv[:, k, :], k == 0, k == 7, mnT.b + wmv.b, [pb[6]])
            S.act(lambda e, i=i: e.activation(out=vm[:, i, :], in_=bank(6), func=AF.Copy), reads=[pb[6]], writes=vm.b)
        AT = attn_temps()
        MB = [dict(qmT=alloc(f"qmT{i}", [128, T], BF16), zs=alloc(f"zs_mem{i}", [128, T], BF16)) for i in range(2)]

        def mprep_gen(h, B):
            qmT = B["qmT"]
            wt = load_w(w_in, O_QM + h * 128, 128)
            for tb in range(NB):
                bk = 6 + (tb % 2)
                proj_fm(wt, tb, bk)
                S.act(lambda e, bk=bk, tb=tb: e.activation(out=qmT[:, tb * 512:(tb + 1) * 512], in_=bank(bk), func=AF.Copy), reads=[pb[bk]], writes=qmT.b)
                yield
            yield from silu_proj_gen(O_ZMEM + h * 128, B["zs"])

        def mattn(h, B):
            qmT = B["qmT"]
            return attn_gen(lambda kt: (kmT[:, h, kt * 128:(kt + 1) * 128], kmT.b),
                            lambda qb: (qmT[:, qb * 512:(qb + 1) * 512], qmT.b),
                            lambda kt: (vm[:, kt, h * 128:(h + 1) * 128], vm.b), 2, SC_MEM, B["zs"], 16 + h, AT)

        run(mprep_gen(0, MB[0]))
        for h in range(4):
            side = mprep_gen(h + 1, MB[(h + 1) % 2]) if h + 1 < 4 else None
            zipper(mattn(h, MB[h % 2]), side, every=1)

    TOP_OFF = ARENA_ELEMS - 28 * 1024

    def merge_weight_tiles():
        save = astate["off"]
        astate["off"] = TOP_OFF
        wbr = alloc("wbr", [128, 20, D], BF16)
        wout = alloc("wout", [128, 8, D], BF16)
        astate["off"] = save
        return wbr, wout

    def prefetch_merge_weights():
        wbr, wout = merge_weight_tiles()
        for br, (nm, nk, c0) in enumerate((("w_br_gdn", 8, 0), ("w_br_mla", 8, 8), ("w_br_mem", 4, 16))):
            src = dr[nm][0].rearrange("(k p) n -> p k n", p=128)
            for k in range(0, nk, 4):
                S.dma(lambda e, src=src, k=k, c0=c0: e.dma_start(out=wbr[:, c0 + k:c0 + k + 4, :], in_=src[:, k:k + 4, :]), writes=wbr.b, eng="gpsimd")
        src = dr["w_out"][0].rearrange("(k p) n -> p k n", p=128)
        for k in range(0, 8, 4):
            S.dma(lambda e, src=src, k=k: e.dma_start(out=wout[:, k:k + 4, :], in_=src[:, k:k + 4, :]), writes=wout.b, eng="gpsimd")

    def phase_merge(s, prefetched):
        areset()
        if not prefetched:
            prefetch_merge_weights()
        wbr, wout = merge_weight_tiles()
        oT = [alloc(f"oTb{i}", [128, 20, 512], BF16) for i in range(2)]
        mgs = [alloc(f"mg{i}", [128, 8, 512], BF16) for i in range(2)]
        acc = alloc("macc", [128, 512], F32)
        sig = [alloc(f"sig{i}", [128, 512], F32) for i in range(2)]
        tmp = alloc("mtmp", [128, 512], F32)
        xr = [alloc(f"xr{i}", [128, D], F32) for i in range(2)]
        ysb = [alloc(f"ysb{i}", [128, D], F32) for i in range(2)]
        junk = alloc("fjunk", [128, D], BF16)
        st = [alloc(f"fst{i}", [128, 2], F32) for i in range(2)]
        assert astate["off"] <= TOP_OFF
        branches = ((0, 8, 0), (1, 8, 8), (2, 4, 16))

        def gates_gen(tb):
            sl = slice(tb * 512, (tb + 1) * 512)
            O = oT[tb % 2]
            mg = mgs[tb % 2]
            S.dma(lambda e, O=O, sl=sl: e.dma_start(out=O[:], in_=scr[:, :, sl].rearrange("c p n -> p c n")), reads=scrb, writes=O.b)
            for fo in range(8):
                fsl = slice(fo * 128, (fo + 1) * 128)
                for br, nk, c0 in branches:
                    wg = load_w(w_in, O_GATE + br * 1024 + fo * 128, 128)
                    gb = 6 + (br % 2)
                    proj_fm(wg, tb, gb)
                    SG = sig[br % 2]
                    S.act(lambda e, gb=gb, SG=SG: e.activation(out=SG[:], in_=bank(gb), func=AF.Sigmoid), reads=[pb[gb]], writes=SG.b)
                    bb = 4 + (br % 2)
                    for k in range(nk):
                        mm(bank(bb), wbr[:, c0 + k, fsl], O[:, c0 + k, :], k == 0, k == nk - 1, wbr.b + O.b, [pb[bb]])
                    if br == 0:
                        S.dve(lambda e, bb=bb, SG=SG: e.tensor_tensor(out=acc[:], in0=bank(bb), in1=SG[:], op=ALU.mult), reads=[pb[bb]] + SG.b, writes=acc.b)
                    else:
                        S.dve(lambda e, bb=bb, SG=SG: e.tensor_tensor(out=tmp[:], in0=bank(bb), in1=SG[:], op=ALU.mult), reads=[pb[bb]] + SG.b, writes=tmp.b)
                        if br == 1:
                            S.dve(lambda e: e.tensor_tensor(out=acc[:], in0=acc[:], in1=tmp[:], op=ALU.add), reads=acc.b + tmp.b, writes=acc.b)
                        else:
                            S.dve(lambda e, fo=fo, mg=mg: e.tensor_tensor(out=mg[:, fo, :], in0=acc[:], in1=tmp[:], op=ALU.add), reads=acc.b + tmp.b, writes=mg.b)
                    yield

        def epi_gen(tb):
            mg = mgs[tb % 2]
            for j in range(4):
                t = tb * 4 + j
                X, Y, ST = xr[j % 2], ysb[j % 2], st[j % 2]
                S.dma(lambda e, X=X, t=t: e.dma_start(out=X[:], in_=x[s, t * 128:(t + 1) * 128, :]), writes=X.b)
                for hf in range(2):
                    for k in range(8):
                        mm(bank(hf), mg[:, k, j * 128:(j + 1) * 128], wout[:, k, hf * 512:(hf + 1) * 512], k == 0, k == 7, mg.b + wout.b, [pb[hf]])
                    yield
                S.dve(lambda e, X=X, Y=Y: e.tensor_tensor(out=Y[:], in0=bank2(0).rearrange("p a b -> p (a b)"), in1=X[:], op=ALU.add), reads=[pb[0], pb[1]] + X.b, writes=Y.b)
                S.act(lambda e, Y=Y, ST=ST: e.activation(out=junk[:], in_=Y[:], func=AF.Square, accum_out=ST[:, 0:1]), reads=Y.b, writes=junk.b + ST.b, tag="multi")
                yield
                rsqrt_inplace(ST[:, 0:1], ST.b, 1.0 / D, EPS)
                yield
                S.dve(lambda e, Y=Y, ST=ST: e.scalar_tensor_tensor(out=Y[:], in0=Y[:], scalar=ST[:, 0:1], in1=fgain[:], op0=ALU.mult, op1=ALU.mult),
                      reads=Y.b + ST.b + fgain.b, writes=Y.b)
                S.dma(lambda e, Y=Y, t=t: e.dma_start(out=y[s, t * 128:(t + 1) * 128, :], in_=Y[:]), reads=Y.b, final=True)
                yield

        run(gates_gen(0))
        for tb in range(NB):
            if tb + 1 < NB:
                zipper(gates_gen(tb + 1), epi_gen(tb), every=1)
            else:
                run(epi_gen(tb))

    for s in range(NSEQ):
        S.barrier()
        phase_h(s)
        if "G" in phases:
            phase_gdn(s)
        if "M" in phases:
            S.barrier()
            phase_mla(s)
        if "C" in phases:
            S.barrier()
            if "X" in phases:
                prefetch_merge_weights()
            phase_mem(s)
        if "X" in phases:
            S.barrier()
            phase_merge(s, "C" in phases)
    S.barrier()
    print("ops", S.stats(), flush=True)
    S.emit()
    return nc


_NC_CACHE = {}


def _run(xs, ms, weights, ncores, nseq, phases="GMCX", dbg=False):
    key = (nseq, phases, dbg)
    if key not in _NC_CACHE:
        _NC_CACHE[key] = build_nc(nseq, phases, dbg)
    nc = _NC_CACHE[key]
    consts = host_consts()
    in_maps = []
    for c in range(ncores):
        m = {"x": np.ascontiguousarray(xs[c * nseq:(c + 1) * nseq]), "mem": np.ascontiguousarray(ms[c * nseq:(c + 1) * nseq])}
        for k in WEIGHT_SHAPES:
            m[k] = np.ascontiguousarray(weights[k], dtype=np.float32)
        for k, v in consts.items():
            m["c_" + k] = v
        in_maps.append(m)
    res = run_bass_kernel_spmd(nc, in_maps, core_ids=list(range(ncores)))
    return res


def kernel(**inputs):
    xs = np.concatenate([inputs["x_prompt"], inputs["x_sample"]], axis=0)
    ms = np.concatenate([inputs["mem_prompt"], inputs["mem_sample"]], axis=0)
    weights = {k: np.asarray(inputs[k]) for k in WEIGHT_SHAPES}
    nseq = xs.shape[0] // 8
    res = _run(xs, ms, weights, 8, nseq)
    yall = np.concatenate([r["y"] for r in res.results], axis=0)
    nb = inputs["x_prompt"].shape[0]
    return (np.ascontiguousarray(yall[:nb]), np.ascontiguousarray(yall[nb:]))
```

```python
import contextlib
import math
import numpy as np
import concourse.bass as bass
import concourse.mybir as mybir
from concourse.bass_utils import run_bass_kernel_spmd

F32 = mybir.dt.float32
BF16 = mybir.dt.bfloat16
I32 = mybir.dt.int32
ALU = mybir.AluOpType
AF = mybir.ActivationFunctionType
AX = mybir.AxisListType

ENGINES = ("tensor", "vector", "scalar", "gpsimd", "sync")
N_DMA_SEMS = 12
import os as _os_fw
ATTACH_ENGINES = tuple(x for x in _os_fw.environ.get('ATTACH', 'vector,scalar,gpsimd').split(',') if x)


class Buf:
    __slots__ = ("name", "w", "r", "excl")

    def __init__(self, name):
        self.name = name
        self.excl = False
        self.w = {}
        self.r = {}


class Op:
    __slots__ = ("eng", "idx", "fn", "deps", "dma", "signal", "done_sem", "done_val", "tag")

    def __init__(self, eng, idx, fn, dma, tag):
        self.eng = eng
        self.idx = idx
        self.fn = fn
        self.dma = dma
        self.deps = {}
        self.signal = False
        self.done_sem = None
        self.done_val = None
        self.tag = tag


class Sched:
    def __init__(self, nc):
        self.nc = nc
        self.ops = {e: [] for e in ENGINES}
        self.stack = contextlib.ExitStack()
        self.nbuf = 0
        self.final_waits = []

    def sbuf(self, name, shape, dtype):
        return self.stack.enter_context(self.nc.sbuf_tensor(name, list(shape), dtype))

    def psum(self, name, shape, dtype):
        return self.stack.enter_context(self.nc.psum_tensor(name, list(shape), dtype))

    def buf(self, name=None):
        self.nbuf += 1
        return Buf(name or f"b{self.nbuf}")

    def bufs(self, n, name="b"):
        return [self.buf(f"{name}{i}") for i in range(n)]

    def add(self, eng, fn, reads=(), writes=(), dma=False, tag=None, final=False):
        lst = self.ops[eng]
        op = Op(eng, len(lst), fn, dma, tag)
        lst.append(op)
        deps = []
        for b in reads:
            deps.extend(b.w.values())
            if b.excl:
                deps.extend(o for k_, o in b.r.items() if o.eng != eng)
        for b in writes:
            deps.extend(b.w.values())
            deps.extend(b.r.values())
        for d in deps:
            if d is op:
                continue
            if d.eng == eng and eng == "tensor" and not d.dma and not dma:
                continue
            key = (d.eng, d.idx) if d.dma else d.eng
            cur = op.deps.get(key)
            if cur is None or cur.idx < d.idx:
                op.deps[key] = d
        mykey = (eng, op.idx) if dma else eng
        for b in reads:
            b.r[mykey] = op
        for b in writes:
            b.w = {mykey: op}
            b.r = {}
        if final:
            self.final_waits.append(op)
        return op

    def pe(self, fn, reads=(), writes=(), **k):
        return self.add("tensor", fn, reads, writes, **k)

    def dve(self, fn, reads=(), writes=(), **k):
        return self.add("vector", fn, reads, writes, **k)

    def act(self, fn, reads=(), writes=(), **k):
        return self.add("scalar", fn, reads, writes, **k)

    def pool(self, fn, reads=(), writes=(), **k):
        return self.add("gpsimd", fn, reads, writes, **k)

    def dma(self, fn, reads=(), writes=(), eng="sync", **k):
        return self.add(eng, fn, reads, writes, dma=True, **k)

    def emit(self):
        nc = self.nc
        for e in ENGINES:
            for op in self.ops[e]:
                for d in op.deps.values():
                    d.signal = True
        for op in self.final_waits:
            op.signal = True
        sems = {e: self.stack.enter_context(nc.semaphore(f"s_{e}")) for e in ENGINES}
        dma_sems = {e: [self.stack.enter_context(nc.semaphore(f"d_{e}_{j}")) for j in range(N_DMA_SEMS)]
                    for e in ("sync", "gpsimd", "scalar")}
        dma_prev = {}
        for e in ENGINES:
            cnt = 0
            nd = 0
            dcount = [0] * N_DMA_SEMS
            for op in self.ops[e]:
                if op.dma:
                    j = nd % N_DMA_SEMS
                    nd += 1
                    dcount[j] += 1
                    op.done_sem = dma_sems[e][j]
                    op.done_val = 16 * dcount[j]
                    op.signal = True
                elif op.signal:
                    cnt += 1
                    op.done_sem = sems[e]
                    op.done_val = cnt
        block = self.stack.enter_context(nc.Block())
        sched = self

        def run(e, eng):
            waited = {}
            for op in sched.ops[e]:
                ws = []
                for d in op.deps.values():
                    ws.append((d.done_sem, d.done_val))
                if op.dma and op.done_val > 16:
                    ws.append((op.done_sem, op.done_val - 16))
                need = []
                for sem, val in ws:
                    k = id(sem)
                    if waited.get(k, 0) >= val:
                        continue
                    waited[k] = val
                    need.append((sem, val))
                attach = None
                if need and e in ATTACH_ENGINES and not op.dma and op.tag != "multi":
                    attach = need.pop()
                for sem, val in need:
                    eng.wait_ge(sem, val)
                inst = op.fn(eng)
                if attach is not None:
                    inst._wait_ge(attach[0], attach[1])
                if op.signal:
                    inst.then_inc(op.done_sem, 16 if op.dma else 1)
            if e == "sync":
                for op in sched.final_waits:
                    eng.wait_ge(op.done_sem, op.done_val)

        @block.tensor
        def _(eng):
            run("tensor", eng)

        @block.vector
        def _(eng):
            run("vector", eng)

        @block.scalar
        def _(eng):
            run("scalar", eng)

        @block.gpsimd
        def _(eng):
            run("gpsimd", eng)

        @block.sync
        def _(eng):
            run("sync", eng)

        self.stack.close()

    def stats(self):
        return {e: len(self.ops[e]) for e in ENGINES}

def _sched_barrier(self):
    deps = []
    for e in ENGINES:
        last = None
        for op in reversed(self.ops[e]):
            if not op.dma:
                last = op
                break
        if last is not None:
            deps.append(last)
    deps.extend(self.dma_open)
    self.dma_open = []
    lst = self.ops["sync"]
    op = Op("sync", len(lst), lambda eng: eng.nop(), False, "barrier")
    lst.append(op)
    for d in deps:
        key = (d.eng, d.idx) if d.dma else d.eng
        op.deps[key] = d
    self.bar = op


Sched.barrier = _sched_barrier


class Tile:
    def __init__(self, t, nb, name, S):
        self.t = t
        self.b = [S.buf(f"{name}.{i}") for i in range(nb)]

    def __getitem__(self, k):
        return self.t[k]


T = 2048
D = 1024
NT = 16
NB = 4
EPS = 1e-6
O_QKV, O_AB, O_ZG, O_CQ, O_CKV, O_ZM, O_QM, O_ZMEM, O_GATE = 0, 3072, 3104, 4128, 4512, 4832, 5856, 6368, 6880
SC_GDN = 128 ** -0.5
SC_MLA = 192 ** -0.5
SC_MEM = 128 ** -0.5
ARENA_ELEMS = 74 * 1024


def host_consts():
    c = {}
    p = np.arange(128)
    same = (p[:, None] // 64) == (p[None, :] // 64)
    c["ident"] = np.eye(128, dtype=np.float32)
    c["ones"] = np.ones((128, 128), np.float32)
    c["lfwd"] = (same & (p[:, None] <= p[None, :])).astype(np.float32)
    c["lbwd"] = (same & (p[:, None] >= p[None, :])).astype(np.float32)
    c["cblk"] = same.astype(np.float32)
    c["c0"] = np.repeat((p < 64)[:, None], 128, 1).astype(np.float32)
    c["c1"] = np.repeat((p >= 64)[:, None], 128, 1).astype(np.float32)
    c["ms_f"] = (same & (p[:, None] < p[None, :])).astype(np.float32)
    c["mi_f"] = c["lfwd"].copy()
    c["ms_b"] = (same & (p[:, None] > p[None, :])).astype(np.float32)
    c["mi_b"] = c["lbwd"].copy()
    half = 32
    inv = 10000.0 ** (-np.arange(half, dtype=np.float32) / half)
    ang = np.arange(T, dtype=np.float32)[None, :] * inv[:, None]
    cos = np.cos(ang).astype(np.float32)
    sin = np.sin(ang).astype(np.float32)
    c["cos2"] = np.concatenate([cos, cos], 0)
    c["sins"] = np.concatenate([-sin, sin], 0)
    return {k: np.ascontiguousarray(v, dtype=np.float32) for k, v in c.items()}


CONST_SHAPES = {"ident": [128, 128], "ones": [128, 128], "lfwd": [128, 128], "lbwd": [128, 128], "cblk": [128, 128],
                "c0": [128, 128], "c1": [128, 128], "ms_f": [128, 128], "mi_f": [128, 128], "ms_b": [128, 128],
                "mi_b": [128, 128], "cos2": [64, T], "sins": [64, T]}

WEIGHT_SHAPES = {"attn_norm_gain": [1, 1024], "w_in": [1, 1024, 9952], "conv_w": [1, 5, 3072], "a_log": [1, 2, 8],
                 "dt_bias": [1, 2, 8], "gdn_norm_gain": [1, 128], "q_norm_gain": [1, 384], "w_q_up": [1, 384, 1536],
                 "kv_norm_gain": [1, 256], "w_kv_up": [1, 256, 2048], "mem_norm_gain": [1, 1024],
                 "w_mem_kv": [1, 1024, 1024], "w_br_gdn": [1, 1024, 1024], "w_br_mla": [1, 1024, 1024],
                 "w_br_mem": [1, 512, 1024], "w_out": [1, 1024, 1024], "final_norm_gain": [1024]}


def build_nc(NSEQ, phases="GMCX", dbg=False):
    nc = bass.Bass("TRN2", target_bir_lowering=False)
    dr = {}
    x = nc.dram_tensor("x", [NSEQ, T, D], F32, kind="ExternalInput").ap()
    mem = nc.dram_tensor("mem", [NSEQ, 256, D], F32, kind="ExternalInput").ap()
    for k, shp in WEIGHT_SHAPES.items():
        dr[k] = nc.dram_tensor(k, shp, F32, kind="ExternalInput").ap()
    for k, shp in CONST_SHAPES.items():
        dr[k] = nc.dram_tensor("c_" + k, shp, F32, kind="ExternalInput").ap()
    y = nc.dram_tensor("y", [NSEQ, T, D], F32, kind="ExternalOutput").ap()
    scr = nc.dram_tensor("scr", [20, 128, T], BF16, kind=("ExternalOutput" if dbg else "Internal")).ap()
    w_in = dr["w_in"][0]

    S = Sched(nc)
    S.dma_open = []
    S.bar = None
    _orig_add = S.add

    def add(eng, fn, reads=(), writes=(), dma=False, tag=None, final=False):
        op = _orig_add(eng, fn, reads, writes, dma=dma, tag=tag, final=final)
        if S.bar is not None:
            op.deps["sync"] = S.bar if ("sync" not in op.deps or op.deps["sync"].idx < S.bar.idx) else op.deps["sync"]
        if dma:
            S.dma_open.append(op)
        return op
    S.add = add

    def tile(name, shape, dtype, nb=1):
        return Tile(S.sbuf(name, shape, dtype), nb, name, S)

    cst = {}
    for k in ("ident", "ones", "lfwd", "lbwd", "cblk", "c0", "c1"):
        cst[k] = tile("k_" + k, [128, 128], F32)
    ident_bf = tile("ident_bf", [128, 128], BF16)
    ones_bf = tile("ones_bf", [128, 128], BF16)
    masks = {k: tile("m_" + k, [128, 128], BF16) for k in ("ms_f", "mi_f", "ms_b", "mi_b")}
    hT = tile("hT", [128, 8, T], BF16, nb=NB)
    gcol_attn = tile("gcol_attn", [128, 8], F32)
    gcol_mem = tile("gcol_mem", [128, 8], F32)
    gcol_q = tile("gcol_q", [128, 3], F32)
    gcol_kv = tile("gcol_kv", [128, 2], F32)
    gcol_gdn = tile("gcol_gdn", [128, 1], F32)
    fgain = tile("fgain", [128, D], F32)
    cw = tile("cw", [128, 5, 24], F32)
    alog_bc = tile("alog_bc", [128, 16], F32)
    dtb_bc = tile("dtb_bc", [128, 16], F32)
    negA = tile("negA", [128, 16], F32)
    stage = tile("stage", [128, 128], F32)
    arena = S.sbuf("arena", [128, ARENA_ELEMS], BF16)
    PS = S.psum("PS", [128, 8, 512], F32)
    pb = [S.buf(f"bank{i}") for i in range(8)]
    for _b in pb:
        _b.excl = True
    wpool = [tile(f"wp{i}", [128, 8, 128], BF16) for i in range(5)]
    wctr = [0]

    def bank(i):
        return PS[:, i, :]

    def bank_bf(i):
        return PS[:, i, :].bitcast(BF16)

    def bank2(i):
        return PS[:, i:i + 2, :]

    astate = {"off": 0}

    def areset():
        astate["off"] = 0

    def alloc(name, shape, dtype, nb=1):
        n = int(np.prod(shape[1:]))
        n2 = n * 2 if dtype == F32 else n
        n2 = (n2 + 1) // 2 * 2
        off = astate["off"]
        assert off + n2 <= ARENA_ELEMS, (name, off, n2)
        astate["off"] = off + n2
        v = arena[:shape[0], off:off + n2]
        if dtype == F32:
            v = v.bitcast(F32)
        if len(shape) == 3:
            v = v.rearrange("p (a b) -> p a b", a=shape[1])
        elif len(shape) == 4:
            v = v.rearrange("p (a b c) -> p a b c", a=shape[1], b=shape[2])
        return Tile(v, nb, name, S)

    def load_w(src2d, c0, ncols, nk=8, dst=None):
        if dst is None:
            dst = wpool[wctr[0] % len(wpool)]
            wctr[0] += 1
        src = src2d[:, c0:c0 + ncols].rearrange("(k p) n -> p k n", p=128)
        S.dma(lambda e: e.dma_start(out=dst[:, 0:nk, 0:ncols], in_=src), writes=dst.b, eng="gpsimd")
        return dst

    def mm(out, lhsT, rhs, start, stop, reads, writes):
        S.pe(lambda e: e.matmul(out, lhsT=lhsT, rhs=rhs, start=start, stop=stop), reads=reads, writes=writes)

    def tr(out, in_, reads, writes, f32=False):
        idt = cst["ident"] if f32 else ident_bf
        S.pe(lambda e: e.transpose(out=out, in_=in_, identity=idt[:]), reads=list(reads) + idt.b, writes=writes)

    def proj_fm(wt, tb, bk, nk=8, m=128):
        for k in range(nk):
            mm(bank(bk)[0:m, :], wt[:, k, 0:m], hT[:, k, tb * 512:(tb + 1) * 512], k == 0, k == nk - 1,
               wt.b + [hT.b[tb]], [pb[bk]])

    def rsqrt_inplace(t_ap, bufs, scale, eps):
        S.act(lambda e: e.activation(out=t_ap, in_=t_ap, func=AF.Sqrt, scale=scale, bias=eps_t[:t_ap.shape[0], 0:1]), reads=bufs + eps_t.b, writes=bufs)
        S.dve(lambda e: e.reciprocal(out=t_ap, in_=t_ap), reads=bufs, writes=bufs)

    eps_t = tile("eps_t", [128, 1], F32)
    S.pool(lambda e: e.memset(eps_t[:], EPS), writes=eps_t.b)
    lnsc_t = tile("lnsc_t", [128, 1], F32)
    S.pool(lambda e: e.memset(lnsc_t[:], math.log(SC_GDN)), writes=lnsc_t.b)
    zero_t = tile("zero_t", [128, 1], F32)
    S.pool(lambda e: e.memset(zero_t[:], 0.0), writes=zero_t.b)

    def ld(dst, src, eng="sync"):
        S.dma(lambda e: e.dma_start(out=dst[:], in_=src), writes=dst.b, eng=eng)

    for k in ("ident", "ones", "lfwd", "lbwd", "cblk", "c0", "c1"):
        ld(cst[k], dr[k])
    S.dve(lambda e: e.tensor_copy(out=ident_bf[:], in_=cst["ident"][:]), reads=cst["ident"].b, writes=ident_bf.b)
    S.dve(lambda e: e.tensor_copy(out=ones_bf[:], in_=cst["ones"][:]), reads=cst["ones"].b, writes=ones_bf.b)
    for k in masks:
        S.dma(lambda e, k=k: e.dma_start(out=stage[:], in_=dr[k]), writes=stage.b)
        S.dve(lambda e, k=k: e.tensor_copy(out=masks[k][:], in_=stage[:]), reads=stage.b, writes=masks[k].b)

    def rep8(tl):
        return tl[:].unsqueeze(1).to_broadcast([128, 8, 128])
    S.dma(lambda e: e.dma_start(out=gcol_attn[:], in_=dr["attn_norm_gain"][0].rearrange("(k p) -> p k", p=128), allow_slow_non_contiguous=True), writes=gcol_attn.b)
    S.dma(lambda e: e.dma_start(out=gcol_mem[:], in_=dr["mem_norm_gain"][0].rearrange("(k p) -> p k", p=128), allow_slow_non_contiguous=True), writes=gcol_mem.b)
    S.dma(lambda e: e.dma_start(out=gcol_q[:], in_=dr["q_norm_gain"][0].rearrange("(k p) -> p k", p=128), allow_slow_non_contiguous=True), writes=gcol_q.b)
    S.dma(lambda e: e.dma_start(out=gcol_kv[:], in_=dr["kv_norm_gain"][0].rearrange("(k p) -> p k", p=128), allow_slow_non_contiguous=True), writes=gcol_kv.b)
    S.dma(lambda e: e.dma_start(out=gcol_gdn[:], in_=dr["gdn_norm_gain"][0].rearrange("(k p) -> p k", p=128), allow_slow_non_contiguous=True), writes=gcol_gdn.b)
    S.dma(lambda e: e.dma_start(out=fgain[:], in_=dr["final_norm_gain"].partition_broadcast(128)), writes=fgain.b, eng="gpsimd")
    for j in range(5):
        S.dma(lambda e, j=j: e.dma_start(out=cw[:, j, :], in_=dr["conv_w"][0, j].rearrange("(c p) -> p c", p=128), allow_slow_non_contiguous=True), writes=cw.b)
    S.dma(lambda e: e.dma_start(out=alog_bc[:], in_=dr["a_log"][0].rearrange("a b -> (a b)").partition_broadcast(128)), writes=alog_bc.b, eng="gpsimd")
    S.dma(lambda e: e.dma_start(out=dtb_bc[:], in_=dr["dt_bias"][0].rearrange("a b -> (a b)").partition_broadcast(128)), writes=dtb_bc.b, eng="gpsimd")
    S.act(lambda e: e.activation(out=negA[:], in_=alog_bc[:], func=AF.Exp), reads=alog_bc.b, writes=negA.b)
    S.dve(lambda e: e.tensor_scalar(out=negA[:], in0=negA[:], scalar1=-1.0, scalar2=None, op0=ALU.mult), reads=negA.b, writes=negA.b)

    def phase_h(s):
        areset()
        xt = [alloc(f"xt{i}", [128, D], F32) for i in range(3)]
        junk = alloc("junk", [128, D], BF16)
        hb = [alloc(f"hb{i}", [128, D], BF16) for i in range(2)]
        st = [alloc(f"st{i}", [128, 2], F32) for i in range(2)]
        def stage1(t):
            X, H, ST = xt[t % 3], hb[t % 2], st[t % 2]
            S.dma(lambda e, X=X, t=t: e.dma_start(out=X[:], in_=x[s, t * 128:(t + 1) * 128, :]), writes=X.b)
            S.act(lambda e, X=X, ST=ST: e.activation(out=junk[:], in_=X[:], func=AF.Square, accum_out=ST[:, 0:1]),
                  reads=X.b, writes=junk.b + ST.b, tag="multi")
            rsqrt_inplace(ST[:, 0:1], ST.b, 1.0 / D, EPS)
            S.dve(lambda e, X=X, H=H, ST=ST: e.tensor_scalar(out=H[:], in0=X[:], scalar1=ST[:, 0:1], scalar2=None, op0=ALU.mult),
                  reads=X.b + ST.b, writes=H.b)

        def stage2(t):
            H = hb[t % 2]
            bk = 6 + (t % 2)
            for k in range(8):
                tr(bank_bf(bk)[:, k * 128:(k + 1) * 128], H[:, k * 128:(k + 1) * 128], H.b, [pb[bk]])
            S.dve(lambda e, bk=bk, t=t: e.tensor_tensor(out=hT[:, :, t * 128:(t + 1) * 128],
                                                       in0=bank_bf(bk).rearrange("p (k n) -> p k n", k=8),
                                                       in1=gcol_attn[:].unsqueeze(2).to_broadcast([128, 8, 128]), op=ALU.mult),
                  reads=[pb[bk]] + gcol_attn.b, writes=[hT.b[t // 4]])

        stage1(0)
        for t in range(NT):
            if t + 1 < NT:
                stage1(t + 1)
            stage2(t)

    def finalize_branch(oacc, zs, dst_chunk, gain_col=None, norm=False):
        ob = alloc("ob", [128, T], BF16)
        if norm:
            sq = alloc("fsq", [128, 512], BF16)
            rr = alloc("frr", [128, 512], F32)
            for tb in range(NB):
                sl = slice(tb * 512, (tb + 1) * 512)
                S.act(lambda e, sl=sl: e.activation(out=sq[:], in_=oacc[:, sl], func=AF.Square), reads=oacc.b, writes=sq.b)
                mm(bank(6), ones_bf[:], sq[:], True, True, ones_bf.b + sq.b, [pb[6]])
                S.act(lambda e: e.activation(out=rr[:], in_=bank(6), func=AF.Sqrt, scale=1.0 / 128, bias=eps_t[:, 0:1]), reads=[pb[6]] + eps_t.b, writes=rr.b)
                S.dve(lambda e: e.reciprocal(out=rr[:], in_=rr[:]), reads=rr.b, writes=rr.b)
                S.dve(lambda e, sl=sl: e.tensor_tensor(out=rr[:], in0=oacc[:, sl], in1=rr[:], op=ALU.mult), reads=oacc.b + rr.b, writes=rr.b)
                S.dve(lambda e, sl=sl: e.scalar_tensor_tensor(out=ob[:, sl], in0=rr[:], scalar=gain_col[:, 0:1], in1=zs[:, sl], op0=ALU.mult, op1=ALU.mult),
                      reads=rr.b + gain_col.b + zs.b, writes=ob.b)
        else:
            S.dve(lambda e: e.tensor_tensor(out=ob[:], in0=oacc[:], in1=zs[:], op=ALU.mult), reads=oacc.b + zs.b, writes=ob.b)
        S.dma(lambda e: e.dma_start(out=scr[dst_chunk], in_=ob[:]), reads=ob.b, writes=[scrb[dst_chunk]])

    scrb = [S.buf(f"scr{i}") for i in range(20)]

    def run(gen):
        for _ in gen:
            pass

    def zipper(main, side, every=1, nside=1):
        i = 0
        for _ in main:
            i += 1
            if side is not None and i % every == 0:
                for _r in range(nside):
                    try:
                        next(side)
                    except StopIteration:
                        side = None
                        break
        if side is not None:
            run(side)

    def silu_proj_gen(col0, zs, banks=(6, 7)):
        wt = load_w(w_in, col0, 128)
        for tb in range(NB):
            bk = banks[tb % 2]
            proj_fm(wt, tb, bk)
            S.act(lambda e, bk=bk, tb=tb: e.activation(out=zs[:, tb * 512:(tb + 1) * 512], in_=bank(bk), func=AF.Silu),
                  reads=[pb[bk]], writes=zs.b)
            yield

    def silu_proj(col0, name):
        zs = alloc(name, [128, T], BF16)
        run(silu_proj_gen(col0, zs))
        return zs

    def phase_gdn(s):
        import os as _os
        GZ = int(_os.environ.get('GZ', '1'))
        GE = int(_os.environ.get('GE', '1'))
        SCAN_DVE_SB = int(_os.environ.get('SCAN_DVE_SB', '1'))
        G1B = [int(c) for c in _os.environ.get('G1B', '676767')]
        areset()
        wab = load_w(w_in, O_AB, 32)
        ab = alloc("ab", [128, NT, 32], F32)
        for t in range(NT):
            for k in range(8):
                mm(bank(6)[:, t * 32:(t + 1) * 32], hT[:, k, t * 128:(t + 1) * 128], wab[:, k, 0:32], k == 0, k == 7,
                   [hT.b[t // 4]] + wab.b, [pb[6]])
        S.dve(lambda e: e.tensor_copy(out=ab[:], in_=bank(6).rearrange("p (t n) -> p t n", t=NT)), reads=[pb[6]], writes=ab.b)
        tabs = {n: alloc("tb_" + n, [128, NT, 16], F32) for n in ("xg", "t1", "g", "beta", "gc", "gtot", "cbg", "cd", "nbeta")}
        egrep = alloc("egrep", [128, NT, 2, 16], F32)
        xg, t1, g, beta, gc, gtot, cbg, cd, nbeta = (tabs[n] for n in ("xg", "t1", "g", "beta", "gc", "gtot", "cbg", "cd", "nbeta"))
        bc16 = lambda tl: tl[:].unsqueeze(1).to_broadcast([128, NT, 16])
        S.dve(lambda e: e.tensor_tensor(out=xg[:], in0=ab[:, :, 0:16], in1=bc16(dtb_bc), op=ALU.add), reads=ab.b + dtb_bc.b, writes=xg.b)
        S.act(lambda e: e.activation(out=t1[:], in_=xg[:], func=AF.Abs), reads=xg.b, writes=t1.b)
        S.act(lambda e: e.activation(out=t1[:], in_=t1[:], func=AF.Exp, scale=-1.0), reads=t1.b, writes=t1.b)
        S.act(lambda e: e.activation(out=t1[:], in_=t1[:], func=AF.Ln, bias=1.0), reads=t1.b, writes=t1.b)
        S.dve(lambda e: e.scalar_tensor_tensor(out=t1[:], in0=xg[:], scalar=0.0, in1=t1[:], op0=ALU.max, op1=ALU.add), reads=xg.b + t1.b, writes=t1.b)
        S.dve(lambda e: e.tensor_tensor(out=g[:], in0=t1[:], in1=bc16(negA), op=ALU.mult), reads=t1.b + negA.b, writes=g.b)
        S.act(lambda e: e.activation(out=beta[:], in_=ab[:, :, 16:32], func=AF.Sigmoid), reads=ab.b, writes=beta.b)
        S.dve(lambda e: e.tensor_scalar(out=nbeta[:], in0=beta[:], scalar1=-1.0, scalar2=None, op0=ALU.mult), reads=beta.b, writes=nbeta.b)
        for t in range(NT):
            mm(bank(6)[:, t * 16:t * 16 + 8], cst["lfwd"][:], g[:, t, 0:8], True, True, cst["lfwd"].b + g.b, [pb[6]])
            mm(bank(6)[:, t * 16 + 8:t * 16 + 16], cst["lbwd"][:], g[:, t, 8:16], True, True, cst["lbwd"].b + g.b, [pb[6]])
            mm(bank(7)[:, t * 16:(t + 1) * 16], cst["cblk"][:], g[:, t, :], True, True, cst["cblk"].b + g.b, [pb[7]])
            mm(bank(4)[:, t * 32:t * 32 + 16], cst["c0"][:], g[:, t, :], True, True, cst["c0"].b + g.b, [pb[4]])
            mm(bank(4)[:, t * 32 + 16:t * 32 + 32], cst["c1"][:], g[:, t, :], True, True, cst["c1"].b + g.b, [pb[4]])
        S.dve(lambda e: e.tensor_copy(out=gc[:], in_=bank(6)[:, 0:256].rearrange("p (t n) -> p t n", t=NT)), reads=[pb[6]], writes=gc.b)
        S.dve(lambda e: e.tensor_copy(out=gtot[:], in_=bank(7)[:, 0:256].rearrange("p (t n) -> p t n", t=NT)), reads=[pb[7]], writes=gtot.b)
        S.act(lambda e: e.activation(out=egrep[:].rearrange("p t j n -> p (t j n)"), in_=bank(4), func=AF.Exp), reads=[pb[4]], writes=egrep.b)
        S.act(lambda e: e.activation(out=cbg[:], in_=gc[:], func=AF.Exp), reads=gc.b, writes=cbg.b)
        S.dve(lambda e: e.tensor_tensor(out=cbg[:], in0=cbg[:], in1=beta[:], op=ALU.mult), reads=cbg.b + beta.b, writes=cbg.b)
        S.dve(lambda e: e.tensor_tensor(out=cd[:], in0=gtot[:], in1=gc[:], op=ALU.subtract), reads=gtot.b + gc.b, writes=cd.b)
        S.act(lambda e: e.activation(out=cd[:], in_=cd[:], func=AF.Exp), reads=cd.b, writes=cd.b)
        base_off = astate["off"]
        DIRS = []
        for d in range(2):
            DIRS.append(dict(Kd=alloc(f"Kd{d}", [128, NT, 128], BF16), QgT=alloc(f"QgT{d}", [128, T], BF16),
                             intraT=alloc(f"intraT{d}", [128, NT, 128], BF16), U=alloc(f"U{d}", [128, NT, 128], F32),
                             WT=alloc(f"WT{d}", [128, T], BF16), Kbg=alloc(f"Kbg{d}", [128, NT, 128], BF16),
                             Vb=alloc(f"Vb{d}", [128, NT, 128], BF16)))
        qT = alloc("qT", [128, T], BF16)
        kT = alloc("kT", [128, T], BF16)
        vT = alloc("vT", [128, T], BF16)
        off_R = astate["off"]
        G1T = [dict(xpad=alloc(f"xpad{i}", [128, T + 4], BF16), diag=alloc(f"diag{i}", [128, 5, 128], BF16),
                    ysil=alloc(f"ysil{i}", [128, 512], F32), sq=alloc(f"sq{i}", [128, 512], BF16), rr=alloc(f"rr{i}", [128, 512], F32))
               for i in range(2)]
        G1T.append(dict(xpad=alloc("xpad2", [128, T + 4], BF16), diag=alloc("diag2", [128, 5, 128], BF16), ysil=None, sq=None, rr=None))
        Ktok = alloc("Ktok", [128, NT, 128], BF16)
        Vtok = alloc("Vtok", [128, NT, 128], BF16)
        end1 = astate["off"]
        astate["off"] = off_R
        PT = [dict(diagG=alloc(f"diagG{i}", [128, 8, 128], F32), ET=alloc(f"ET{i}", [128, 8, 128], BF16),
                   EMs=alloc(f"EMs{i}", [128, 8, 128], BF16), EMi=alloc(f"EMi{i}", [128, 8, 128], BF16),
                   expR=alloc(f"expR{i}", [128, 8, 128], BF16), X=alloc(f"X{i}", [128, 8, 128], BF16),
                   N=alloc(f"N{i}", [128, 8, 128], BF16), P=alloc(f"P{i}", [128, 8, 128], BF16)) for i in range(2)]
        astate["off"] = max(end1, astate["off"])
        oacc = alloc("oacc", [128, T], F32, nb=NB)
        Sf = [alloc(f"Sf{d}", [128, 128], F32) for d in range(2)]
        Sb = [alloc(f"Sb{d}", [128, 128], BF16) for d in range(2)]
        vn = [[alloc(f"vn{d}{i}", [128, 128], BF16) for i in range(2)] for d in range(2)]
        zs_g = alloc("zs_g", [128, T], BF16)
        ob_g = alloc("ob_g", [128, T], BF16)
        fsq = alloc("fsq", [128, 512], BF16)
        frr = alloc("frr", [128, 512], F32)

        def rsqrt_act(out_ap, in_ap, scale, rd, wr):
            S.act(lambda e: e.activation(out=out_ap, in_=in_ap, func=AF.Ln, scale=scale, bias=eps_t[:, 0:1]), reads=rd + eps_t.b, writes=wr)
            S.act(lambda e: e.activation(out=out_ap, in_=out_ap, func=AF.Exp, scale=-0.5), reads=wr, writes=wr)

        def g1_chunk(which, h, dstT, TT_, bA, bB):
            xpad, diag, ysil, sq, rr = TT_["xpad"], TT_["diag"], TT_["ysil"], TT_["sq"], TT_["rr"]
            chunk = which * 8 + h
            wt = load_w(w_in, O_QKV + chunk * 128, 128)
            S.pool(lambda e: e.memset(xpad[:, 0:2], 0.0), writes=xpad.b)
            S.pool(lambda e: e.memset(xpad[:, T + 2:T + 4], 0.0), writes=xpad.b)
            for j in range(5):
                S.pool(lambda e, j=j: e.tensor_scalar(out=diag[:, j, :], in0=cst["ident"][:], scalar1=cw[:, j, chunk:chunk + 1], scalar2=None, op0=ALU.mult),
                       reads=cst["ident"].b + cw.b, writes=diag.b)
            yield
            for tb in range(NB):
                proj_fm(wt, tb, bA)
                S.act(lambda e, tb=tb: e.activation(out=xpad[:, 2 + tb * 512:2 + (tb + 1) * 512], in_=bank(bA), func=AF.Copy),
                      reads=[pb[bA]], writes=xpad.b)
                yield

            def stage_c(tb):
                sl = slice(tb * 512, (tb + 1) * 512)
                mm(bank(bA), ones_bf[:], sq[:], True, True, ones_bf.b + sq.b, [pb[bA]])
                S.act(lambda e: e.activation(out=rr[:], in_=bank(bA), func=AF.Ln, scale=1.0, bias=eps_t[:, 0:1]), reads=[pb[bA]] + eps_t.b, writes=rr.b)
                lb = lnsc_t if which == 0 else zero_t
                S.act(lambda e, lb=lb: e.activation(out=rr[:], in_=rr[:], func=AF.Exp, scale=-0.5, bias=lb[:, 0:1]), reads=rr.b + lb.b, writes=rr.b)
                S.pool(lambda e, sl=sl: e.tensor_tensor(out=dstT[:, sl], in0=ysil[:], in1=rr[:], op=ALU.mult), reads=ysil.b + rr.b, writes=dstT.b)

            for tb in range(NB):
                sl = slice(tb * 512, (tb + 1) * 512)
                for j in range(5):
                    mm(bank(bB), diag[:, j, :], xpad[:, tb * 512 + j:tb * 512 + j + 512], j == 0, j == 4, diag.b + xpad.b, [pb[bB]])
                if which == 2:
                    S.act(lambda e, sl=sl: e.activation(out=dstT[:, sl], in_=bank(bB), func=AF.Silu), reads=[pb[bB]], writes=dstT.b)
                    yield
                    continue
                S.act(lambda e: e.activation(out=ysil[:], in_=bank(bB), func=AF.Silu), reads=[pb[bB]], writes=ysil.b)
                S.pool(lambda e: e.tensor_tensor(out=sq[:], in0=ysil[:], in1=ysil[:], op=ALU.mult), reads=ysil.b, writes=sq.b)
                yield
                stage_c(tb)
                yield

        def rr_zip(gens):
            gens = list(gens)
            while gens:
                for g_ in list(gens):
                    try:
                        next(g_)
                    except StopIteration:
                        gens.remove(g_)
                    yield

        def g1_gen(h):
            return rr_zip([g1_chunk(0, h, qT, G1T[0], G1B[0], G1B[1]), g1_chunk(1, h, kT, G1T[1], G1B[2], G1B[3]), g1_chunk(2, h, vT, G1T[2], G1B[4], G1B[5])])

        def g23(h):
            for srcT, dst in ((kT, Ktok), (vT, Vtok)):
                for half in range(2):
                    bk = 6 + half
                    for j in range(8):
                        t = half * 8 + j
                        tr(bank_bf(bk)[:, j * 128:(j + 1) * 128], srcT[:, t * 128:(t + 1) * 128], srcT.b, [pb[bk]])
                    S.act(lambda e, bk=bk, dst=dst, half=half: e.activation(out=dst[:, half * 8:(half + 1) * 8, :], in_=bank_bf(bk).rearrange("p (t n) -> p t n", t=8), func=AF.Copy),
                          reads=[pb[bk]], writes=dst.b)
            for d in range(2):
                DD = DIRS[d]
                hd = d * 8 + h
                bcc = lambda tl, hd=hd: tl[:, :, hd:hd + 1].to_broadcast([128, NT, 128])
                S.pool(lambda e, DD=DD, bcc=bcc: e.tensor_tensor(out=DD["Kbg"][:], in0=Ktok[:], in1=bcc(cbg), op=ALU.mult), reads=Ktok.b + cbg.b, writes=DD["Kbg"].b)
                S.dve(lambda e, DD=DD, bcc=bcc: e.tensor_tensor(out=DD["Kd"][:], in0=Ktok[:], in1=bcc(cd), op=ALU.mult), reads=Ktok.b + cd.b, writes=DD["Kd"].b)
                S.dve(lambda e, DD=DD, bcc=bcc: e.tensor_tensor(out=DD["Vb"][:], in0=Vtok[:], in1=bcc(beta), op=ALU.mult), reads=Vtok.b + beta.b, writes=DD["Vb"].b)

        def prep_inst(h, d, half, TP, b0):
            DD = DIRS[d]
            hd = d * 8 + h
            MS = masks["ms_f" if d == 0 else "ms_b"]
            MI = masks["mi_f" if d == 0 else "mi_b"]
            diagG, ET, EMs, EMi, expR, X, N, P = (TP[k_] for k_ in ("diagG", "ET", "EMs", "EMi", "expR", "X", "N", "P"))
            Dm, Y = diagG, ET
            t0 = half * 8
            tsl = slice(t0, t0 + 8)
            csl = slice(t0 * 128, (t0 + 8) * 128)
            B0, B1, B2, B3 = b0, b0 + 1, b0 + 2, b0 + 3
            v3 = lambda bk: bank2(bk).rearrange("p a (b c) -> p (a b) c", c=128)
            blk = lambda bk, j: bank(bk + j // 4)[:, (j % 4) * 128:(j % 4 + 1) * 128]
            gcb = gc[:, tsl, hd:hd + 1].to_broadcast([128, 8, 128])
            S.dve(lambda e: e.tensor_tensor(out=diagG[:], in0=rep8(cst["ident"]), in1=gcb, op=ALU.mult), reads=cst["ident"].b + gc.b, writes=diagG.b)
            for q in range(2):
                mm(bank(B0 + q), cst["ones"][:], diagG[:, q * 4:(q + 1) * 4, :].rearrange("p a b -> p (a b)"), True, True, cst["ones"].b + diagG.b, [pb[B0 + q]])
            yield
            S.act(lambda e: e.activation(out=expR[:], in_=v3(B0), func=AF.Exp), reads=[pb[B0], pb[B1]], writes=expR.b)
            S.dve(lambda e: e.tensor_tensor(out=Dm[:], in0=v3(B0), in1=gcb, op=ALU.subtract), reads=[pb[B0], pb[B1]] + gc.b, writes=Dm.b)
            S.dve(lambda e: e.tensor_scalar(out=Dm[:], in0=Dm[:], scalar1=0.0, scalar2=None, op0=ALU.min), reads=Dm.b, writes=Dm.b)
            yield
            S.act(lambda e: e.activation(out=ET[:], in_=Dm[:], func=AF.Exp), reads=Dm.b, writes=ET.b)
            S.dve(lambda e: e.tensor_tensor(out=EMs[:], in0=ET[:], in1=rep8(MS), op=ALU.mult), reads=ET.b + MS.b, writes=EMs.b)
            S.dve(lambda e: e.tensor_tensor(out=EMi[:], in0=ET[:], in1=rep8(MI), op=ALU.mult), reads=ET.b + MI.b, writes=EMi.b)
            S.pool(lambda e: e.tensor_tensor(out=DD["QgT"][:, csl], in0=qT[:, csl], in1=expR[:].rearrange("p a b -> p (a b)"), op=ALU.mult),
                   reads=qT.b + expR.b, writes=DD["QgT"].b)
            for j in range(8):
                ksl = slice((t0 + j) * 128, (t0 + j + 1) * 128)
                mm(blk(B2, j), kT[:, ksl], kT[:, ksl], True, True, kT.b, [pb[B2 + j // 4]])
            for j in range(8):
                ksl = slice((t0 + j) * 128, (t0 + j + 1) * 128)
                mm(blk(B0, j), kT[:, ksl], qT[:, ksl], True, True, kT.b + qT.b, [pb[B0 + j // 4]])
            yield
            S.dve(lambda e: e.tensor_tensor(out=Y[:], in0=v3(B2), in1=EMs[:], op=ALU.mult), reads=[pb[B2], pb[B3]] + EMs.b, writes=Y.b)
            S.dve(lambda e: e.tensor_tensor(out=DD["intraT"][:, tsl, :], in0=v3(B0), in1=EMi[:], op=ALU.mult),
                  reads=[pb[B0], pb[B1]] + EMi.b, writes=DD["intraT"].b)
            for j in range(8):
                tr(bank_bf(B2)[:, j * 128:(j + 1) * 128], Y[:, j, :], Y.b, [pb[B2]])
            yield
            nbb = nbeta[:, tsl, hd:hd + 1].to_broadcast([128, 8, 128])
            S.dve(lambda e: e.tensor_tensor(out=N[:], in0=bank_bf(B2).rearrange("p (t n) -> p t n", t=8), in1=nbb, op=ALU.mult),
                  reads=[pb[B2]] + nbeta.b, writes=N.b)
            for j in range(8):
                tr(bank_bf(B3)[:, j * 128:(j + 1) * 128], N[:, j, :], N.b, [pb[B3]])
            yield
            S.act(lambda e: e.activation(out=X[:], in_=bank_bf(B3).rearrange("p (t n) -> p t n", t=8), func=AF.Copy), reads=[pb[B3]], writes=X.b)
            S.dve(lambda e: e.tensor_tensor(out=P[:], in0=bank_bf(B3).rearrange("p (t n) -> p t n", t=8), in1=rep8(ident_bf), op=ALU.add),
                  reads=[pb[B3]] + ident_bf.b, writes=P.b)
            yield
            for n in range(1, 6):
                for j in range(8):
                    mm(blk(B0, j), X[:, j, :], N[:, j, :], True, True, X.b + N.b, [pb[B0 + j // 4]])
                if n <= 4:
                    for j in range(8):
                        mm(blk(B2, j), N[:, j, :], X[:, j, :], True, True, X.b + N.b, [pb[B2 + j // 4]])
                yield
                S.act(lambda e: e.activation(out=N[:], in_=v3(B0), func=AF.Copy), reads=[pb[B0], pb[B1]], writes=N.b)
                if n <= 4:
                    S.dve(lambda e: e.tensor_copy(out=X[:], in_=v3(B2)), reads=[pb[B2], pb[B3]], writes=X.b)
                for j in range(8):
                    mm(blk(B0, j), N[:, j, :], P[:, j, :], True, True, N.b + P.b, [pb[B0 + j // 4]])
                yield
                S.dve(lambda e: e.tensor_tensor(out=P[:], in0=v3(B0), in1=P[:], op=ALU.add), reads=[pb[B0], pb[B1]] + P.b, writes=P.b)
            for j in range(8):
                t = t0 + j
                mm(blk(B2, j), P[:, j, :], DD["Vb"][:, t, :], True, True, P.b + DD["Vb"].b, [pb[B2 + j // 4]])
            for j in range(8):
                t = t0 + j
                mm(blk(B0, j), DD["Kbg"][:, t, :], P[:, j, :], True, True, P.b + DD["Kbg"].b, [pb[B0 + j // 4]])
            yield
            S.act(lambda e: e.activation(out=DD["U"][:, tsl, :], in_=v3(B2), func=AF.Copy), reads=[pb[B2], pb[B3]], writes=DD["U"].b)
            S.dve(lambda e: e.tensor_copy(out=DD["WT"][:, csl], in_=bank2(B0).rearrange("p a b -> p (a b)")), reads=[pb[B0], pb[B1]], writes=DD["WT"].b)
            yield

        def scan_gen(h):
            for d in range(2):
                S.pool(lambda e, d=d: e.memset(Sf[d][:], 0.0), writes=Sf[d].b)
                S.pool(lambda e, d=d: e.memset(Sb[d][:], 0.0), writes=Sb[d].b)
            seen = [0] * NB
            for step in range(32):
                info = []
                for d in range(2):
                    ci = step if d == 0 else 31 - step
                    t, jc = ci // 2, ci % 2
                    info.append(dict(d=d, DD=DIRS[d], hd=d * 8 + h, ci=ci, t=t, jc=jc, p0=jc * 64, blk=ci // 8, obk=d, wsb=2 + d, sub=4 + d,
                                     V=vn[d][step % 2], csl=slice(ci * 64, (ci + 1) * 64), oc=slice((ci % 8) * 64, (ci % 8 + 1) * 64)))
                for I in info:
                    d, DD, t = I["d"], I["DD"], I["t"]
                    mm(bank(I["wsb"])[:, 0:128], DD["WT"][:, t * 128:(t + 1) * 128], Sb[d][:], True, True, DD["WT"].b + Sb[d].b, [pb[I["wsb"]]])
                for I in info:
                    d, DD = I["d"], I["DD"]
                    mm(bank(I["obk"])[:, I["oc"]], Sb[d][:], DD["QgT"][:, I["csl"]], True, False, Sb[d].b + DD["QgT"].b, [pb[I["obk"]]])
                for I in info:
                    DD, V, t, p0, wsb = I["DD"], I["V"], I["t"], I["p0"], I["wsb"]
                    S.dve(lambda e, DD=DD, V=V, t=t, p0=p0, wsb=wsb: e.tensor_tensor(out=V[p0:p0 + 64, :], in0=DD["U"][p0:p0 + 64, t, :], in1=bank(wsb)[p0:p0 + 64, 0:128], op=ALU.subtract),
                          reads=DD["U"].b + [pb[wsb]], writes=V.b)
                for I in info:
                    DD, V, t, p0, sub = I["DD"], I["V"], I["t"], I["p0"], I["sub"]
                    mm(bank(sub)[:, 0:128], DD["Kd"][p0:p0 + 64, t, :], V[p0:p0 + 64, :], True, True, DD["Kd"].b + V.b, [pb[sub]])
                for I in info:
                    d, DD, V, t, p0 = I["d"], I["DD"], I["V"], I["t"], I["p0"]
                    mm(bank(I["obk"])[:, I["oc"]], V[p0:p0 + 64, :], DD["intraT"][p0:p0 + 64, t, p0:p0 + 64], False, True, V.b + DD["intraT"].b, [pb[I["obk"]]])
                if SCAN_DVE_SB:
                    for I in info:
                        d, sub = I["d"], I["sub"]
                        eg = egrep[:, I["t"], I["jc"], I["hd"]:I["hd"] + 1]
                        S.dve(lambda e, d=d, eg=eg, sub=sub: e.scalar_tensor_tensor(out=Sb[d][:], in0=Sf[d][:], scalar=eg, in1=bank(sub)[:, 0:128], op0=ALU.mult, op1=ALU.add),
                              reads=Sf[d].b + egrep.b + [pb[sub]], writes=Sb[d].b)
                for I in info:
                    d, sub = I["d"], I["sub"]
                    eg = egrep[:, I["t"], I["jc"], I["hd"]:I["hd"] + 1]
                    S.dve(lambda e, d=d, eg=eg, sub=sub: e.scalar_tensor_tensor(out=Sf[d][:], in0=Sf[d][:], scalar=eg, in1=bank(sub)[:, 0:128], op0=ALU.mult, op1=ALU.add),
                          reads=Sf[d].b + egrep.b + [pb[sub]], writes=Sf[d].b)
                if not SCAN_DVE_SB:
                    for I in info:
                        d = I["d"]
                        S.act(lambda e, d=d: e.activation(out=Sb[d][:], in_=Sf[d][:], func=AF.Copy), reads=Sf[d].b, writes=Sb[d].b)
                for I in info:
                    d, ci, blk_, obk = I["d"], I["ci"], I["blk"], I["obk"]
                    last = (ci % 8 == 7) if d == 0 else (ci % 8 == 0)
                    if last:
                        bsl = slice(blk_ * 512, (blk_ + 1) * 512)
                        if seen[blk_] == 0:
                            S.act(lambda e, bsl=bsl, obk=obk: e.activation(out=oacc[:, bsl], in_=bank(obk), func=AF.Copy), reads=[pb[obk]], writes=[oacc.b[blk_]])
                        else:
                            S.dve(lambda e, bsl=bsl, obk=obk: e.tensor_tensor(out=oacc[:, bsl], in0=bank(obk), in1=oacc[:, bsl], op=ALU.add), reads=[pb[obk], oacc.b[blk_]], writes=[oacc.b[blk_]])
                        seen[blk_] += 1
                yield
            yield from silu_proj_gen(O_ZG + h * 128, zs_g, banks=(6, 6))
            for tb in range(NB):
                sl = slice(tb * 512, (tb + 1) * 512)
                S.act(lambda e, sl=sl: e.activation(out=fsq[:], in_=oacc[:, sl], func=AF.Square), reads=[oacc.b[tb]], writes=fsq.b)
                mm(bank(7), ones_bf[:], fsq[:], True, True, ones_bf.b + fsq.b, [pb[7]])
                rsqrt_act(frr[:], bank(7), 1.0 / 128, [pb[7]], frr.b)
                S.dve(lambda e, sl=sl: e.tensor_tensor(out=frr[:], in0=oacc[:, sl], in1=frr[:], op=ALU.mult), reads=[oacc.b[tb]] + frr.b, writes=frr.b)
                S.dve(lambda e, sl=sl: e.scalar_tensor_tensor(out=ob_g[:, sl], in0=frr[:], scalar=gcol_gdn[:, 0:1], in1=zs_g[:, sl], op0=ALU.mult, op1=ALU.mult),
                      reads=frr.b + gcol_gdn.b + zs_g.b, writes=ob_g.b)
                yield
            S.dma(lambda e: e.dma_start(out=scr[h], in_=ob_g[:]), reads=ob_g.b, writes=[scrb[h]])

        run(g1_gen(0))
        for h in range(8):
            g23(h)
            S.barrier()
            for half in range(2):
                run(rr_zip([prep_inst(h, 0, half, PT[0], 0), prep_inst(h, 1, half, PT[1], 4)]))
            S.barrier()
            zipper(scan_gen(h), g1_gen(h + 1) if h < 7 else None, every=GE, nside=GZ)

    def attn_gen(kT_fn, q_fn, v_fn, nkt, scale, zs, dst_chunk, AT, extra=None):
        pTs, rec, tmpo, ob = AT["pTs"], AT["rec"], AT["tmpo"], AT["ob"][AT["ctr"] % 2]
        AT["ctr"] += 1
        its = [(qb, kt) for qb in range(NB) for kt in range(nkt)]
        SBK = (0, 1, 2)

        def emit_s(i):
            qb, kt = its[i]
            sbk = SBK[i % len(SBK)]
            pT = pTs[i % len(pTs)]
            ka, kb_ = kT_fn(kt)
            qa, qb_ = q_fn(qb)
            mm(bank(sbk), ka, qa, True, extra is None, kb_ + qb_, [pb[sbk]])
            if extra is not None:
                k2, k2b = extra[0](kt)
                q2, q2b = extra[1](qb)
                mm(bank(sbk), k2, q2, False, True, k2b + q2b, [pb[sbk]])
            S.act(lambda e, pT=pT, sbk=sbk: e.activation(out=pT[:], in_=bank(sbk), func=AF.Exp, scale=scale), reads=[pb[sbk]], writes=pT.b)

        def emit_pv(i):
            qb, kt = its[i]
            ob_, sb_ = 3 + (qb % 2), 5
            pT = pTs[i % len(pTs)]
            va, vb_ = v_fn(kt)
            mm(bank(ob_), va, pT[:], kt == 0, kt == nkt - 1, vb_ + pT.b, [pb[ob_]])
            if PAIRSUM and nkt % 2 == 0:
                if kt % 2 == 1:
                    pP = pTs[(i - 1) % len(pTs)]
                    p2 = AT["p2"][(i // 2) % 2]
                    S.pool(lambda e, p2=p2, pP=pP, pT=pT: e.tensor_tensor(out=p2[:], in0=pP[:], in1=pT[:], op=ALU.add), reads=pP.b + pT.b, writes=p2.b)
                    mm(bank(sb_), ones_bf[:], p2[:], kt == 1, kt == nkt - 1, ones_bf.b + p2.b, [pb[sb_]])
            else:
                mm(bank(sb_), ones_bf[:], pT[:], kt == 0, kt == nkt - 1, ones_bf.b + pT.b, [pb[sb_]])
            if kt == nkt - 1:
                sl = slice(qb * 512, (qb + 1) * 512)
                S.act(lambda e, sb_=sb_: e.activation(out=rec[:], in_=bank(sb_), func=AF.Copy), reads=[pb[sb_]], writes=rec.b)
                S.dve(lambda e: e.reciprocal(out=rec[:], in_=rec[:]), reads=rec.b, writes=rec.b)
                S.dve(lambda e, ob_=ob_: e.tensor_tensor(out=tmpo[:], in0=bank(ob_), in1=rec[:], op=ALU.mult), reads=[pb[ob_]] + rec.b, writes=tmpo.b)
                S.pool(lambda e, sl=sl: e.tensor_tensor(out=ob[:, sl], in0=tmpo[:], in1=zs[:, sl], op=ALU.mult), reads=tmpo.b + zs.b, writes=ob.b)

        import os as _os
        SK = int(_os.environ.get("SKEW", "3"))
        PAIRSUM = int(_os.environ.get("PAIRSUM", "0"))
        for i0 in range(min(SK, len(its))):
            emit_s(i0)
        for i in range(len(its)):
            if i + SK < len(its):
                emit_s(i + SK)
            emit_pv(i)
            yield
        S.dma(lambda e: e.dma_start(out=scr[dst_chunk], in_=ob[:]), reads=ob.b, writes=[scrb[dst_chunk]])

    def attn_temps():
        return dict(pTs=[alloc(f"pT{i}", [128, 512], BF16) for i in range(5)], rec=alloc("a_rec", [128, 512], F32),
                    tmpo=alloc("a_tmpo", [128, 512], F32), p2=[alloc(f"a_p2{i}", [128, 512], BF16) for i in range(2)], sacc=[alloc(f"a_sacc{i}", [128, 512], F32) for i in range(2)], ob=[alloc(f"a_ob{i}", [128, T], BF16) for i in range(2)], ctr=0)

    def phase_mla(s):
        areset()
        cqg = alloc("cqg", [128, 3, T], BF16)
        kvg = alloc("kvg", [128, 2, T], BF16)
        rq = alloc("rq", [128, T], F32)
        rkv = alloc("rkv", [128, T], F32)
        rkvc = alloc("rkvc", [128, NT], F32)
        kpeT = alloc("kpeT", [64, T], BF16)
        sqs = [alloc(f"msq{i}", [128, 512], BF16) for i in range(3)]
        r1 = alloc("r1", [64, 512], F32)
        r2 = alloc("r2", [64, 512], F32)
        cos2 = alloc("cos2", [64, T], F32)
        sins = alloc("sins", [64, T], F32)
        S.dma(lambda e: e.dma_start(out=cos2[:], in_=dr["cos2"]), writes=cos2.b)
        S.dma(lambda e: e.dma_start(out=sins[:], in_=dr["sins"]), writes=sins.b)
        for (col0, nchunk, dstg, gcolt, rdst, nfeat) in ((O_CQ, 3, cqg, gcol_q, rq, 384), (O_CKV, 2, kvg, gcol_kv, rkv, 256)):
            wts = [load_w(w_in, col0 + c * 128, 128) for c in range(nchunk)]
            for tb in range(NB):
                sl = slice(tb * 512, (tb + 1) * 512)
                for c in range(nchunk):
                    bk = 6 + (c % 2)
                    proj_fm(wts[c], tb, bk)
                    S.act(lambda e, bk=bk, c=c: e.activation(out=sqs[c][:], in_=bank(bk), func=AF.Square), reads=[pb[bk]], writes=sqs[c].b)
                    S.dve(lambda e, bk=bk, c=c, sl=sl, dstg=dstg, gcolt=gcolt: e.tensor_scalar(out=dstg[:, c, sl], in0=bank(bk), scalar1=gcolt[:, c:c + 1], scalar2=None, op0=ALU.mult),
                          reads=[pb[bk]] + gcolt.b + sqs[c].b, writes=dstg.b)
                for c in range(nchunk):
                    mm(bank(5), ones_bf[:], sqs[c][:], c == 0, c == nchunk - 1, ones_bf.b + sqs[c].b, [pb[5]])
                S.act(lambda e, sl=sl, rdst=rdst, nfeat=nfeat: e.activation(out=rdst[:, sl], in_=bank(5), func=AF.Sqrt, scale=1.0 / nfeat, bias=eps_t[:, 0:1]),
                      reads=[pb[5]] + eps_t.b, writes=rdst.b)
                S.dve(lambda e, sl=sl, rdst=rdst: e.reciprocal(out=rdst[:, sl], in_=rdst[:, sl]), reads=rdst.b, writes=rdst.b)
                import os as _os
                if nchunk == 2 and not _os.environ.get("NO_N1"):
                    for j in range(4):
                        t = tb * 4 + j
                        for c in range(2):
                            mm(bank(4)[:, t:t + 1], sqs[c][:, j * 128:(j + 1) * 128], ones_bf[:, 0:1], c == 0, c == 1, sqs[c].b + ones_bf.b, [pb[4]])
        S.act(lambda e: e.activation(out=rkvc[:], in_=bank(4)[:, 0:NT], func=AF.Sqrt, scale=1.0 / 256, bias=eps_t[:, 0:1]), reads=[pb[4]] + eps_t.b, writes=rkvc.b)
        S.dve(lambda e: e.reciprocal(out=rkvc[:], in_=rkvc[:]), reads=rkvc.b, writes=rkvc.b)

        def rope_pair(wsrc, colA, nk, rhs_fn, dst, rmul):
            wa = load_w(wsrc, colA, 64, nk=nk)
            wsw = wpool[wctr[0] % len(wpool)]
            wctr[0] += 1
            srcA = wsrc[:, colA + 32:colA + 64].rearrange("(k p) n -> p k n", p=128)
            srcB = wsrc[:, colA:colA + 32].rearrange("(k p) n -> p k n", p=128)
            S.dma(lambda e: e.dma_start(out=wsw[:, 0:nk, 0:32], in_=srcA), writes=wsw.b, eng="gpsimd")
            S.dma(lambda e: e.dma_start(out=wsw[:, 0:nk, 32:64], in_=srcB), writes=wsw.b, eng="gpsimd")
            for tb in range(NB):
                sl = slice(tb * 512, (tb + 1) * 512)
                for k in range(nk):
                    ra, rb = rhs_fn(k, sl)
                    mm(bank(6)[0:64, :], wa[:, k, 0:64], ra, k == 0, k == nk - 1, wa.b + rb, [pb[6]])
                for k in range(nk):
                    ra, rb = rhs_fn(k, sl)
                    mm(bank(7)[0:64, :], wsw[:, k, 0:64], ra, k == 0, k == nk - 1, wsw.b + rb, [pb[7]])
                S.dve(lambda e, sl=sl: e.tensor_tensor(out=r1[:], in0=bank(6)[0:64, :], in1=cos2[:, sl], op=ALU.mult), reads=[pb[6]] + cos2.b, writes=r1.b)
                S.dve(lambda e, sl=sl: e.tensor_tensor(out=r2[:], in0=bank(7)[0:64, :], in1=sins[:, sl], op=ALU.mult), reads=[pb[7]] + sins.b, writes=r2.b)
                if rmul is None:
                    S.dve(lambda e, sl=sl: e.tensor_tensor(out=dst[0:64, sl], in0=r1[:], in1=r2[:], op=ALU.add), reads=r1.b + r2.b, writes=dst.b)
                else:
                    S.dve(lambda e: e.tensor_tensor(out=r1[:], in0=r1[:], in1=r2[:], op=ALU.add), reads=r1.b + r2.b, writes=r1.b)
                    S.dve(lambda e, sl=sl: e.tensor_tensor(out=dst[0:64, sl], in0=r1[:], in1=rmul[0:64, sl], op=ALU.mult), reads=r1.b + rmul.b, writes=dst.b)
                yield

        import os as _os
        _stop = int(_os.environ.get("MLA_STOP", "9"))
        if _stop < 1:
            return
        run(rope_pair(w_in, O_CKV + 256, 8, lambda k, sl: (hT[:, k, sl], [hT.b[sl.start // 512]]), kpeT, None))
        wq = dr["w_q_up"][0]
        wkv = dr["w_kv_up"][0]
        HB = [dict(qnT=alloc(f"qnT{i}", [128, T], BF16), qpT=alloc(f"qpT{i}", [64, T], BF16), knT=alloc(f"knT{i}", [128, T], BF16),
                   vtok=alloc(f"vtok{i}", [128, NT, 128], BF16), zs=alloc(f"zs_m{i}", [128, T], BF16)) for i in range(2)]
        AT = attn_temps()

        def prep_gen(h, B):
            qnT, qpT, knT, vtok, zs = B["qnT"], B["qpT"], B["knT"], B["vtok"], B["zs"]
            wqn = load_w(wq, h * 192, 128, nk=3)
            wkn = load_w(wkv, h * 256, 128, nk=2)
            wv = load_w(wkv, h * 256 + 128, 128, nk=2)
            for tb in range(NB):
                sl = slice(tb * 512, (tb + 1) * 512)
                for k in range(3):
                    mm(bank(6), wqn[:, k, :], cqg[:, k, sl], k == 0, k == 2, wqn.b + cqg.b, [pb[6]])
                S.dve(lambda e, sl=sl: e.tensor_tensor(out=qnT[:, sl], in0=bank(6), in1=rq[:, sl], op=ALU.mult), reads=[pb[6]] + rq.b, writes=qnT.b)
                for k in range(2):
                    mm(bank(7), wkn[:, k, :], kvg[:, k, sl], k == 0, k == 1, wkn.b + kvg.b, [pb[7]])
                S.dve(lambda e, sl=sl: e.tensor_tensor(out=knT[:, sl], in0=bank(7), in1=rkv[:, sl], op=ALU.mult), reads=[pb[7]] + rkv.b, writes=knT.b)
                yield
            for q4 in range(4):
                bk = 6 + (q4 % 2)
                for j in range(4):
                    t = q4 * 4 + j
                    for k in range(2):
                        mm(bank(bk)[:, j * 128:(j + 1) * 128], kvg[:, k, t * 128:(t + 1) * 128], wv[:, k, :], k == 0, k == 1, kvg.b + wv.b, [pb[bk]])
                S.dve(lambda e, q4=q4, bk=bk: e.tensor_tensor(out=vtok[:, q4 * 4:(q4 + 1) * 4, :], in0=bank(bk).rearrange("p (t n) -> p t n", t=4),
                                                             in1=rkvc[:, q4 * 4:(q4 + 1) * 4].unsqueeze(2).to_broadcast([128, 4, 128]), op=ALU.mult),
                      reads=[pb[bk]] + rkvc.b, writes=vtok.b)
                yield
            yield from rope_pair(wq, h * 192 + 128, 3, lambda k, sl: (cqg[:, k, sl], cqg.b), qpT, rq)
            yield from silu_proj_gen(O_ZM + h * 128, zs)

        def head_attn(h, B):
            qnT, qpT, knT, vtok, zs = B["qnT"], B["qpT"], B["knT"], B["vtok"], B["zs"]
            return attn_gen(lambda kt: (knT[:, kt * 128:(kt + 1) * 128], knT.b),
                            lambda qb: (qnT[:, qb * 512:(qb + 1) * 512], qnT.b),
                            lambda kt: (vtok[:, kt, :], vtok.b), NT, SC_MLA, zs, 8 + h, AT,
                            extra=(lambda kt: (kpeT[0:64, kt * 128:(kt + 1) * 128], kpeT.b),
                                   lambda qb: (qpT[0:64, qb * 512:(qb + 1) * 512], qpT.b)))

        run(prep_gen(0, HB[0]))
        for h in range(8):
            side = prep_gen(h + 1, HB[(h + 1) % 2]) if h + 1 < 8 else None
            zipper(head_attn(h, HB[h % 2]), side, every=3)

    def phase_mem(s):
        areset()
        mt_ = [alloc(f"memt{i}", [128, D], F32) for i in range(2)]
        junk = alloc("mjunk", [128, D], BF16)
        mb = alloc("mb", [128, D], BF16)
        mst = alloc("mst", [128, 2], F32)
        mnT = alloc("mnT", [128, 8, 256], BF16)
        kmT = alloc("kmT", [128, 4, 256], BF16)
        vm = alloc("vm", [128, 2, 512], BF16)
        wmv = alloc("wmv", [128, 8, 512], BF16)
        wm = dr["w_mem_kv"][0]
        for i in range(2):
            X = mt_[i]
            S.dma(lambda e, X=X, i=i: e.dma_start(out=X[:], in_=mem[s, i * 128:(i + 1) * 128, :]), writes=X.b)
            S.act(lambda e, X=X: e.activation(out=junk[:], in_=X[:], func=AF.Square, accum_out=mst[:, 0:1]), reads=X.b, writes=junk.b + mst.b, tag="multi")
            rsqrt_inplace(mst[:, 0:1], mst.b, 1.0 / D, EPS)
            S.dve(lambda e, X=X: e.tensor_scalar(out=mb[:], in0=X[:], scalar1=mst[:, 0:1], scalar2=None, op0=ALU.mult), reads=X.b + mst.b, writes=mb.b)
            for k in range(8):
                tr(bank_bf(6)[:, k * 128:(k + 1) * 128], mb[:, k * 128:(k + 1) * 128], mb.b, [pb[6]])
            S.dve(lambda e, i=i: e.tensor_tensor(out=mnT[:, :, i * 128:(i + 1) * 128], in0=bank_bf(6).rearrange("p (k n) -> p k n", k=8),
                                                 in1=gcol_mem[:].unsqueeze(2).to_broadcast([128, 8, 128]), op=ALU.mult),
                  reads=[pb[6]] + gcol_mem.b, writes=mnT.b)
        for h in range(4):
            wt = load_w(wm, h * 128, 128)
            for k in range(8):
                mm(bank(7)[:, 0:256], wt[:, k, :], mnT[:, k, :], k == 0, k == 7, wt.b + mnT.b, [pb[7]])
            S.act(lambda e, h=h: e.activation(out=kmT[:, h, :], in_=bank(7)[:, 0:256], func=AF.Copy), reads=[pb[7]], writes=kmT.b)
        load_w(wm, 512, 512, dst=wmv)
        for i in range(2):
            for k in range(8):
                mm(bank(6), mnT[:, k, i * 128:(i + 1) * 128], wmv[:, k, :], k == 0, k == 7, mnT.b + wmv.b, [pb[6]])
            S.act(lambda e, i=i: e.activation(out=vm[:, i, :], in_=bank(6), func=AF.Copy), reads=[pb[6]], writes=vm.b)
        AT = attn_temps()
        MB = [dict(qmT=alloc(f"qmT{i}", [128, T], BF16), zs=alloc(f"zs_mem{i}", [128, T], BF16)) for i in range(2)]

        def mprep_gen(h, B):
            qmT = B["qmT"]
            wt = load_w(w_in, O_QM + h * 128, 128)
            for tb in range(NB):
                bk = 6 + (tb % 2)
                proj_fm(wt, tb, bk)
                S.act(lambda e, bk=bk, tb=tb: e.activation(out=qmT[:, tb * 512:(tb + 1) * 512], in_=bank(bk), func=AF.Copy), reads=[pb[bk]], writes=qmT.b)
                yield
            yield from silu_proj_gen(O_ZMEM + h * 128, B["zs"])

        def mattn(h, B):
            qmT = B["qmT"]
            return attn_gen(lambda kt: (kmT[:, h, kt * 128:(kt + 1) * 128], kmT.b),
                            lambda qb: (qmT[:, qb * 512:(qb + 1) * 512], qmT.b),
                            lambda kt: (vm[:, kt, h * 128:(h + 1) * 128], vm.b), 2, SC_MEM, B["zs"], 16 + h, AT)

        run(mprep_gen(0, MB[0]))
        for h in range(4):
            side = mprep_gen(h + 1, MB[(h + 1) % 2]) if h + 1 < 4 else None
            zipper(mattn(h, MB[h % 2]), side, every=1)

    TOP_OFF = ARENA_ELEMS - 28 * 1024

    def merge_weight_tiles():
        save = astate["off"]
        astate["off"] = TOP_OFF
        wbr = alloc("wbr", [128, 20, D], BF16)
        wout = alloc("wout", [128, 8, D], BF16)
        astate["off"] = save
        return wbr, wout

    def prefetch_merge_weights():
        wbr, wout = merge_weight_tiles()
        for br, (nm, nk, c0) in enumerate((("w_br_gdn", 8, 0), ("w_br_mla", 8, 8), ("w_br_mem", 4, 16))):
            src = dr[nm][0].rearrange("(k p) n -> p k n", p=128)
            for k in range(0, nk, 4):
                S.dma(lambda e, src=src, k=k, c0=c0: e.dma_start(out=wbr[:, c0 + k:c0 + k + 4, :], in_=src[:, k:k + 4, :]), writes=wbr.b, eng="gpsimd")
        src = dr["w_out"][0].rearrange("(k p) n -> p k n", p=128)
        for k in range(0, 8, 4):
            S.dma(lambda e, src=src, k=k: e.dma_start(out=wout[:, k:k + 4, :], in_=src[:, k:k + 4, :]), writes=wout.b, eng="gpsimd")

    def phase_merge(s, prefetched):
        areset()
        if not prefetched:
            prefetch_merge_weights()
        wbr, wout = merge_weight_tiles()
        oT = [alloc(f"oTb{i}", [128, 20, 512], BF16) for i in range(2)]
        mgs = [alloc(f"mg{i}", [128, 8, 512], BF16) for i in range(2)]
        acc = alloc("macc", [128, 512], F32)
        sig = [alloc(f"sig{i}", [128, 512], F32) for i in range(2)]
        tmp = alloc("mtmp", [128, 512], F32)
        xr = [alloc(f"xr{i}", [128, D], F32) for i in range(2)]
        ysb = [alloc(f"ysb{i}", [128, D], F32) for i in range(2)]
        junk = alloc("fjunk", [128, D], BF16)
        st = [alloc(f"fst{i}", [128, 2], F32) for i in range(2)]
        assert astate["off"] <= TOP_OFF
        branches = ((0, 8, 0), (1, 8, 8), (2, 4, 16))

        def gates_gen(tb):
            sl = slice(tb * 512, (tb + 1) * 512)
            O = oT[tb % 2]
            mg = mgs[tb % 2]
            S.dma(lambda e, O=O, sl=sl: e.dma_start(out=O[:], in_=scr[:, :, sl].rearrange("c p n -> p c n")), reads=scrb, writes=O.b)
            for fo in range(8):
                fsl = slice(fo * 128, (fo + 1) * 128)
                for br, nk, c0 in branches:
                    wg = load_w(w_in, O_GATE + br * 1024 + fo * 128, 128)
                    gb = 6 + (br % 2)
                    proj_fm(wg, tb, gb)
                    SG = sig[br % 2]
                    S.act(lambda e, gb=gb, SG=SG: e.activation(out=SG[:], in_=bank(gb), func=AF.Sigmoid), reads=[pb[gb]], writes=SG.b)
                    bb = 4 + (br % 2)
                    for k in range(nk):
                        mm(bank(bb), wbr[:, c0 + k, fsl], O[:, c0 + k, :], k == 0, k == nk - 1, wbr.b + O.b, [pb[bb]])
                    if br == 0:
                        S.dve(lambda e, bb=bb, SG=SG: e.tensor_tensor(out=acc[:], in0=bank(bb), in1=SG[:], op=ALU.mult), reads=[pb[bb]] + SG.b, writes=acc.b)
                    else:
                        S.dve(lambda e, bb=bb, SG=SG: e.tensor_tensor(out=tmp[:], in0=bank(bb), in1=SG[:], op=ALU.mult), reads=[pb[bb]] + SG.b, writes=tmp.b)
                        if br == 1:
                            S.dve(lambda e: e.tensor_tensor(out=acc[:], in0=acc[:], in1=tmp[:], op=ALU.add), reads=acc.b + tmp.b, writes=acc.b)
                        else:
                            S.dve(lambda e, fo=fo, mg=mg: e.tensor_tensor(out=mg[:, fo, :], in0=acc[:], in1=tmp[:], op=ALU.add), reads=acc.b + tmp.b, writes=mg.b)
                    yield

        def epi_gen(tb):
            mg = mgs[tb % 2]
            for j in range(4):
                t = tb * 4 + j
                X, Y, ST = xr[j % 2], ysb[j % 2], st[j % 2]
                S.dma(lambda e, X=X, t=t: e.dma_start(out=X[:], in_=x[s, t * 128:(t + 1) * 128, :]), writes=X.b)
                for hf in range(2):
                    for k in range(8):
                        mm(bank(hf), mg[:, k, j * 128:(j + 1) * 128], wout[:, k, hf * 512:(hf + 1) * 512], k == 0, k == 7, mg.b + wout.b, [pb[hf]])
                    yield
                S.dve(lambda e, X=X, Y=Y: e.tensor_tensor(out=Y[:], in0=bank2(0).rearrange("p a b -> p (a b)"), in1=X[:], op=ALU.add), reads=[pb[0], pb[1]] + X.b, writes=Y.b)
                S.act(lambda e, Y=Y, ST=ST: e.activation(out=junk[:], in_=Y[:], func=AF.Square, accum_out=ST[:, 0:1]), reads=Y.b, writes=junk.b + ST.b, tag="multi")
                yield
                rsqrt_inplace(ST[:, 0:1], ST.b, 1.0 / D, EPS)
                yield
                S.dve(lambda e, Y=Y, ST=ST: e.scalar_tensor_tensor(out=Y[:], in0=Y[:], scalar=ST[:, 0:1], in1=fgain[:], op0=ALU.mult, op1=ALU.mult),
                      reads=Y.b + ST.b + fgain.b, writes=Y.b)
                S.dma(lambda e, Y=Y, t=t: e.dma_start(out=y[s, t * 128:(t + 1) * 128, :], in_=Y[:]), reads=Y.b, final=True)
                yield

        run(gates_gen(0))
        for tb in range(NB):
            if tb + 1 < NB:
                zipper(gates_gen(tb + 1), epi_gen(tb), every=1)
            else:
                run(epi_gen(tb))

    for s in range(NSEQ):
        S.barrier()
        phase_h(s)
        if "G" in phases:
            S.barrier()
            phase_gdn(s)
        if "M" in phases:
            S.barrier()
            phase_mla(s)
        if "C" in phases:
            S.barrier()
            if "X" in phases:
                prefetch_merge_weights()
            phase_mem(s)
        if "X" in phases:
            S.barrier()
            phase_merge(s, "C" in phases)
    S.barrier()
    print("ops", S.stats(), flush=True)
    S.emit()
    return nc


_NC_CACHE = {}


def _run(xs, ms, weights, ncores, nseq, phases="GMCX", dbg=False):
    key = (nseq, phases, dbg)
    if key not in _NC_CACHE:
        _NC_CACHE[key] = build_nc(nseq, phases, dbg)
    nc = _NC_CACHE[key]
    consts = host_consts()
    in_maps = []
    for c in range(ncores):
        m = {"x": np.ascontiguousarray(xs[c * nseq:(c + 1) * nseq]), "mem": np.ascontiguousarray(ms[c * nseq:(c + 1) * nseq])}
        for k in WEIGHT_SHAPES:
            m[k] = np.ascontiguousarray(weights[k], dtype=np.float32)
        for k, v in consts.items():
            m["c_" + k] = v
        in_maps.append(m)
    res = run_bass_kernel_spmd(nc, in_maps, core_ids=list(range(ncores)))
    return res


def kernel(**inputs):
    xs = np.concatenate([inputs["x_prompt"], inputs["x_sample"]], axis=0)
    ms = np.concatenate([inputs["mem_prompt"], inputs["mem_sample"]], axis=0)
    weights = {k: np.asarray(inputs[k]) for k in WEIGHT_SHAPES}
    nseq = xs.shape[0] // 8
    res = _run(xs, ms, weights, 8, nseq)
    yall = np.concatenate([r["y"] for r in res.results], axis=0)
    nb = inputs["x_prompt"].shape[0]
    return (np.ascontiguousarray(yall[:nb]), np.ascontiguousarray(yall[nb:]))
```
